# Optimizing a Trainium2 kernel written in Bass

```python
import math
import jax, jax.numpy as jnp
from jax import lax
import numpy as np

D_MODEL = 2048
BATCH = 2
SEQ = 4096
DEPTH = 1
DEC_BATCH = 8
DEC_SEQ = 4
PAST_LEN = 16384
PAGE_SIZE = 128

RET_HEADS = 8
RET_DK = 128
RET_DV = 128
RET_CHUNK = 128
RET_THETA = 10000.0
DSA_HEADS = 8
DSA_KV_HEADS = 2
DSA_DH = 128
IDX_HEADS = 16
IDX_DH = 64
TOPK_MAX = 256
Q_BLOCK = 128
ROPE_THETA = 500000.0
ROPE_FRAC = 4
MEM_LEN = 256
MEM_HEADS = 4
MEM_DH = 128
D_FF = 4 * D_MODEL
EPS = 1e-6

RET_QK = RET_HEADS * RET_DK
RET_VW = RET_HEADS * RET_DV
DSA_QW = DSA_HEADS * DSA_DH
DSA_KVW = DSA_KV_HEADS * DSA_DH
IDX_QW = IDX_HEADS * IDX_DH
IN_SPLITS = (RET_QK, RET_QK, RET_VW, RET_VW, DSA_QW, DSA_KVW, DSA_KVW, IDX_QW, IDX_DH, IDX_HEADS, D_MODEL, D_MODEL)
N_IN = 2 * RET_QK + 2 * RET_VW + DSA_QW + 2 * DSA_KVW + IDX_QW + IDX_DH + IDX_HEADS + 2 * D_MODEL
MEM_W = MEM_HEADS * MEM_DH

kernel_name = 'retention_dsa_gated_hybrid_step'


def rmsnorm(x, g):
    xf = x.astype(jnp.float32)
    y = xf * lax.rsqrt(jnp.mean(xf * xf, axis=-1, keepdims=True) + EPS)
    return (y * g.astype(jnp.float32)).astype(x.dtype)


def rope(x, pos, n_rot, theta):
    half = n_rot // 2
    inv = theta ** (-jnp.arange(half, dtype=jnp.float32) / half)
    ang = pos.astype(jnp.float32)[:, None] * inv[None, :]
    cos = jnp.cos(ang)[:, None, :]
    sin = jnp.sin(ang)[:, None, :]
    xr = x[..., :n_rot].astype(jnp.float32)
    x1, x2 = xr[..., :half], xr[..., half:]
    rot = jnp.concatenate([x1 * cos - x2 * sin, x2 * cos + x1 * sin], axis=-1)
    return jnp.concatenate([rot.astype(x.dtype), x[..., n_rot:]], axis=-1)


def mixer_inputs(u, pos, w_in):
    B, T, _ = u.shape
    h = jnp.einsum('btd,dn->btn', u, w_in)
    parts = []
    o = 0
    for n in IN_SPLITS:
        parts.append(h[..., o:o + n])
        o += n
    rq, rk, rv, rg, dq, dk, dv, iq, ik, iw, ga, gb = parts
    rq = rope(rq.reshape(B, T, RET_HEADS, RET_DK), pos, RET_DK, RET_THETA)
    rk = rope(rk.reshape(B, T, RET_HEADS, RET_DK), pos, RET_DK, RET_THETA) * (RET_DK ** -0.5)
    rv = rv.reshape(B, T, RET_HEADS, RET_DV)
    dq = rope(dq.reshape(B, T, DSA_HEADS, DSA_DH), pos, DSA_DH // ROPE_FRAC, ROPE_THETA)
    dk = rope(dk.reshape(B, T, DSA_KV_HEADS, DSA_DH), pos, DSA_DH // ROPE_FRAC, ROPE_THETA)
    dv = dv.reshape(B, T, DSA_KV_HEADS, DSA_DH)
    iq = rope(iq.reshape(B, T, IDX_HEADS, IDX_DH), pos, IDX_DH // ROPE_FRAC, ROPE_THETA)
    ik = rope(ik.reshape(B, T, 1, IDX_DH), pos, IDX_DH // ROPE_FRAC, ROPE_THETA)[:, :, 0]
    return rq, rk, rv, rg, dq, dk, dv, iq, ik, iw, ga, gb


def retention(q, k, v, state0):
    B, T, H, dk = q.shape
    dv = v.shape[-1]
    C = RET_CHUNK if T % RET_CHUNK == 0 else T
    n = T // C
    log_g = jnp.log1p(-jnp.exp2(-5.0 - jnp.arange(H, dtype=jnp.float32)))
    j = jnp.arange(C, dtype=jnp.float32)
    diff = j[:, None] - j[None, :]
    intra = jnp.where(diff >= 0, jnp.exp(log_g[:, None, None] * jnp.maximum(diff, 0.0)), 0.0)
    q_dec = jnp.exp(log_g[None, :] * (j[:, None] + 1.0))[None, :, :, None]
    k_dec = jnp.exp(log_g[None, :] * (C - 1.0 - j[:, None]))[None, :, :, None]
    c_dec = jnp.exp(log_g * C)[None, :, None, None]

    def to_chunks(a):
        return a.astype(jnp.float32).reshape(B, n, C, H, a.shape[-1]).swapaxes(0, 1)

    def step(S, inp):
        qc, kc, vc = inp
        s = jnp.einsum('bihd,bjhd->bhij', qc, kc) * intra[None]
        o = jnp.einsum('bhij,bjhe->bihe', s, vc) + jnp.einsum('bihd,bhde->bihe', qc, S) * q_dec
        S = S * c_dec + jnp.einsum('bjhd,bjhe->bhde', kc * k_dec, vc)
        return S, o

    S, o = lax.scan(step, state0.astype(jnp.float32), (to_chunks(q), to_chunks(k), to_chunks(v)))
    return o.swapaxes(0, 1).reshape(B, T, H, dv), S


def index_select(iq, iw, ik, qpos, topk):
    L = ik.shape[1]
    dots = jnp.einsum('bthd,bsd->bths', iq, ik, preferred_element_type=jnp.float32)
    score = jnp.einsum('bths,bth->bts', jax.nn.relu(dots), iw.astype(jnp.float32))
    admissible = jnp.arange(L)[None, None, :] <= qpos[None, :, None]
    score = jnp.where(admissible, score, -jnp.inf)
    _, idx = lax.top_k(score, topk)
    valid = idx <= qpos[None, :, None]
    return idx, valid


gather_rows = jax.vmap(lambda a, i: a[i])


def sparse_attend(q, kg, vg, valid):
    B, T, H, dh = q.shape
    hkv = kg.shape[3]
    qg = q.reshape(B, T, hkv, H // hkv, dh)
    logits = jnp.einsum('btkgd,btnkd->btkgn', qg, kg, preferred_element_type=jnp.float32) * (dh ** -0.5)
    logits = jnp.where(valid[:, :, None, None, :], logits, -jnp.inf)
    p = jax.nn.softmax(logits, axis=-1).astype(vg.dtype)
    o = jnp.einsum('btkgn,btnkd->btkgd', p, vg)
    return o.reshape(B, T, H, dh)


def dsa_prompt(q, k, v, iq, ik, iw):
    B, S, H, dh = q.shape
    topk = min(TOPK_MAX, S // 4)

    def block(bi):
        start = bi * Q_BLOCK
        sl = lambda a: lax.dynamic_slice_in_dim(a, start, Q_BLOCK, axis=1)
        qpos = start + jnp.arange(Q_BLOCK, dtype=jnp.int32)
        idx, valid = index_select(sl(iq), sl(iw), ik, qpos, topk)
        return sparse_attend(sl(q), gather_rows(k, idx), gather_rows(v, idx), valid)

    out = lax.map(block, jnp.arange(S // Q_BLOCK, dtype=jnp.int32))
    return out.swapaxes(0, 1).reshape(B, S, H, dh)


def dsa_sample(q, k, v, iq, ik, iw, pool_k, pool_v, pool_ik, page_table):
    DB, T = q.shape[:2]
    L = PAST_LEN + T
    topk = min(TOPK_MAX, L // 4)
    ik_past = pool_ik[page_table].reshape(DB, PAST_LEN, IDX_DH).astype(ik.dtype)
    ik_all = jnp.concatenate([ik_past, ik], axis=1)
    qpos = PAST_LEN + jnp.arange(T, dtype=jnp.int32)
    idx, valid = index_select(iq, iw, ik_all, qpos, topk)
    in_past = idx < PAST_LEN
    pidx = jnp.minimum(idx, PAST_LEN - 1)
    page = pidx // PAGE_SIZE
    off = pidx % PAGE_SIZE
    phys = jnp.take_along_axis(page_table, page.reshape(DB, -1), axis=1).reshape(page.shape)
    nidx = jnp.clip(idx - PAST_LEN, 0, T - 1)
    sel = in_past[..., None, None]
    kg = jnp.where(sel, pool_k[phys, off].astype(k.dtype), gather_rows(k, nidx))
    vg = jnp.where(sel, pool_v[phys, off].astype(v.dtype), gather_rows(v, nidx))
    return sparse_attend(q, kg, vg, valid)


def layer_tail(x, ret_o, rg, dsa_o, ga, gb, mk, mv, gn_ret, w_ret_out, w_dsa_out, w_o,
               g_mem, w_mq, w_mo, g_mlp, w_up, w_down):
    B, T, _ = x.shape
    mu = jnp.mean(ret_o, axis=-1, keepdims=True)
    var = jnp.mean(jnp.square(ret_o - mu), axis=-1, keepdims=True)
    ret_n = ((ret_o - mu) * lax.rsqrt(var + EPS)).reshape(B, T, -1) * gn_ret.astype(jnp.float32)
    ret_y = (jax.nn.silu(rg.astype(jnp.float32)) * ret_n).astype(x.dtype)
    ya = ret_y @ w_ret_out
    yb = dsa_o.reshape(B, T, -1) @ w_dsa_out
    merged = jax.nn.sigmoid(ga) * ya + jax.nn.sigmoid(gb) * yb
    h = x + merged @ w_o
    qm = (rmsnorm(h, g_mem) @ w_mq).reshape(B, T, MEM_HEADS, MEM_DH)
    logits = jnp.einsum('bthd,bmhd->bhtm', qm, mk.astype(qm.dtype), preferred_element_type=jnp.float32) * (MEM_DH ** -0.5)
    p = jax.nn.softmax(logits, axis=-1).astype(qm.dtype)
    om = jnp.einsum('bhtm,bmhd->bthd', p, mv.astype(qm.dtype)).reshape(B, T, -1)
    h = h + om @ w_mo
    u = rmsnorm(h, g_mlp)
    h = h + jnp.square(jax.nn.relu(u @ w_up)) @ w_down
    return h


def setup_inputs(seed: int = 0) -> dict:
    key = jax.random.key(seed)
    ks = iter(list(jax.random.split(key, 40)))
    f32 = jnp.float32
    n_pages = PAST_LEN // PAGE_SIZE
    n_used = DEC_BATCH * n_pages
    n_pool = n_used + n_used // 4

    def nrm(shape, scale):
        return jax.random.normal(next(ks), shape, f32) * scale

    def gain(shape):
        return 1.0 + 0.01 * jax.random.normal(next(ks), shape, f32)

    page_table = jax.random.permutation(next(ks), n_pool)[:n_used].reshape(DEC_BATCH, n_pages).astype(jnp.int32)
    return {
        'x_prompt': nrm((BATCH, SEQ, D_MODEL), 1.0),
        'x_sample': nrm((DEC_BATCH, DEC_SEQ, D_MODEL), 1.0),
        'mem_prompt': nrm((BATCH, MEM_LEN, D_MODEL), 1.0),
        'cache_k': nrm((DEPTH, n_pool, PAGE_SIZE, DSA_KV_HEADS, DSA_DH), 1.0),
        'cache_v': nrm((DEPTH, n_pool, PAGE_SIZE, DSA_KV_HEADS, DSA_DH), 1.0),
        'cache_idx_k': nrm((DEPTH, n_pool, PAGE_SIZE, IDX_DH), 1.0),
        'state_ret': nrm((DEPTH, DEC_BATCH, RET_HEADS, RET_DK, RET_DV), 0.1),
        'cache_mem_k': nrm((DEPTH, DEC_BATCH, MEM_LEN, MEM_HEADS, MEM_DH), 1.0),
        'cache_mem_v': nrm((DEPTH, DEC_BATCH, MEM_LEN, MEM_HEADS, MEM_DH), 1.0),
        'page_table': page_table,
        'g_mix': gain((DEPTH, D_MODEL)),
        'w_in': nrm((DEPTH, D_MODEL, N_IN), D_MODEL ** -0.5),
        'gn_ret': gain((DEPTH, RET_VW)),
        'w_ret_out': nrm((DEPTH, RET_VW, D_MODEL), RET_VW ** -0.5),
        'w_dsa_out': nrm((DEPTH, DSA_QW, D_MODEL), DSA_QW ** -0.5),
        'w_o': nrm((DEPTH, D_MODEL, D_MODEL), D_MODEL ** -0.5),
        'g_mem': gain((DEPTH, D_MODEL)),
        'g_memkv': gain((DEPTH, D_MODEL)),
        'w_mq': nrm((DEPTH, D_MODEL, MEM_W), D_MODEL ** -0.5),
        'w_mk': nrm((DEPTH, D_MODEL, MEM_W), D_MODEL ** -0.5),
        'w_mv': nrm((DEPTH, D_MODEL, MEM_W), D_MODEL ** -0.5),
        'w_mo': nrm((DEPTH, MEM_W, D_MODEL), MEM_W ** -0.5),
        'g_mlp': gain((DEPTH, D_MODEL)),
        'w_up': nrm((DEPTH, D_MODEL, D_FF), D_MODEL ** -0.5),
        'w_down': nrm((DEPTH, D_FF, D_MODEL), D_FF ** -0.5),
        'g_final': gain((D_MODEL,)),
    }


def reference(x_prompt, x_sample, mem_prompt, cache_k, cache_v, cache_idx_k, state_ret, cache_mem_k, cache_mem_v,
              page_table, g_mix, w_in, gn_ret, w_ret_out, w_dsa_out, w_o, g_mem, g_memkv, w_mq, w_mk, w_mv, w_mo,
              g_mlp, w_up, w_down, g_final):
    B, S, _ = x_prompt.shape
    DB, T, _ = x_sample.shape
    M = mem_prompt.shape[1]
    pos_p = jnp.arange(S, dtype=jnp.int32)
    pos_s = PAST_LEN + jnp.arange(T, dtype=jnp.int32)
    hp, hs = x_prompt, x_sample
    sp_l, kp_l, vp_l, ikp_l, mkp_l, mvp_l = [], [], [], [], [], []
    ss_l, ks_l, vs_l, iks_l = [], [], [], []
    for l in range(DEPTH):
        up = rmsnorm(hp, g_mix[l])
        rq, rk, rv, rg, dq, dk, dv, iq, ik, iw, ga, gb = mixer_inputs(up, pos_p, w_in[l])
        ret_o, st = retention(rq, rk, rv, jnp.zeros((B, RET_HEADS, RET_DK, RET_DV), jnp.float32))
        dsa_o = dsa_prompt(dq, dk, dv, iq, ik, iw)
        mn = rmsnorm(mem_prompt, g_memkv[l])
        mk = (mn @ w_mk[l]).reshape(B, M, MEM_HEADS, MEM_DH)
        mv = (mn @ w_mv[l]).reshape(B, M, MEM_HEADS, MEM_DH)
        hp = layer_tail(hp, ret_o, rg, dsa_o, ga, gb, mk, mv, gn_ret[l], w_ret_out[l], w_dsa_out[l], w_o[l],
                        g_mem[l], w_mq[l], w_mo[l], g_mlp[l], w_up[l], w_down[l])
        sp_l.append(st); kp_l.append(dk); vp_l.append(dv); ikp_l.append(ik); mkp_l.append(mk); mvp_l.append(mv)
        us = rmsnorm(hs, g_mix[l])
        rq, rk, rv, rg, dq, dk, dv, iq, ik, iw, ga, gb = mixer_inputs(us, pos_s, w_in[l])
        ret_o, st = retention(rq, rk, rv, state_ret[l])
        dsa_o = dsa_sample(dq, dk, dv, iq, ik, iw, cache_k[l], cache_v[l], cache_idx_k[l], page_table)
        hs = layer_tail(hs, ret_o, rg, dsa_o, ga, gb, cache_mem_k[l], cache_mem_v[l], gn_ret[l], w_ret_out[l],
                        w_dsa_out[l], w_o[l], g_mem[l], w_mq[l], w_mo[l], g_mlp[l], w_up[l], w_down[l])
        ss_l.append(st); ks_l.append(dk); vs_l.append(dv); iks_l.append(ik)
    y_prompt = rmsnorm(hp, g_final)
    y_sample = rmsnorm(hs, g_final)
    ret_state_prompt = jnp.stack(sp_l)
    k_rows_prompt = jnp.stack(kp_l)
    v_rows_prompt = jnp.stack(vp_l)
    idx_k_rows_prompt = jnp.stack(ikp_l)
    mem_k_prompt = jnp.stack(mkp_l)
    mem_v_prompt = jnp.stack(mvp_l)
    ret_state_sample = jnp.stack(ss_l)
    k_rows_sample = jnp.stack(ks_l)
    v_rows_sample = jnp.stack(vs_l)
    idx_k_rows_sample = jnp.stack(iks_l)
    return (y_prompt, y_sample, ret_state_prompt, k_rows_prompt, v_rows_prompt, idx_k_rows_prompt,
            mem_k_prompt, mem_v_prompt, ret_state_sample, k_rows_sample, v_rows_sample, idx_k_rows_sample)
```

```python
import numpy as np
from contextlib import ExitStack
import concourse.bass as bass
import concourse.mybir as mybir
from concourse.bass_utils import run_bass_kernel_spmd

F32 = mybir.dt.float32
BF16 = mybir.dt.bfloat16
I32 = mybir.dt.int32
ALU = mybir.AluOpType
AF = mybir.ActivationFunctionType
AX = mybir.AxisListType

D = 2048
NIN = 10832
C_RQ, C_RK, C_RV, C_RG, C_DQ, C_DK, C_DV, C_IQ, C_IK, C_IW, C_GA, C_GB = (
    0, 1024, 2048, 3072, 4096, 5120, 5376, 5632, 6656, 6720, 6736, 8784)
EPS = 1e-6
PAST = 16384
NEG = -30000.0
NIT = 16
TOPK = 256
ENGS = ("pe", "act", "dve", "pool", "sp")
USE_POOLS = True


class Buf:
    __slots__ = ("name", "last_w", "readers", "excl", "last_any")

    def __init__(self, name="", excl=False):
        self.name = name
        self.last_w = None
        self.readers = []
        self.excl = excl
        self.last_any = None


class Op:
    __slots__ = ("eng", "emit", "deps", "is_dma", "sem", "val", "signal")

    def __init__(self, eng, emit, is_dma):
        self.eng = eng
        self.emit = emit
        self.deps = []
        self.is_dma = is_dma
        self.sem = None
        self.val = None
        self.signal = is_dma


class Sched:
    def __init__(self, nc, n_dma_sems=24):
        self.nc = nc
        self.ops = {e: [] for e in ENGS}
        self.n_dma_sems = n_dma_sems

    def op(self, eng, emit, reads=(), writes=(), dma=False):
        o = Op(eng, emit, dma)
        deps = []
        for b in reads:
            if b.last_w is not None:
                deps.append(b.last_w)
        for b in writes:
            if b.last_w is not None:
                deps.append(b.last_w)
            deps.extend(b.readers)
        for b in list(reads) + list(writes):
            if b.excl:
                if b.last_any is not None and b.last_any.eng != eng:
                    deps.append(b.last_any)
                b.last_any = o
        seen = set()
        for d in deps:
            if id(d) in seen or d is o:
                continue
            seen.add(id(d))
            if (not d.is_dma) and (not dma) and d.eng == "pe" and eng == "pe":
                continue
            o.deps.append(d)
            d.signal = True
        for b in reads:
            b.readers.append(o)
        for b in writes:
            b.last_w = o
            b.readers = []
        self.ops[eng].append(o)
        return o

    def dma(self, q, out, in_, reads=(), writes=()):
        return self.op(q, lambda e: e.dma_start(out=out, in_=in_), reads, writes, dma=True)

    def emit_all(self):
        nc = self.nc
        with ExitStack() as st:
            esem = {e: st.enter_context(nc.semaphore(f"s_{e}")) for e in ENGS}
            dsem = {q: [st.enter_context(nc.semaphore(f"d_{q}{i}")) for i in range(self.n_dma_sems)]
                    for q in ("sp", "pool", "act")}
            for e in ENGS:
                cnt = 0
                k = 0
                dcount = [0] * self.n_dma_sems
                for o in self.ops[e]:
                    if o.is_dma:
                        i = k % self.n_dma_sems
                        k += 1
                        dcount[i] += 16
                        o.sem = dsem[e][i]
                        o.val = dcount[i]
                    elif o.signal:
                        cnt += 1
                        o.sem = esem[e]
                        o.val = cnt
            block = st.enter_context(nc.Block())
            ops = self.ops

            def gen(e, engobj):
                clock = {}

                def wait(sem, val):
                    if clock.get(id(sem), 0) >= val:
                        return
                    engobj.wait_ge(sem, val)
                    clock[id(sem)] = val
                for o in ops[e]:
                    need = {}
                    for d in o.deps:
                        k = id(d.sem)
                        if k not in need or need[k][1] < d.val:
                            need[k] = (d.sem, d.val)
                    if o.is_dma and o.val > 16:
                        k = id(o.sem)
                        if k not in need or need[k][1] < o.val - 16:
                            need[k] = (o.sem, o.val - 16)
                    for sem, val in need.values():
                        wait(sem, val)
                    ins = o.emit(engobj)
                    if o.signal:
                        ins.then_inc(o.sem, 16 if o.is_dma else 1)
                if e == "sp":
                    for q in ("sp", "pool", "act"):
                        last = {}
                        for o in ops[q]:
                            if o.is_dma:
                                last[id(o.sem)] = (o.sem, o.val)
                        for sem, val in last.values():
                            wait(sem, val)

            @block.tensor
            def _(eng):
                gen("pe", eng)

            @block.scalar
            def _(eng):
                gen("act", eng)

            @block.vector
            def _(eng):
                gen("dve", eng)

            @block.gpsimd
            def _(eng):
                gen("pool", eng)

            @block.sync
            def _(eng):
                gen("sp", eng)


class Prog:
    def __init__(self, do_sample=True, n_own_groups=2, n_pre_groups=6, stop_after=None):
        self.nc = bass.Bass("TRN2", target_bir_lowering=False)
        self.S = Sched(self.nc)
        self.st = ExitStack()
        self.do_sample = do_sample
        self.n_own_groups = n_own_groups
        self.n_pre_groups = n_pre_groups
        self.stop_after = stop_after
        self.use_pools = False
        self.ps_ctr = 0
        self.pinned = set()
        self.ev_ctr = 0
        self.stat_ctr = 0

    def dram_in(self, name, shape, dt=F32):
        return self.nc.dram_tensor(name, list(shape), dt, kind="ExternalInput").ap()

    def dram_out(self, name, shape, dt=F32):
        return self.nc.dram_tensor(name, list(shape), dt, kind="ExternalOutput").ap()

    def sb(self, name, shape, dt):
        return self.st.enter_context(self.nc.sbuf_tensor(name, list(shape), dt))

    def mm(self, out, lhsT, rhs, st, sp, R, W, skip=False):
        if skip:
            self.S.op("pe", lambda e: e.matmul(out, lhsT=lhsT, rhs=rhs, start=st, stop=sp, skip_group_check=True), R, W)
        else:
            self.S.op("pe", lambda e: e.matmul(out, lhsT=lhsT, rhs=rhs, start=st, stop=sp), R, W)

    def act(self, out, in_, func, R, W, scale=1.0, bias=0.0, accum=None):
        if accum is None:
            self.S.op("act", lambda e: e.activation(out=out, in_=in_, func=func, bias=bias, scale=scale), R, W)
        else:
            self.S.op("act", lambda e: e.activation(out=out, in_=in_, func=func, bias=bias, scale=scale,
                                                    accum_out=accum), R, W)

    def tt(self, eng, out, in0, in1, op, R, W):
        self.S.op(eng, lambda e: e.tensor_tensor(out=out, in0=in0, in1=in1, op=op), R, W)

    def ts(self, eng, out, in0, s1, s2, op0, op1, R, W, accum=None):
        if op1 is None:
            self.S.op(eng, lambda e: e.tensor_scalar(out=out, in0=in0, scalar1=s1, scalar2=None, op0=op0), R, W)
        elif accum is None:
            self.S.op(eng, lambda e: e.tensor_scalar(out=out, in0=in0, scalar1=s1, scalar2=s2, op0=op0, op1=op1), R, W)
        else:
            self.S.op(eng, lambda e: e.tensor_scalar(out=out, in0=in0, scalar1=s1, scalar2=s2, op0=op0, op1=op1,
                                                     accum_out=accum), R, W)

    def rsqrt(self, out, in_, scale, b):
        self.ts("dve", out, in_, scale, EPS, ALU.mult, ALU.add, [b], [b])
        self.S.op("act", lambda e: e.activation(out=out, in_=out, func=AF.Sqrt), [b], [b])
        self.S.op("dve", lambda e: e.reciprocal(out=out, in_=out), [b], [b])

    def stt(self, eng, out, in0, scalar, in1, op0, op1, R, W):
        self.S.op(eng, lambda e: e.scalar_tensor_tensor(out=out, in0=in0, scalar=scalar, in1=in1, op0=op0, op1=op1), R, W)

    def cp(self, eng, out, in_, R, W):
        if eng == "act":
            self.act(out, in_, AF.Copy, R, W)
        else:
            self.S.op(eng, lambda e: e.tensor_copy(out=out, in_=in_), R, W)

    def red(self, eng, out, in_, op, R, W):
        self.S.op(eng, lambda e: e.tensor_reduce(out=out, in_=in_, axis=AX.X, op=op), R, W)

    def memset(self, eng, ap, v, W):
        self.S.op(eng, lambda e: e.memset(ap, v), [], W)

    def ev_eng(self):
        self.ev_ctr += 1
        return "act" if self.ev_ctr % 2 else "dve"

    def nextps(self):
        while True:
            b = self.ps_ctr % 8
            self.ps_ctr += 1
            if b not in self.pinned:
                return b

    def build(self):
        nc, S = self.nc, self.S
        P = self
        x_own = P.dram_in("x_own", [1024, D])
        x_pre = P.dram_in("x_pre", [3072, D])
        x_smp = P.dram_in("x_smp", [128, D])
        mem = P.dram_in("mem", [256, D])
        tab_own = P.dram_in("tab_own", [1024, 192])
        tab_pre = P.dram_in("tab_pre", [3072, 192])
        tab_smp = P.dram_in("tab_smp", [128, 192])
        cst = P.dram_in("cst", [128, 64])
        cbf_d = P.dram_in("cbf", [128, 2304])
        gvec = P.dram_in("gvec", [128, 96])
        gfin_d = P.dram_in("gfin", [128, D])
        w_in = P.dram_in("w_in", [D, NIN])
        w_ro = P.dram_in("w_ret_out", [1024, D])
        w_do = P.dram_in("w_dsa_out", [1024, D])
        w_o = P.dram_in("w_o", [D, D])
        w_mq = P.dram_in("w_mq", [D, 512])
        w_mk = P.dram_in("w_mk", [D, 512])
        w_mv = P.dram_in("w_mv", [D, 512])
        w_mo = P.dram_in("w_mo", [512, D])
        w_up = P.dram_in("w_up", [D, 8192])
        w_dn = P.dram_in("w_down", [8192, D])
        st_in = P.dram_in("state_smp", [8, 128, 128])
        cmk = P.dram_in("cmk", [256, 512])
        cmv = P.dram_in("cmv", [256, 512])
        pool_k2 = P.dram_in("pool_k", [1280 * 16, 2048])
        pool_v2 = P.dram_in("pool_v", [1280 * 16, 2048])
        pool_ik2 = P.dram_in("pool_ik", [1280 * 8, 1024])
        ptab = P.dram_in("ptab", [128, 1], I32)

        y_own = P.dram_out("y_own", [1024, D])
        y_smp = P.dram_out("y_smp", [128, D])
        st_own = P.dram_out("st_own", [8, 128, 128])
        st_smp = P.dram_out("st_smp", [8, 128, 128])
        k_own = P.dram_out("k_own", [1024, 256])
        v_own = P.dram_out("v_own", [1024, 256])
        ik_own = P.dram_out("ik_own", [1024, 64])
        k_smp = P.dram_out("k_smp", [128, 256])
        v_smp = P.dram_out("v_smp", [128, 256])
        ik_smp = P.dram_out("ik_smp", [128, 64])
        mk_o = P.dram_out("mk_o", [256, 512])
        mv_o = P.dram_out("mv_o", [256, 512])

        sb = P.sb
        KT = sb("KT", [128, 2, 4096], BF16); bKT = [Buf() for _ in range(33)]
        Vb = sb("Vb", [128, 32, 256], BF16); bVb = [Buf() for _ in range(33)]
        ikT = sb("ikT", [128, 4096], BF16); bikT = [Buf() for _ in range(33)]
        S32 = sb("S32", [128, 8, 128], F32); bS32 = [Buf(), Buf()]
        Sbf = sb("Sbf", [128, 8, 128], BF16); bSbf = [Buf(), Buf()]
        S32s, Sbfs, bS32s, bSbfs = S32, Sbf, bS32, bSbf
        cstf = sb("cstf_sb", [128, 64], F32); bcst = Buf()
        cbf = sb("cbf_sb", [128, 2304], BF16)
        ident = cbf[:, 0:128]
        tri = cbf[:, 128:256]
        intraT = cbf[:, 256:1280].rearrange("p (h i) -> p h i", h=8)
        diagq = cbf[:, 1280:2304].rearrange("p (h i) -> p h i", h=8)
        I4 = sb("I4", [128, 512], BF16)
        ones = sb("ones", [128, 128], BF16)
        gv = sb("gv", [128, 96], F32); bgv = Buf()
        kdec_own = cstf[:, 8:16]
        kdec_smp = cstf[:, 16:24]
        cdec_own = cstf[:, 24:32]
        cdec_smp = cstf[:, 32:40]
        slotb = cstf[:, 40:43]

        NW = 2
        Wr = [sb(f"Wr{i}", [128, 16, 512], BF16) for i in range(NW)]
        bW = [[Buf() for _ in range(4)] for _ in range(NW)]
        U = sb("U", [128, 16, 512], BF16); bU = [Buf() for _ in range(4)]
        H = sb("H", [128, 4, D], F32); bH = [Buf() for _ in range(4)]
        R = sb("R", [128, 32, 512], BF16)
        retyT = R[:, 0:8, :]; bretyT = [Buf() for _ in range(4)]
        dsaoT = R[:, 8:16, :]; bdsaoT = [Buf() for _ in range(4)]
        mrgT = R[:, 16:32, :]; bmrgT = [Buf() for _ in range(4)]
        aT = R; baT = Buf()
        gfin = R[:, 0:8, :].rearrange("p a b -> p (a b)").bitcast(F32)
        mbs = R[:, 16:24, :].rearrange("p a b -> p (a b)"); bmb = Buf()
        diagw = R[:, 24:28, :].rearrange("p a (b c) -> p (a b) c", c=128); bdiagw = Buf()
        rl = [R[:, 28 + i, :] for i in range(3)]; brl = [Buf() for _ in range(3)]
        xn4 = [R[:, 16 + 4 * i:20 + 4 * i, :].rearrange("p a b -> p (a b)") for i in range(4)]; bxn4 = [Buf() for _ in range(4)]
        stat = sb("stat", [128, 28, 8], F32); bstat = [Buf() for _ in range(28)]
        tab = [sb(f"tab{i}", [128, 192], F32) for i in range(4)]; btab = [Buf() for _ in range(4)]
        tabB = [sb(f"tabB{i}", [128, 192], F32) for i in range(4)]; btabB = [Buf() for _ in range(4)]
        U2 = R[:, 0:16, :]; bU2 = [Buf() for _ in range(4)]
        tmpA = [sb(f"tmpA{i}", [128, 512], F32) for i in range(2)]; btmpA = [Buf() for _ in range(2)]
        rtmp = [sb(f"rtmp{i}", [128, 512], F32) for i in range(2)]; brtmp = Buf()
        bfA = [sb(f"bfA{i}", [128, 512], BF16) for i in range(2)]; bbfA = [Buf() for _ in range(2)]
        PTb = [sb(f"PT{i}", [128, 512], BF16) for i in range(3)]; bPT = [Buf() for _ in range(3)]
        AR = sb("AR", [128, 9216], BF16)
        q4 = AR[:, 0:2048].rearrange("p (a b) -> p a b", a=4); bq4 = [Buf() for _ in range(4)]
        k4 = AR[:, 2048:4096].rearrange("p (a b) -> p a b", a=4); bk4 = [Buf() for _ in range(4)]
        v4 = AR[:, 4096:6144].rearrange("p (a b) -> p a b", a=4); bv4 = [Buf() for _ in range(4)]
        kd4 = AR[:, 6144:6656]; bkd4 = Buf()
        tr3 = [AR[:, 6656 + i * 512:6656 + (i + 1) * 512] for i in range(3)]; btr3 = [Buf() for _ in range(3)]
        smT = AR[:, 8192:8704]; bsmT = Buf()
        rpre = AR[:, 8704:9216]; brpre = Buf()
        QT = AR[:, 0:4096].rearrange("p (a h t) -> p a h t", a=4, h=8); bQT = [Buf() for _ in range(4)]
        iqT = AR[:, 4096:8192].rearrange("p (a h t) -> p a h t", a=4, h=8); biqT = [Buf() for _ in range(4)]
        kbf = AR[:, 8192:8448]; bkbf = Buf()
        kbf4 = [AR[:, 8192 + 256 * i:8448 + 256 * i] for i in range(4)]; bkbf4 = [Buf() for _ in range(4)]
        ik2 = sb("ik2", [128, 128], BF16); bik2 = Buf()
        qmT = AR[:, 0:2048].rearrange("p (a b) -> p a b", a=4); bqmT = Buf()
        PTm = AR[:, 2048:3072].rearrange("p (a b) -> p a b", a=2); bPTm = Buf()
        omT = AR[:, 3072:5120].rearrange("p (a b) -> p a b", a=4); bomT = Buf()
        recf2 = AR[:, 5120:6144].bitcast(F32); brec2 = Buf()
        sg4 = H[:, 0, :].bitcast(BF16)[:, 0:2048].rearrange("p (a b) -> p a b", a=4); bsg4 = [bH[0]] * 4
        iwf = sb("iwf", [128, 4, 16], F32); biw = [Buf() for _ in range(4)]
        ikf = sb("ikf", [128, 64], F32); bikf = Buf()
        mkT = sb("mkT", [128, 4, 256], BF16); bmkT = Buf()
        mvb = sb("mvb", [128, 2, 512], BF16); bmvb = Buf()
        mkTs, mvbs, bmkTs, bmvbs = mkT, mvb, bmkT, bmvb
        orow = H[:, 2, 0:512]; borow = bH[2]
        kvf = H[:, 2, 512:1024]; bkvf = bH[2]
        recf = H[:, 2, 1024:1536]; brec = bH[2]
        m1 = H[:, 3, :].rearrange("p (a b) -> p a b", a=4); bm1 = [bH[3]] * 4
        Isc = H[:, 0:2, :].rearrange("p a b -> p (a b)")
        bIsc = [bH[0], bH[1]]
        Isc2 = H[:, 2:4, :].rearrange("p a b -> p (a b)")
        bIsc2 = [bH[2], bH[3]]

        MR = R[:, 16:32, :].rearrange("p a b -> p (a b)")
        sIT = MR[:, 0:1032].bitcast(F32); bsIT = Buf()
        sMask = MR[:, 1040:1556]; bsMask = Buf()
        sG = MR[:, 1560:2076]; bsG = Buf()
        sRl = MR[:, 2080:3104].bitcast(F32); bsRl = Buf()
        sTmp = MR[:, 3104:4128].bitcast(F32); bsTmp = Buf()
        sE = MR[:, 4128:4640].bitcast(F32); bsE = Buf()
        sPT = MR[:, 4640:4896]; bsPT = Buf()
        sKTr = [MR[:, 4896:5408], MR[:, 5408:5920]]; bsKTr = [Buf(), Buf()]
        sikTr = [MR[:, 5920:6432], MR[:, 6432:6944]]; bsikTr = [Buf(), Buf()]
        sKTo = MR[:, 6944:7200]; bsKTo = Buf()
        sVo = MR[:, 7200:7456]; bsVo = Buf()
        sikTo = MR[:, 7456:7584]; bsikTo = Buf()
        sWb = MR[:, 7584:7712].bitcast(F32); bsWb = Buf()
        sSt = MR[:, 7712:7840].bitcast(F32); bsSt = Buf()
        sX = MR[:, 7840:7904]; bsX = Buf()
        sPcb = MR[:, 7904:7912]; bsPcb = Buf()
        sIdxF = MR[:, 7912:7976].bitcast(F32)
        sIdxI = MR[:, 7976:8040].bitcast(I32); bsIdx = Buf()
        sPt = MR[:, 8040:8042].bitcast(I32)
        Kst = [KT[:, i, :].bitcast(F32) for i in range(2)]; bKst = [Buf(), Buf()]
        Vst = [Vb[:, 16 * i:16 * (i + 1), :].rearrange("p a b -> p (a b)").bitcast(F32) for i in range(2)]; bVst = [Buf(), Buf()]
        Ist = [ikT[:, 2048 * i:2048 * (i + 1)].bitcast(F32) for i in range(2)]; bIst = [Buf(), Buf()]
        H1bf = H[:, 1, :].bitcast(BF16)
        sKbf = H1bf[:, 0:2048]; sVbf = H1bf[:, 2048:4096]
        sIbf = H[:, 3, 512:1536].bitcast(BF16)
        triT4 = cstf[:, 44:48]

        PB = [self.st.enter_context(nc.psum_tensor(f"pb{i}", [128, 512], F32)) for i in range(8)]
        bPB = [Buf(excl=True) for _ in range(8)]

        S.dma("sp", cstf[:], cst, writes=[bcst])
        S.dma("sp", gv[:], gvec, writes=[bgv])
        bc = Buf()
        S.dma("pool", cbf[:], cbf_d, writes=[bc])
        for j in range(4):
            P.cp("dve", I4[:, j * 128:(j + 1) * 128], ident, [bc], [bc])
        P.memset("dve", ones[:], 1.0, [bc])
        G_MIX, G_MEM, G_MEMKV, G_MLP, G_GN = 0, 16, 32, 48, 64
        RC = [bc, bcst, bgv]

        def new_stat():
            i = P.stat_ctr % 28
            P.stat_ctr += 1
            return stat[:, i, :], bstat[i]

        def rms_a(items):
            sts = []
            for (xap, bx, ti) in items:
                stt_, bst = new_stat()
                sts.append((stt_, bst))
                P.memset("dve", stt_[:, 0:1], 0.0, [bst])
                P.act(xn4[ti], xap, AF.Square, [bx, bst], [bxn4[ti], bst], accum=stt_[:, 0:1])
            for (xap, bx, ti), (stt_, bst) in zip(items, sts):
                P.ts("dve", stt_[:, 1:2], stt_[:, 0:1], 1.0 / D, EPS, ALU.mult, ALU.add, [bst], [bst])
            for (xap, bx, ti), (stt_, bst) in zip(items, sts):
                P.S.op("act", lambda e, o=stt_[:, 1:2]: e.activation(out=o, in_=o, func=AF.Sqrt), [bst], [bst])
            for (xap, bx, ti), (stt_, bst) in zip(items, sts):
                P.S.op("dve", lambda e, o=stt_[:, 1:2]: e.reciprocal(out=o, in_=o), [bst], [bst])
            for (xap, bx, ti), (stt_, bst) in zip(items, sts):
                P.ts("dve", xn4[ti], xap, stt_[:, 1:2], None, ALU.mult, None, [bx, bst], [bxn4[ti]])

        def rms_b(items, goff, Ud=None, bUd=None):
            Ud = U if Ud is None else Ud
            bUd = bU if bUd is None else bUd
            pend = []
            for (xap, bx, ti) in items:
                for q in range(4):
                    bk = P.nextps()
                    for j in range(4):
                        kc = q * 4 + j
                        P.mm(PB[bk][:, j * 128:(j + 1) * 128], xn4[ti][:, kc * 128:(kc + 1) * 128], ident, True, True,
                             [bxn4[ti]] + RC, [bPB[bk]])
                    pend.append((bk, q, ti))
                    if len(pend) >= 4:
                        evac_T(pend.pop(0), goff, Ud, bUd)
            while pend:
                evac_T(pend.pop(0), goff, Ud, bUd)

        def rmsnorm_T_multi(items, goff):
            rms_a(items)
            rms_b(items, goff)

        def evac_T(item, goff, Ud, bUd):
            bk, q, ti = item
            dst = Ud[:, 4 * q:4 * q + 4, ti * 128:(ti + 1) * 128]
            src = PB[bk][:].rearrange("p (j t) -> p j t", j=4)
            gb_ = gv[:, goff + 4 * q:goff + 4 * q + 4].unsqueeze(2).to_broadcast([128, 4, 128])
            P.tt("dve", dst, src, gb_, ALU.mult, [bPB[bk]] + RC, [bUd[ti]])

        def rmsnorm_T(xap, bx, goff, ti, ncols_tile=128):
            rmsnorm_T_multi([(xap, bx, ti)], goff)

        brt4 = [Buf() for _ in range(4)]

        def rope(eng, src3, dst3, cos2, sin2, Hh, half, dh, Rr, Ww, inplace=False):
            cb_ = cos2.unsqueeze(1).to_broadcast([128, Hh, half])
            sb_ = sin2.unsqueeze(1).to_broadcast([128, Hh, half])
            x1 = src3[:, :, 0:half]
            x2 = src3[:, :, half:2 * half]
            n = Hh * half
            tA = rtmp[0][:, 0:n].rearrange("p (h d) -> p h d", h=Hh)
            tB = rtmp[0][:, 256:256 + n].rearrange("p (h d) -> p h d", h=Hh)
            tC = rtmp[1][:, 0:n].rearrange("p (h d) -> p h d", h=Hh)
            tD = rtmp[1][:, 256:256 + n].rearrange("p (h d) -> p h d", h=Hh)
            Rr = list(Rr)
            P.tt(eng, tA, x1, cb_, ALU.mult, Rr, [brt4[0]])
            P.tt(eng, tB, x2, sb_, ALU.mult, Rr, [brt4[1]])
            P.tt(eng, tC, x1, sb_, ALU.mult, Rr, [brt4[2]])
            P.tt(eng, tD, x2, cb_, ALU.mult, Rr, [brt4[3]])
            P.tt(eng, dst3[:, :, 0:half], tA, tB, ALU.subtract, [brt4[0], brt4[1], brt4[2], brt4[3]] + Rr, Ww)
            P.tt(eng, dst3[:, :, half:2 * half], tC, tD, ALU.add, [brt4[2], brt4[3]] + Rr, Ww)
            if 2 * half < dh and not inplace:
                P.cp(eng, dst3[:, :, 2 * half:dh], src3[:, :, 2 * half:dh], Rr, Ww)

        def load_tab(src_rows):
            i = P.stat_ctr % 4
            P.stat_ctr += 1
            S.dma("sp", tab[i][:], src_rows, writes=[btab[i]])
            return tab[i], btab[i]

        def proj_tiles(slot, bslot, nkc, ncol, tiles, lhs_of, lhs_bufs_of, evac):
            banks = []
            for ti in tiles:
                b = P.nextps()
                P.pinned.add(b)
                banks.append(b)
            nq = (nkc + 3) // 4
            for q in range(nq):
                for i, ti in enumerate(tiles):
                    b = banks[i]
                    for kc in range(q * 4, min(nkc, q * 4 + 4)):
                        P.mm(PB[b][:, 0:ncol], lhs_of(kc, ti), slot[:, kc, 0:ncol], kc == 0, kc == nkc - 1,
                             lhs_bufs_of(ti) + [bslot[kc // 4]], [bPB[b]])
            for b in banks:
                P.pinned.discard(b)
            for i, ti in enumerate(tiles):
                evac(ti, PB[banks[i]], bPB[banks[i]])

        def u_lhs(kc, ti):
            return U[:, kc, ti * 128:(ti + 1) * 128]

        def u_bufs(ti):
            return [bU[ti]]

        def transposes_to(src_bf, bsrc, n, dst_of, bdst_of, scale_of=None, part=128):
            b = P.nextps()
            for j in range(n):
                P.mm(PB[b][:, j * 128:(j + 1) * 128], src_bf[:, j * 128:(j + 1) * 128], ident, True, True,
                     [bsrc] + RC, [bPB[b]])
            eng = P.ev_eng()
            for j in range(n):
                src = PB[b][:, j * 128:(j + 1) * 128]
                if scale_of is not None:
                    P.ts("dve", dst_of(j), src, scale_of(j), None, ALU.mult, None, [bPB[b]] + RC, [bdst_of(j)])
                else:
                    P.cp(eng, dst_of(j), src, [bPB[b]], [bdst_of(j)])

        def evac_dkdv(ps, bps, tb, btb, chunk, outk=None, outv=None, to_keys=True):
            P.cp("act", kvf, ps[:, 0:512], [bps], [bkvf])
            rope("dve", kvf[:, 0:256].rearrange("p (h d) -> p h d", h=2), kvf[:, 0:256].rearrange("p (h d) -> p h d", h=2),
                 tb[:, 128:144], tb[:, 144:160], 2, 16, 128, [bkvf, btb], [bkvf], inplace=True)
            if outk is not None:
                S.dma("sp", outk, kvf[:, 0:256], reads=[bkvf])
                S.dma("sp", outv, kvf[:, 256:512], reads=[bkvf])
            if to_keys:
                kb_ = kbf4[chunk % 4]
                bkb2 = bkbf4[chunk % 4]
                P.cp("dve", kb_, kvf[:, 0:256], [bkvf], [bkb2])
                P.cp("dve", Vb[:, chunk, :], kvf[:, 256:512], [bkvf], [bVb[chunk]])
                defer(lambda: transposes_to(kb_, bkb2, 2, lambda j: KT[:, j, chunk * 128:(chunk + 1) * 128], lambda j: bKT[chunk]), 2)

        def evac_ik(ps, bps, tb, btb, chunk, outik=None, to_keys=True):
            P.cp("act", ikf[:], ps[:, 0:64], [bps], [bikf])
            v3 = ikf[:].rearrange("p (h d) -> p h d", h=1)
            rope("dve", v3, v3, tb[:, 160:168], tb[:, 168:176], 1, 8, 64, [bikf, btb], [bikf], inplace=True)
            if self.stop_after == "projB":
                return
            if outik is not None:
                S.dma("sp", outik, ikf[:], reads=[bikf])
            if self.stop_after == "projC":
                return
            if to_keys:
                i2 = PTb[0][:, (chunk % 4) * 128:(chunk % 4 + 1) * 128]
                P.cp("dve", i2[:, 0:64], ikf[:], [bikf], [bPT[0]])
                P.cp("dve", i2[:, 64:128], ikf[:], [bikf], [bPT[0]])
                defer(lambda: transposes_to(i2, bPT[0], 1, lambda j: ikT[:, chunk * 128:(chunk + 1) * 128], lambda j: bikT[chunk]), 2)

        def evac_rk(ps, bps, tb, btb, dst_bf, bdst):
            i = P.ev_ctr % 2
            P.ev_ctr += 1
            P.act(tmpA[i][:], ps[:, 0:512], AF.Copy, [bps], [btmpA[i]], scale=128.0 ** -0.5)
            rope("dve", tmpA[i][:].rearrange("p (h d) -> p h d", h=4), dst_bf.rearrange("p (h d) -> p h d", h=4),
                 tb[:, 0:64], tb[:, 64:128], 4, 64, 128, [btmpA[i], btb], [bdst])

        def wseg(src2d, nkc, coff, ncol):
            return (src2d, nkc, coff, ncol)

        P.deferred = []

        def defer(fn, delay=1):
            P.deferred.append([delay, fn])

        def run_deferred(flush=False):
            keep = []
            todo = []
            for it in P.deferred:
                it[0] -= 1
                if flush or it[0] <= 0:
                    todo.append(it[1])
                else:
                    keep.append(it)
            P.deferred = keep
            for fn in todo:
                fn()

        wcache = {}

        def run_steps(steps):
            def issue(i):
                slot = i % NW
                for (src, nkc, coff, ncol) in steps[i][0]:
                    key = str(src)
                    first = key not in wcache
                    if first:
                        scr = nc.dram_tensor(f"wc{len(wcache)}", [128, nkc, ncol], BF16).ap()
                        wcache[key] = (scr, [Buf() for _ in range(4)])
                    scr, bscr = wcache[key]
                    for k0 in range(0, nkc, 4):
                        k1 = min(nkc, k0 + 4)
                        q = k0 // 4
                        dst = Wr[slot][:, k0:k1, coff:coff + ncol]
                        if first:
                            S.dma("pool", dst, src[k0 * 128:k1 * 128, :].rearrange("(k p) n -> p k n", p=128), writes=[bW[slot][q]])
                            S.dma("sp", scr[:, k0:k1, :], dst, reads=[bW[slot][q]], writes=[bscr[q]])
                        else:
                            S.dma("sp", dst, scr[:, k0:k1, :], reads=[bscr[q]], writes=[bW[slot][q]])
            if not steps:
                return
            issue(0)
            for i in range(len(steps)):
                if i + 1 < len(steps):
                    issue(i + 1)
                steps[i][1](Wr[i % NW], bW[i % NW])
                run_deferred()

        def phase_mem():
            steps = []

            def mk_step(slot, bslot):
                for mt in range(2):
                    S.dma("sp", H[:, mt, :], mem[mt * 128:(mt + 1) * 128, :], writes=[bH[mt]])
                rmsnorm_T_multi([(H[:, mt, :], bH[mt], mt) for mt in range(2)], G_MEMKV)
                def evac(ti, ps, bps):
                    i = ti % 2
                    P.cp("act", tmpA[i][:], ps[:, 0:512], [bps], [btmpA[i]])
                    S.dma("sp", mk_o[ti * 128:(ti + 1) * 128, :], tmpA[i][:], reads=[btmpA[i]])
                    P.cp("dve", bfA[i][:], tmpA[i][:], [btmpA[i]], [bbfA[i]])
                    transposes_to(bfA[i], bbfA[i], 4, lambda j: mkT[:, j, ti * 128:(ti + 1) * 128], lambda j: bmkT)
                proj_tiles(slot, bslot, 16, 512, [0, 1], u_lhs, u_bufs, evac)

            def mv_step(slot, bslot):
                def evac(ti, ps, bps):
                    i = ti % 2
                    P.cp("act", tmpA[i][:], ps[:, 0:512], [bps], [btmpA[i]])
                    S.dma("sp", mv_o[ti * 128:(ti + 1) * 128, :], tmpA[i][:], reads=[btmpA[i]])
                    P.cp("dve", mvb[:, ti, :], tmpA[i][:], [btmpA[i]], [bmvb])
                proj_tiles(slot, bslot, 16, 512, [0, 1], u_lhs, u_bufs, evac)
            steps.append(([wseg(w_mk, 16, 0, 512)], mk_step))
            steps.append(([wseg(w_mv, 16, 0, 512)], mv_step))
            return steps

        def load_sample_mem():
            for mt in range(2):
                i = mt
                S.dma("sp", tmpA[i][:], cmk[mt * 128:(mt + 1) * 128, :], writes=[btmpA[i]])
                P.cp("dve", bfA[i][:], tmpA[i][:], [btmpA[i]], [bbfA[i]])
                transposes_to(bfA[i], bbfA[i], 4, lambda j, mt=mt: mkTs[:, j, mt * 128:(mt + 1) * 128], lambda j: bmkTs)
            S.dma("pool", mvbs[:], cmv.rearrange("(c p) n -> p c n", p=128), writes=[bmvbs])

        pre_state_first = [True, True]

        pre_loads = {}

        def prefix_group(g, last):
            tiles = list(range(4))

            Ug, bUg = (U, bU) if g % 2 == 0 else (U2, bU2)
            tabs, btabs = (tab, btab) if g % 2 == 0 else (tabB, btabB)
            items = [(H[:, ti, :], bH[ti], ti) for ti in tiles]

            def loads_a():
                if g == 0:
                    P.pinned.add(6)
                    P.pinned.add(7)
                    P.memset("dve", PB[6][:], 0.0, [bPB[6]])
                    P.memset("dve", PB[7][:], 0.0, [bPB[7]])
                for ti in tiles:
                    row0 = (g * 4 + ti) * 128
                    S.dma("sp", H[:, ti, :], x_pre[row0:row0 + 128, :], writes=[bH[ti]])
                    S.dma("sp", tabs[ti][:], tab_pre[row0:row0 + 128, :], writes=[btabs[ti]])
                rms_a(items)

            def loads_b():
                rms_b(items, G_MIX, Ug, bUg)
            pre_loads[g] = (loads_a, loads_b)

            def ug_lhs(kc, ti):
                return Ug[:, kc, ti * 128:(ti + 1) * 128]

            def ug_bufs(ti):
                return [bUg[ti]]
            steps = []
            kbuf = [k4, AR[:, 0:2048].rearrange("p (a b) -> p a b", a=4)]
            vbuf = [v4, AR[:, 6144:8192].rearrange("p (a b) -> p a b", a=4)]
            bkb_ = [bk4, bq4]
            bvb_ = [bv4, [bkd4, btr3[0], btr3[1], btr3[2]]]

            def state_mm(hg):
                bank = 6 + hg
                for ti in tiles:
                    for h in range(4):
                        lastf = last and ti == 3
                        P.mm(PB[bank][:, h * 128:(h + 1) * 128], kbuf[hg][:, ti, h * 128:(h + 1) * 128],
                             vbuf[hg][:, ti, h * 128:(h + 1) * 128], False, lastf, [bkb_[hg][ti], bvb_[hg][ti]], [bPB[bank]], skip=True)

            for hg in range(2):
                def rk_step(slot, bslot, hg=hg):
                    if hg == 0 and g == 0:
                        loads_a()
                        loads_b()

                    def evac(ti, ps, bps):
                        evac_rk(ps, bps, tabs[ti], btabs[ti], kbuf[hg][:, ti, :], bkb_[hg][ti])
                        kd3 = kbuf[hg][:, ti, :].rearrange("p (h d) -> p h d", h=4)
                        dec = tabs[ti][:, 176 + hg * 4:176 + hg * 4 + 4].unsqueeze(2).to_broadcast([128, 4, 128])
                        P.tt("dve", kd3, kd3, dec, ALU.mult, [bkb_[hg][ti], btabs[ti]], [bkb_[hg][ti]])
                    proj_tiles(slot, bslot, 16, 512, tiles, ug_lhs, ug_bufs, evac)
                    if hg == 1:
                        state_mm(0)

                def rv_step(slot, bslot, hg=hg):
                    def evac(ti, ps, bps):
                        P.cp(P.ev_eng(), vbuf[hg][:, ti, :], ps[:, 0:512], [bps], [bvb_[hg][ti]])
                    proj_tiles(slot, bslot, 16, 512, tiles, ug_lhs, ug_bufs, evac)
                    if (g + 1) in pre_loads:
                        pre_loads[g + 1][hg]()
                steps.append(([wseg(w_in[:, C_RK + hg * 512:C_RK + hg * 512 + 512], 16, 0, 512)], rk_step))
                steps.append(([wseg(w_in[:, C_RV + hg * 512:C_RV + hg * 512 + 512], 16, 0, 512)], rv_step))

            def kv_step(slot, bslot):
                def evac(ti, ps, bps):
                    evac_dkdv(ps, bps, tabs[ti], btabs[ti], g * 4 + ti)
                proj_tiles(slot, bslot, 16, 512, tiles, ug_lhs, ug_bufs, evac)
                state_mm(1)

            def ik_step(slot, bslot):
                def evac(ti, ps, bps):
                    evac_ik(ps, bps, tabs[ti], btabs[ti], g * 4 + ti)
                proj_tiles(slot, bslot, 16, 64, tiles, ug_lhs, ug_bufs, evac)
                if last:
                    prefix_finish()
                    P.pinned.discard(6)
                    P.pinned.discard(7)
            steps.append(([wseg(w_in[:, C_DK:C_DK + 512], 16, 0, 512)], kv_step))
            steps.append(([wseg(w_in[:, C_IK:C_IK + 64], 16, 0, 64)], ik_step))
            return steps

        def prefix_finish():
            for hg in range(2):
                P.cp("dve", S32[:, hg * 4:(hg + 1) * 4, :].rearrange("p h e -> p (h e)"), PB[6 + hg][:], [bPB[6 + hg]], [bS32[hg]])
                P.cp("act", Sbf[:, hg * 4:(hg + 1) * 4, :].rearrange("p h e -> p (h e)"), PB[6 + hg][:], [bPB[6 + hg]], [bSbf[hg]])

        def retention(ti, hg, smp):
            S32_, Sbf_, bS32_, bSbf_ = (S32s, Sbfs, bS32s, bSbfs) if smp else (S32, Sbf, bS32, bSbf)
            kdec = kdec_smp if smp else kdec_own
            cdec = cdec_smp if smp else cdec_own
            qsrc = q4[:, ti, :]
            ksrc = k4[:, ti, :]
            vsrc = v4[:, ti, :]
            P.tt("dve", kd4.rearrange("p (h d) -> p h d", h=4), ksrc.rearrange("p (h d) -> p h d", h=4),
                 kdec[:, hg * 4:hg * 4 + 4].unsqueeze(2).to_broadcast([128, 4, 128]), ALU.mult, [bk4[ti]] + RC, [bkd4])
            for which, (src, bsrc, rhs_of) in enumerate((
                    (qsrc, bq4[ti], lambda h: ident),
                    (ksrc, bk4[ti], lambda h: ident),
                    (qsrc, bq4[ti], lambda h: diagq[:, hg * 4 + h, :]))):
                b = P.nextps()
                for h in range(4):
                    P.mm(PB[b][:, h * 128:(h + 1) * 128], src[:, h * 128:(h + 1) * 128], rhs_of(h), True, True,
                         [bsrc] + RC, [bPB[b]])
                P.cp(P.ev_eng(), tr3[which], PB[b][:], [bPB[b]], [btr3[which]])
            qT, kT, qdT = tr3
            b = P.nextps()
            for h in range(4):
                P.mm(PB[b][:, h * 128:(h + 1) * 128], kT[:, h * 128:(h + 1) * 128], qT[:, h * 128:(h + 1) * 128], True, True,
                     [btr3[0], btr3[1]], [bPB[b]])
            P.tt("dve", smT, PB[b][:], intraT[:, hg * 4:(hg + 1) * 4, :].rearrange("p h i -> p (h i)"), ALU.mult,
                 [bPB[b]] + RC, [bsmT])
            bo = P.nextps()
            for h in range(4):
                P.mm(PB[bo][:, h * 128:(h + 1) * 128], smT[:, h * 128:(h + 1) * 128], vsrc[:, h * 128:(h + 1) * 128], True, False,
                     [bsmT, bv4[ti]], [bPB[bo]])
                P.mm(PB[bo][:, h * 128:(h + 1) * 128], qdT[:, h * 128:(h + 1) * 128], Sbf_[:, hg * 4 + h, :], False, True,
                     [btr3[2], bSbf_[hg]], [bPB[bo]])
            P.cp("act", orow, PB[bo][:], [bPB[bo]], [borow])
            bs = P.nextps()
            for h in range(4):
                P.mm(PB[bs][:, h * 128:(h + 1) * 128], kd4[:, h * 128:(h + 1) * 128], vsrc[:, h * 128:(h + 1) * 128], True, True,
                     [bkd4, bv4[ti]], [bPB[bs]])
            for h in range(4):
                P.stt("dve", S32_[:, hg * 4 + h, :], S32_[:, hg * 4 + h, :], cdec[:, hg * 4 + h:hg * 4 + h + 1],
                      PB[bs][:, h * 128:(h + 1) * 128], ALU.mult, ALU.add, [bPB[bs]] + RC, [bS32_[hg]])
            P.cp("act", Sbf_[:, hg * 4:(hg + 1) * 4, :], S32_[:, hg * 4:(hg + 1) * 4, :], [bS32_[hg]], [bSbf_[hg]])
            stt_, bst = new_stat()
            o3 = orow.rearrange("p (h e) -> p h e", h=4)
            P.red("dve", stt_[:, 0:4], o3, ALU.add, [borow], [bst])
            sq = tmpA[1]
            P.tt("dve", sq[:], orow, orow, ALU.mult, [borow], [btmpA[1]])
            P.red("dve", stt_[:, 4:8], sq[:].rearrange("p (h e) -> p h e", h=4), ALU.add, [btmpA[1]], [bst])
            stt2, bst2 = new_stat()
            P.ts("dve", stt2[:, 0:4], stt_[:, 0:4], 1.0 / 128, None, ALU.mult, None, [bst], [bst2])
            P.tt("dve", stt2[:, 4:8], stt2[:, 0:4], stt2[:, 0:4], ALU.mult, [bst2], [bst2])
            P.stt("dve", stt2[:, 4:8], stt_[:, 4:8], 1.0 / 128, stt2[:, 4:8], ALU.mult, ALU.subtract, [bst, bst2], [bst2])
            P.rsqrt(stt2[:, 4:8], stt2[:, 4:8], 1.0, bst2)
            for h in range(4):
                P.ts("dve", orow[:, h * 128:(h + 1) * 128], orow[:, h * 128:(h + 1) * 128], stt2[:, h:h + 1], stt2[:, 4 + h:5 + h],
                     ALU.subtract, ALU.mult, [borow, bst2], [borow])
            P.tt("dve", rpre, orow, sg4[:, ti, :], ALU.mult, [borow, bsg4[ti]], [brpre])
            transposes_to(rpre, brpre, 4, lambda j: retyT[:, hg * 4 + j, ti * 128:(ti + 1) * 128], lambda j: bretyT[ti],
                          scale_of=lambda j: gv[:, G_GN + hg * 4 + j:G_GN + hg * 4 + j + 1])

        def dsa_idx(ti, it, par):
            Isc_, bIsc_ = (Isc, bIsc) if par == 0 else (Isc2, bIsc2)
            nkc = 24 + it + 1
            Sk = nkc * 128
            nblk = (Sk + 511) // 512
            keyb = [bKT[c] for c in range(nkc)]
            for h in range(16):
                P.act(diagw[:, h, :], ident, AF.Copy, [biw[ti]] + RC, [bdiagw], scale=iwf[:, ti, h:h + 1])
            stt_, bst = new_stat()
            stt2, bst2 = new_stat()
            bIs = [P.nextps(), None]
            P.pinned.add(bIs[0])
            bIs[1] = P.nextps()
            P.pinned.add(bIs[1])
            items = [(kb, h) for kb in range(nblk) for h in range(16)]

            def stage_a(kb, h, i):
                c0 = kb * 512
                n = min(512, Sk - c0)
                b = P.nextps()
                hp = (h % 2) * 64
                P.mm(PB[b][:, 0:n], iqT[hp:hp + 64, ti, h // 2, :], ikT[hp:hp + 64, c0:c0 + n], True, True,
                     [biqT[ti]] + [bikT[c] for c in range(c0 // 128, (c0 + n) // 128)], [bPB[b]])
                r = i % 3
                P.act(rl[r][:, 0:n], PB[b][:, 0:n], AF.Relu, [bPB[b]], [brl[r]])

            def stage_b(kb, h, i):
                c0 = kb * 512
                n = min(512, Sk - c0)
                bI = bIs[kb % 2]
                r = i % 3
                P.mm(PB[bI][:, 0:n], diagw[:, h, :], rl[r][:, 0:n], h == 0, h == 15, [bdiagw, brl[r]], [bPB[bI]])
                if h == 15:
                    P.cp("act", Isc_[:, c0:c0 + n], PB[bI][:, 0:n], [bPB[bI]], bIsc_)
                    P.red("dve", stt_[:, kb:kb + 1], Isc_[:, c0:c0 + n], ALU.max, bIsc_, [bst])
                    P.red("dve", stt2[:, kb:kb + 1], Isc_[:, c0:c0 + n], ALU.min, bIsc_, [bst2])
                    if c0 < 3072:
                        P.ts("dve", Isc_[:, c0:c0 + n], Isc_[:, c0:c0 + n], slotb[:, c0 // 1024:c0 // 1024 + 1], None, ALU.add, None, bIsc_ + RC, bIsc_)
                    else:
                        d0 = 3072 + it * 128
                        if d0 >= c0 and d0 < c0 + n:
                            P.tt("dve", Isc_[:, d0:d0 + 128], Isc_[:, d0:d0 + 128], tri, ALU.add, bIsc_ + RC, bIsc_)
            LAG = 2
            for i, (kb, h) in enumerate(items):
                stage_a(kb, h, i)
                if i >= LAG:
                    stage_b(items[i - LAG][0], items[i - LAG][1], i - LAG)
            for i in range(max(0, len(items) - LAG), len(items)):
                stage_b(items[i][0], items[i][1], i)
            P.pinned.discard(bIs[0])
            P.pinned.discard(bIs[1])
            return dict(ti=ti, it=it, nkc=nkc, Sk=Sk, nblk=nblk, Isc=Isc_, bIsc=bIsc_, stt=stt_, bst=bst, stt2=stt2, bst2=bst2)

        def dsa_bis(c):
            ti, it, nkc, Sk, nblk = c["ti"], c["it"], c["nkc"], c["Sk"], c["nblk"]
            Isc_, bIsc_, stt_, bst, stt2, bst2 = c["Isc"], c["bIsc"], c["stt"], c["bst"], c["stt2"], c["bst2"]
            st3, bst3 = new_stat()
            P.red("dve", st3[:, 0:1], stt_[:, 0:nblk], ALU.max, [bst], [bst3])
            P.red("dve", st3[:, 1:2], stt2[:, 0:nblk], ALU.min, [bst2], [bst3])
            P.stt("dve", st3[:, 2:3], st3[:, 0:1], 1.0, st3[:, 1:2], ALU.add, ALU.subtract, [bst3], [bst3])
            P.ts("dve", st3[:, 2:3], st3[:, 2:3], 0.5, None, ALU.mult, None, [bst3], [bst3])
            cnts, bcn = new_stat()
            cnts2, bcn2 = new_stat()
            cnts3, bcn3 = new_stat()
            P.memset("dve", cnts[:], 0.0, [bcn])
            P.memset("dve", cnts2[:], 0.0, [bcn2])
            P.memset("dve", cnts3[:], 0.0, [bcn3])
            for itn in range(NIT):
                cc, bcc = (cnts, bcn) if itn < 8 else ((cnts2, bcn2) if itn < 16 else (cnts3, bcn3))
                col = itn % 8
                P.tt("dve", st3[:, 3:4], st3[:, 1:2], st3[:, 2:3], ALU.add, [bst3], [bst3])
                P.ts("dve", mbs[:, 0:Sk], Isc_[:, 0:Sk], st3[:, 3:4], 0.0, ALU.is_ge, ALU.add, bIsc_ + [bst3, bcc], [bmb, bcc],
                     accum=cc[:, col:col + 1])
                P.ts("dve", st3[:, 4:5], cc[:, col:col + 1], float(TOPK), st3[:, 2:3], ALU.is_ge, ALU.mult, [bcc, bst3], [bst3])
                P.tt("dve", st3[:, 1:2], st3[:, 1:2], st3[:, 4:5], ALU.add, [bst3], [bst3])
                P.ts("dve", st3[:, 2:3], st3[:, 2:3], 0.5, None, ALU.mult, None, [bst3], [bst3])
            P.ts("dve", mbs[:, 0:Sk], Isc_[:, 0:Sk], st3[:, 1:2], NEG, ALU.is_lt, ALU.mult, bIsc_ + [bst3], [bmb])

        def dsa_att(c):
            ti, it, nkc, Sk = c["ti"], c["it"], c["nkc"], c["Sk"]
            for g in range(2):
                bo = P.nextps()
                P.pinned.add(bo)
                bd = P.nextps()
                P.pinned.add(bd)

                def att_a(kc):
                    b = P.nextps()
                    P.mm(PB[b][:], KT[:, g, kc * 128:(kc + 1) * 128], QT[:, ti, g * 4:(g + 1) * 4, :].rearrange("p h t -> p (h t)"),
                         True, False, [bKT[kc], bQT[ti]], [bPB[b]])
                    P.mm(PB[b][:], mbs[:, kc * 128:(kc + 1) * 128], I4[:], False, True, [bmb] + RC, [bPB[b]])
                    r = kc % 3
                    P.act(PTb[r][:], PB[b][:], AF.Exp, [bPB[b]], [bPT[r]], scale=128.0 ** -0.5)

                def att_b(kc):
                    r = kc % 3
                    P.mm(PB[bo][:], Vb[:, kc, g * 128:(g + 1) * 128], PTb[r][:], kc == 0, kc == nkc - 1, [bVb[kc], bPT[r]], [bPB[bo]])
                    P.mm(PB[bd][:], ones[:], PTb[r][:], kc == 0, kc == nkc - 1, [bPT[r]] + RC, [bPB[bd]])
                LAG2 = 2
                for kc in range(nkc):
                    att_a(kc)
                    if kc >= LAG2:
                        att_b(kc - LAG2)
                for kc in range(max(0, nkc - LAG2), nkc):
                    att_b(kc)
                P.S.op("dve", lambda e, bd=bd: e.reciprocal(out=tmpA[0][:], in_=PB[bd][:]), [bPB[bd]], [btmpA[0]])
                for hh in range(4):
                    P.tt("dve", dsaoT[:, g * 4 + hh, ti * 128:(ti + 1) * 128], PB[bo][:, hh * 128:(hh + 1) * 128],
                         tmpA[0][:, hh * 128:(hh + 1) * 128], ALU.mult, [bPB[bo], btmpA[0]], [bdsaoT[ti]])
                P.pinned.discard(bo)
                P.pinned.discard(bd)

        def smp_index_prep():
            S.dma("sp", sPt, ptab, writes=[bsIdx])
            P.cp("dve", sIdxF[:, 31:32], sPt, [bsIdx], [bsIdx])
            for rb in range(16):
                P.ts("dve", sIdxF[:, rb:rb + 1], sIdxF[:, 31:32], 16.0, float(rb), ALU.mult, ALU.add, [bsIdx], [bsIdx])
            for rb in range(8):
                P.ts("dve", sIdxF[:, 16 + rb:17 + rb], sIdxF[:, 31:32], 8.0, float(rb), ALU.mult, ALU.add, [bsIdx], [bsIdx])
            P.cp("dve", sIdxI[:, 0:24], sIdxF[:, 0:24], [bsIdx], [bsIdx])

        def gather(dst, bdst, src2d, col, extra_w=()):
            S.op("pool", lambda e: e.indirect_dma_start(out=dst, out_offset=None, in_=src2d,
                                                        in_offset=bass.IndirectOffsetOnAxis(ap=sIdxI[:, col:col + 1], axis=0)),
                 [bsIdx], [bdst] + list(extra_w), dma=True)

        def dsa_sample(ti):
            SC = 128.0 ** -0.5
            P.memset("dve", dsaoT[:, :, ti * 128:(ti + 1) * 128], 0.0, [bdsaoT[ti]])
            X4 = sX[0:4, 0:64].rearrange("p (q t j) -> p t q j", t=4, q=2)
            iw4 = iwf[0:4, ti, :].rearrange("p (j q) -> p q j", q=2).unsqueeze(1).to_broadcast([4, 4, 2, 8])
            id4 = ident[0:4, 0:4].unsqueeze(2).unsqueeze(3).to_broadcast([4, 4, 2, 8])
            P.tt("dve", X4, iw4, id4, ALU.mult, [biw[ti]] + RC, [bsX])
            b = P.nextps()
            P.mm(PB[b][:, 0:64], ones[0:4, 0:128], sX[0:4, 0:64], True, True, [bsX] + RC, [bPB[b]])
            P.cp("dve", sWb, PB[b][:, 0:64], [bPB[b]], [bsWb])

            def score_batch(bDs, nr, dst_cols):
                n = nr * 32
                for par in range(2):
                    o0 = par * 256
                    P.act(sRl[:, o0:o0 + n], PB[bDs[par]][:, 0:n], AF.Relu, [bPB[bDs[par]]], [bsRl])
                    P.tt("dve", sTmp[:, o0:o0 + n].rearrange("p (r c) -> p r c", c=32), sRl[:, o0:o0 + n].rearrange("p (r c) -> p r c", c=32),
                         sWb[:, par * 32:(par + 1) * 32].unsqueeze(1).to_broadcast([128, nr, 32]), ALU.mult, [bsRl, bsWb], [bsTmp])
                    P.red("dve", sRl[:, o0:o0 + nr * 4], sTmp[:, o0:o0 + n].rearrange("p (a j) -> p a j", j=8), ALU.add, [bsTmp, bsRl], [bsRl])
                P.tt("dve", sIT[:, dst_cols], sRl[:, 0:nr * 4], sRl[:, 256:256 + nr * 4], ALU.add, [bsRl], [bsIT])

            def dots_mm(bDs, col0, ikT_ap, bik):
                for par in range(2):
                    outv = PB[bDs[par]][:, col0:col0 + 32]
                    rhs = iqT[par * 64:(par + 1) * 64, ti, :, 0:4].rearrange("p j t -> p t j")
                    P.mm(outv, ikT_ap[par * 64:(par + 1) * 64, :], rhs, True, True, [biqT[ti], bik], [bPB[bDs[par]]])

            gather(Ist[0], bIst[0], pool_ik2, 16, extra_w=bikT)
            for ib in range(8):
                if ib + 1 < 8:
                    gather(Ist[(ib + 1) % 2], bIst[(ib + 1) % 2], pool_ik2, 16 + ib + 1, extra_w=bikT if ib == 0 else ())
                src3 = Ist[ib % 2].rearrange("p (r d) -> p r d", d=64)
                dst3 = sIbf.rearrange("p (r d) -> p r d", d=128)
                P.cp("dve", dst3[:, :, 0:64], src3, [bIst[ib % 2]], [bH[3]])
                P.cp("act", dst3[:, :, 64:128], src3, [bIst[ib % 2]], [bH[3]])
                for half in range(2):
                    bDs = [P.nextps(), None]
                    P.pinned.add(bDs[0])
                    bDs[1] = P.nextps()
                    P.pinned.add(bDs[1])
                    for q in range(2):
                        qq = half * 2 + q
                        bt = P.nextps()
                        for rr in range(4):
                            r = qq * 4 + rr
                            P.mm(PB[bt][:, rr * 128:(rr + 1) * 128], dst3[:, r, :], ident, True, True, [bH[3]] + RC, [bPB[bt]])
                        P.cp(P.ev_eng(), sikTr[qq % 2], PB[bt][:], [bPB[bt]], [bsikTr[qq % 2]])
                        for rr in range(4):
                            dots_mm(bDs, (q * 4 + rr) * 32, sikTr[qq % 2][:, rr * 128:(rr + 1) * 128], bsikTr[qq % 2])
                    c0 = (ib * 16 + half * 8) * 4
                    score_batch(bDs, 8, slice(c0, c0 + 32))
                    P.pinned.discard(bDs[0])
                    P.pinned.discard(bDs[1])
            bDs = [P.nextps(), None]
            P.pinned.add(bDs[0])
            bDs[1] = P.nextps()
            P.pinned.discard(bDs[0])
            dots_mm(bDs, 0, sikTo, bsikTo)
            score_batch(bDs, 1, slice(512, 516))
            P.tt("dve", sIT[:, 512:516], sIT[:, 512:516], triT4, ALU.add, [bsIT] + RC, [bsIT])
            IT3 = sIT[:, 0:516].rearrange("p (c t) -> p t c", t=4)
            P.red("dve", sSt[:, 0:4], IT3, ALU.max, [bsIT], [bsSt])
            P.red("dve", sSt[:, 4:8], sIT[:, 0:512].rearrange("p (c t) -> p t c", t=4), ALU.min, [bsIT], [bsSt])
            P.cp("dve", sPcb[:, 0:8], sSt[:, 0:8], [bsSt], [bsPcb])
            b = P.nextps()
            P.mm(PB[b][0:4, 0:128], sPcb[:, 0:4], ident, True, True, [bsPcb] + RC, [bPB[b]])
            P.mm(PB[b][0:4, 128:256], sPcb[:, 4:8], ident, True, True, [bsPcb] + RC, [bPB[b]])
            g4 = sSt[0:4, 8:16]
            P.red("dve", g4[:, 0:1], PB[b][0:4, 0:128], ALU.max, [bPB[b]], [bsSt])
            P.red("dve", g4[:, 1:2], PB[b][0:4, 128:256], ALU.min, [bPB[b]], [bsSt])
            P.ts("dve", g4[:, 2:3], g4[:, 0:1], 1.02, None, ALU.mult, None, [bsSt], [bsSt])
            P.ts("dve", g4[:, 4:5], g4[:, 0:1], 0.98, None, ALU.mult, None, [bsSt], [bsSt])
            P.tt("dve", g4[:, 3:4], g4[:, 2:3], g4[:, 4:5], ALU.max, [bsSt], [bsSt])
            P.ts("dve", g4[:, 3:4], g4[:, 3:4], 1.0, None, ALU.add, None, [bsSt], [bsSt])
            P.ts("dve", g4[:, 2:3], g4[:, 1:2], 1.02, None, ALU.mult, None, [bsSt], [bsSt])
            P.ts("dve", g4[:, 4:5], g4[:, 1:2], 0.98, None, ALU.mult, None, [bsSt], [bsSt])
            P.tt("dve", g4[:, 5:6], g4[:, 2:3], g4[:, 4:5], ALU.min, [bsSt], [bsSt])
            P.ts("dve", g4[:, 5:6], g4[:, 5:6], -1.0, None, ALU.add, None, [bsSt], [bsSt])
            D8 = sX[0:4, 0:8]
            P.ts("dve", D8[:, 0:4], ident[0:4, 0:4], g4[:, 3:4], None, ALU.mult, None, [bsSt] + RC, [bsX])
            P.ts("dve", D8[:, 4:8], ident[0:4, 0:4], g4[:, 5:6], None, ALU.mult, None, [bsSt] + RC, [bsX])
            b = P.nextps()
            P.mm(PB[b][:, 0:8], ones[0:4, 0:128], D8, True, True, [bsX] + RC, [bPB[b]])
            hi_b = sSt[:, 16:20]; thr = sSt[:, 20:24]; step = sSt[:, 24:28]; cand = sSt[:, 28:32]; mt = sSt[:, 32:36]
            P.cp("dve", sSt[:, 16:24], PB[b][:, 0:8], [bPB[b]], [bsSt])
            P.tt("dve", step, hi_b, thr, ALU.subtract, [bsSt], [bsSt])
            P.ts("dve", step, step, 0.5, None, ALU.mult, None, [bsSt], [bsSt])
            IT3n = sIT[:, 0:516].rearrange("p (c t) -> p c t", t=4)
            G3n = sG[:, 0:516].rearrange("p (c t) -> p c t", t=4)
            G3 = sG[:, 0:516].rearrange("p (c t) -> p t c", t=4)
            for itn in range(20):
                P.tt("dve", cand, thr, step, ALU.add, [bsSt], [bsSt])
                P.tt("dve", G3n, IT3n, cand.unsqueeze(1).to_broadcast([128, 129, 4]), ALU.is_ge, [bsIT, bsSt], [bsG])
                P.red("dve", sSt[:, 36:40], G3, ALU.add, [bsG], [bsSt])
                P.cp("dve", sPcb[:, 0:4], sSt[:, 36:40], [bsSt], [bsPcb])
                b = P.nextps()
                P.mm(PB[b][:, 0:4], ones[:], sPcb[:, 0:4], True, True, [bsPcb] + RC, [bPB[b]])
                P.ts("dve", mt, PB[b][:, 0:4], float(TOPK), None, ALU.is_ge, None, [bPB[b]], [bsSt])
                P.tt("dve", mt, mt, step, ALU.mult, [bsSt], [bsSt])
                P.tt("dve", thr, thr, mt, ALU.add, [bsSt], [bsSt])
                P.ts("dve", step, step, 0.5, None, ALU.mult, None, [bsSt], [bsSt])
            M3n = sMask[:, 0:516].rearrange("p (c t) -> p c t", t=4)
            P.tt("dve", M3n, IT3n, thr.unsqueeze(1).to_broadcast([128, 129, 4]), ALU.is_ge, [bsIT, bsSt], [bsMask])
            bO = [P.nextps(), None]
            P.pinned.add(bO[0])
            bO[1] = P.nextps()
            P.pinned.add(bO[1])
            bDn = P.nextps()
            P.pinned.add(bDn)
            Kb3 = sKbf.rearrange("p (r c) -> p r c", c=256)
            Vb3 = sVbf.rearrange("p (r c) -> p r c", c=256)
            PT3 = sPT.rearrange("p (r c) -> p r c", c=32)

            def qrhs(g):
                return QT[:, ti, g * 4:(g + 1) * 4, 0:4]

            def softmax_pv(bL, nr, mcol0, vsrc_of, bv, first, last):
                n = nr * 32
                P.act(sE[:, 0:n], PB[bL][:, 0:n], AF.Exp, [bPB[bL]], [bsE], scale=SC)
                P.tt("dve", sPT[:, 0:n].rearrange("p (r a t) -> p r a t", a=8, t=4),
                     sE[:, 0:n].rearrange("p (r a t) -> p r a t", a=8, t=4),
                     sMask[:, mcol0:mcol0 + nr * 4].rearrange("p (r t) -> p r t", t=4).unsqueeze(2).to_broadcast([128, nr, 8, 4]),
                     ALU.mult, [bsE, bsMask], [bsPT])
                for r8 in range(nr):
                    st_ = first and r8 == 0
                    sp_ = last and r8 == nr - 1
                    for g in range(2):
                        P.mm(PB[bO[g]][:, 0:16], vsrc_of(r8, g), PT3[:, r8, g * 16:(g + 1) * 16], st_, sp_, [bv, bsPT], [bPB[bO[g]]])
                    P.mm(PB[bDn][:, 0:32], ones[:], PT3[:, r8, :], st_, sp_, [bsPT] + RC, [bPB[bDn]])

            gather(Kst[0], bKst[0], pool_k2, 0, extra_w=bKT)
            gather(Vst[0], bVst[0], pool_v2, 0, extra_w=bVb)
            H2bf = H[:, 2, :].bitcast(BF16)
            Kbf2 = [sKbf, H2bf[:, 0:2048]]
            Vbf2 = [sVbf, H2bf[:, 2048:4096]]
            bKV2 = [bH[1], bH[2]]
            KTr4 = [MR[:, 4896 + 512 * i:5408 + 512 * i] for i in range(4)]
            bKTr4 = [bsKTr[0], bsKTr[1], bsikTr[0], bsikTr[1]]
            for kb in range(16):
                if kb + 1 < 16:
                    gather(Kst[(kb + 1) % 2], bKst[(kb + 1) % 2], pool_k2, kb + 1, extra_w=bKT if kb == 0 else ())
                    gather(Vst[(kb + 1) % 2], bVst[(kb + 1) % 2], pool_v2, kb + 1, extra_w=bVb if kb == 0 else ())
                pz = kb % 2
                P.cp("dve", Kbf2[pz], Kst[kb % 2], [bKst[kb % 2]], [bKV2[pz]])
                P.cp("act", Vbf2[pz], Vst[kb % 2], [bVst[kb % 2]], [bKV2[pz]])
                Kb3 = Kbf2[pz].rearrange("p (r c) -> p r c", c=256)
                Vb3 = Vbf2[pz].rearrange("p (r c) -> p r c", c=256)
                bL = P.nextps()
                P.pinned.add(bL)
                bts = []
                for q in range(4):
                    bt = P.nextps()
                    P.pinned.add(bt)
                    bts.append(bt)
                    for rr in range(2):
                        for g in range(2):
                            P.mm(PB[bt][:, (rr * 2 + g) * 128:(rr * 2 + g + 1) * 128], Kb3[:, q * 2 + rr, g * 128:(g + 1) * 128], ident,
                                 True, True, [bKV2[pz]] + RC, [bPB[bt]])
                for q in range(4):
                    P.cp("act" if q % 2 == 0 else "dve", KTr4[q], PB[bts[q]][:], [bPB[bts[q]]], [bKTr4[q]])
                    P.pinned.discard(bts[q])
                for q in range(4):
                    for rr in range(2):
                        for g in range(2):
                            r8 = q * 2 + rr
                            P.mm(PB[bL][:, r8 * 32 + g * 16:r8 * 32 + g * 16 + 16], KTr4[q][:, (rr * 2 + g) * 128:(rr * 2 + g + 1) * 128],
                                 qrhs(g), True, True, [bKTr4[q], bQT[ti]], [bPB[bL]])
                softmax_pv(bL, 8, kb * 32, lambda r8, g, Vb3=Vb3: Vb3[:, r8, g * 128:(g + 1) * 128], bKV2[pz], kb == 0, False)
                P.pinned.discard(bL)
            bL = P.nextps()
            for g in range(2):
                P.mm(PB[bL][:, g * 16:(g + 1) * 16], sKTo[:, g * 128:(g + 1) * 128], qrhs(g), True, True, [bsKTo, bQT[ti]], [bPB[bL]])
            softmax_pv(bL, 1, 512, lambda r8, g: sVo[:, g * 128:(g + 1) * 128], bsVo, False, True)
            P.S.op("dve", lambda e: e.reciprocal(out=sSt[:, 0:32], in_=PB[bDn][:, 0:32]), [bPB[bDn]], [bsSt])
            for g in range(2):
                P.tt("dve", dsaoT[:, g * 4:(g + 1) * 4, ti * 128:ti * 128 + 4], PB[bO[g]][:, 0:16].rearrange("p (a t) -> p a t", t=4),
                     sSt[:, g * 16:(g + 1) * 16].rearrange("p (a t) -> p a t", t=4), ALU.mult, [bPB[bO[g]], bsSt], [bdsaoT[ti]])
            for bb in (bO[0], bO[1], bDn):
                P.pinned.discard(bb)

        def own_group(tiles_info, pre=None):
            T = len(tiles_info)
            tiles = list(range(T))
            N = T * 128

            def loads():
                if pre is not None:
                    pre()
                for ti, inf in enumerate(tiles_info):
                    S.dma("sp", H[:, ti, :], inf["x"], writes=[bH[ti]])
                    S.dma("sp", tab[ti][:], inf["tab"], writes=[btab[ti]])
                rmsnorm_T_multi([(H[:, ti, :], bH[ti], ti) for ti in tiles], G_MIX)
            steps = []
            for hg in range(2):
                def rq_step(slot, bslot, hg=hg):
                    if hg == 0:
                        loads()

                    def evac(ti, ps, bps):
                        i = P.ev_ctr % 2
                        P.ev_ctr += 1
                        P.cp("act", tmpA[i][:], ps[:, 0:512], [bps], [btmpA[i]])
                        rope("dve", tmpA[i][:].rearrange("p (h d) -> p h d", h=4), q4[:, ti, :].rearrange("p (h d) -> p h d", h=4),
                             tab[ti][:, 0:64], tab[ti][:, 64:128], 4, 64, 128, [btmpA[i], btab[ti]], [bq4[ti]])
                    proj_tiles(slot, bslot, 16, 512, tiles, u_lhs, u_bufs, evac)

                def rk_step(slot, bslot, hg=hg):
                    def evac(ti, ps, bps):
                        evac_rk(ps, bps, tab[ti], btab[ti], k4[:, ti, :], bk4[ti])
                    proj_tiles(slot, bslot, 16, 512, tiles, u_lhs, u_bufs, evac)

                def rv_step(slot, bslot, hg=hg):
                    def evac(ti, ps, bps):
                        P.cp(P.ev_eng(), v4[:, ti, :], ps[:, 0:512], [bps], [bv4[ti]])
                    proj_tiles(slot, bslot, 16, 512, tiles, u_lhs, u_bufs, evac)

                def rg_step(slot, bslot, hg=hg):
                    def evac(ti, ps, bps):
                        P.act(sg4[:, ti, :], ps[:, 0:512], AF.Silu, [bps], [bsg4[ti]])
                    proj_tiles(slot, bslot, 16, 512, tiles, u_lhs, u_bufs, evac)
                    for ti, inf in enumerate(tiles_info):
                        retention(ti, hg, inf["kind"] == "smp")
                for c0, fn in ((C_RQ, rq_step), (C_RK, rk_step), (C_RV, rv_step), (C_RG, rg_step)):
                    steps.append(([wseg(w_in[:, c0 + hg * 512:c0 + hg * 512 + 512], 16, 0, 512)], fn))
            if self.stop_after == "ret":
                return steps
            for blk in range(2):
                def dq_step(slot, bslot, blk=blk):
                    def evac(ti, ps, bps):
                        i = P.ev_ctr % 2
                        P.ev_ctr += 1
                        P.cp("act", tmpA[i][:], ps[:, 0:512], [bps], [btmpA[i]])
                        rope("dve", tmpA[i][:].rearrange("p (h d) -> p h d", h=4), bfA[i][:].rearrange("p (h d) -> p h d", h=4),
                             tab[ti][:, 128:144], tab[ti][:, 144:160], 4, 16, 128, [btmpA[i], btab[ti]], [bbfA[i]])
                        transposes_to(bfA[i], bbfA[i], 4, lambda j: QT[:, ti, blk * 4 + j, :], lambda j: bQT[ti])
                    proj_tiles(slot, bslot, 16, 512, tiles, u_lhs, u_bufs, evac)
                steps.append(([wseg(w_in[:, C_DQ + blk * 512:C_DQ + blk * 512 + 512], 16, 0, 512)], dq_step))

            if self.stop_after == "dq":
                return steps

            def kv_step(slot, bslot):
                def evac(ti, ps, bps):
                    inf = tiles_info[ti]
                    if inf["kind"] == "own":
                        evac_dkdv(ps, bps, tab[ti], btab[ti], 24 + inf["it"], inf["k_out"], inf["v_out"], True)
                    else:
                        evac_dkdv(ps, bps, tab[ti], btab[ti], 32, inf["k_out"], inf["v_out"], False)
                        P.cp("dve", kbf, kvf[:, 0:256], [bkvf], [bkbf])
                        P.cp("dve", sVo, kvf[:, 256:512], [bkvf], [bsVo])
                        transposes_to(kbf, bkbf, 2, lambda j: sKTo[:, j * 128:(j + 1) * 128], lambda j: bsKTo)
                proj_tiles(slot, bslot, 16, 512, tiles, u_lhs, u_bufs, evac)
            steps.append(([wseg(w_in[:, C_DK:C_DK + 512], 16, 0, 512)], kv_step))
            if self.stop_after == "kv":
                return steps
            for blk in range(2):
                def iq_step(slot, bslot, blk=blk):
                    def evac(ti, ps, bps):
                        i = P.ev_ctr % 2
                        P.ev_ctr += 1
                        P.cp("act", tmpA[i][:], ps[:, 0:512], [bps], [btmpA[i]])
                        rope("dve", tmpA[i][:].rearrange("p (h d) -> p h d", h=8), bfA[i][:].rearrange("p (h d) -> p h d", h=8),
                             tab[ti][:, 160:168], tab[ti][:, 168:176], 8, 8, 64, [btmpA[i], btab[ti]], [bbfA[i]])
                        transposes_to(bfA[i], bbfA[i], 4, lambda j: iqT[:, ti, blk * 4 + j, :], lambda j: biqT[ti])
                    proj_tiles(slot, bslot, 16, 512, tiles, u_lhs, u_bufs, evac)
                steps.append(([wseg(w_in[:, C_IQ + blk * 512:C_IQ + blk * 512 + 512], 16, 0, 512)], iq_step))

            if self.stop_after == "iq":
                return steps

            def ikw_step(slot, bslot):
                def evac(ti, ps, bps):
                    inf = tiles_info[ti]
                    P.cp("dve", iwf[:, ti, :], ps[:, 64:80], [bps], [biw[ti]])
                    if self.stop_after == "projA":
                        return
                    if inf["kind"] == "own":
                        evac_ik(ps, bps, tab[ti], btab[ti], 24 + inf["it"], inf["ik_out"], True)
                    else:
                        evac_ik(ps, bps, tab[ti], btab[ti], 32, inf["ik_out"], False)
                        P.cp("dve", ik2[:, 0:64], ikf[:], [bikf], [bik2])
                        P.cp("dve", ik2[:, 64:128], ikf[:], [bikf], [bik2])
                        transposes_to(ik2, bik2, 1, lambda j: sikTo, lambda j: bsikTo)
                proj_tiles(slot, bslot, 16, 80, tiles, u_lhs, u_bufs, evac)
                if self.stop_after in ("proj", "projA", "projB", "projC", "projD"):
                    return
                run_deferred(flush=True)
                if tiles_info[0]["kind"] == "own":
                    ctx = [None] * T
                    ctx[0] = dsa_idx(0, tiles_info[0]["it"], 0)
                    for ti in range(T):
                        dsa_bis(ctx[ti])
                        if ti + 1 < T:
                            ctx[ti + 1] = dsa_idx(ti + 1, tiles_info[ti + 1]["it"], (ti + 1) % 2)
                            dsa_att(ctx[ti])
                        else:
                            defer(lambda c=ctx[ti]: dsa_att(c), 3)
                else:
                    dsa_sample(0)
            steps.append(([wseg(w_in[:, C_IK:C_IK + 128], 16, 0, 128)], ikw_step))
            if self.stop_after in ("proj", "projA", "projB", "projC", "projD"):
                return steps
            for cb in range(4):
                def ga_step(slot, bslot, cb=cb):
                    def evac(ti, ps, bps):
                        P.act(sg4[:, ti, :], ps[:, 0:512], AF.Sigmoid, [bps], [bsg4[ti]])
                    proj_tiles(slot, bslot, 16, 512, tiles, u_lhs, u_bufs, evac)

                def ro_step(slot, bslot, cb=cb):
                    def evac(ti, ps, bps):
                        P.tt("dve", m1[:, ti, :], ps[:, 0:512], sg4[:, ti, :], ALU.mult, [bps, bsg4[ti]], [bm1[ti]])
                    proj_tiles(slot, bslot, 8, 512, tiles, lambda kc, ti: retyT[:, kc, ti * 128:(ti + 1) * 128],
                               lambda ti: [bretyT[ti]], evac)

                def gb_step(slot, bslot, cb=cb):
                    def evac(ti, ps, bps):
                        P.act(sg4[:, ti, :], ps[:, 0:512], AF.Sigmoid, [bps], [bsg4[ti]])
                    proj_tiles(slot, bslot, 16, 512, tiles, u_lhs, u_bufs, evac)

                def do_step(slot, bslot, cb=cb):
                    def evac(ti, ps, bps):
                        i = P.ev_ctr % 2
                        P.ev_ctr += 1
                        P.tt("dve", tmpA[i][:], ps[:, 0:512], sg4[:, ti, :], ALU.mult, [bps, bsg4[ti]], [btmpA[i]])
                        P.tt("dve", bfA[i][:], tmpA[i][:], m1[:, ti, :], ALU.add, [btmpA[i], bm1[ti]], [bbfA[i]])
                        transposes_to(bfA[i], bbfA[i], 4, lambda j: mrgT[:, cb * 4 + j, ti * 128:(ti + 1) * 128], lambda j: bmrgT[ti])
                    proj_tiles(slot, bslot, 8, 512, tiles, lambda kc, ti: dsaoT[:, kc, ti * 128:(ti + 1) * 128],
                               lambda ti: [bdsaoT[ti]], evac)
                steps.append(([wseg(w_in[:, C_GA + cb * 512:C_GA + cb * 512 + 512], 16, 0, 512)], ga_step))
                steps.append(([wseg(w_ro[:, cb * 512:cb * 512 + 512], 8, 0, 512)], ro_step))
                steps.append(([wseg(w_in[:, C_GB + cb * 512:C_GB + cb * 512 + 512], 16, 0, 512)], gb_step))
                steps.append(([wseg(w_do[:, cb * 512:cb * 512 + 512], 8, 0, 512)], do_step))
            for cb in range(4):
                def wo_step(slot, bslot, cb=cb):
                    if cb == 0:
                        for ti, inf in enumerate(tiles_info):
                            S.dma("sp", H[:, ti, :], inf["x"], writes=[bH[ti]])

                    def evac(ti, ps, bps):
                        hs = H[:, ti, cb * 512:(cb + 1) * 512]
                        P.tt("dve", hs, ps[:, 0:512], hs, ALU.add, [bps, bH[ti]], [bH[ti]])
                    proj_tiles(slot, bslot, 16, 512, tiles, lambda kc, ti: mrgT[:, kc, ti * 128:(ti + 1) * 128],
                               lambda ti: [bmrgT[ti]], evac)
                steps.append(([wseg(w_o[:, cb * 512:cb * 512 + 512], 16, 0, 512)], wo_step))
            smp_group = tiles_info[0]["kind"] == "smp"
            mkT_, mvb_, bmkT_, bmvb_ = (mkTs, mvbs, bmkTs, bmvbs) if smp_group else (mkT, mvb, bmkT, bmvb)

            def mq_step(slot, bslot):
                rmsnorm_T_multi([(H[:, ti, :], bH[ti], ti) for ti in tiles], G_MEM)
                for hd in range(4):
                    b = P.nextps()
                    for kc in range(16):
                        P.mm(PB[b][:, 0:N], slot[:, kc, hd * 128:(hd + 1) * 128], U[:, kc, 0:N], kc == 0, kc == 15,
                             [bU[t] for t in tiles] + bslot, [bPB[b]])
                    P.cp(P.ev_eng(), qmT[:, hd, 0:N], PB[b][:, 0:N], [bPB[b]], [bqmT])
                for hd in range(4):
                    for mc in range(2):
                        b = P.nextps()
                        P.mm(PB[b][:, 0:N], mkT_[:, hd, mc * 128:(mc + 1) * 128], qmT[:, hd, 0:N], True, True, [bmkT_, bqmT], [bPB[b]])
                        P.act(PTm[:, mc, 0:N], PB[b][:, 0:N], AF.Exp, [bPB[b]], [bPTm], scale=128.0 ** -0.5)
                    bo = P.nextps()
                    bd = P.nextps()
                    for mc in range(2):
                        P.mm(PB[bo][:, 0:N], mvb_[:, mc, hd * 128:(hd + 1) * 128], PTm[:, mc, 0:N], mc == 0, mc == 1, [bmvb_, bPTm], [bPB[bo]])
                    for mc in range(2):
                        P.mm(PB[bd][:, 0:N], ones[:], PTm[:, mc, 0:N], mc == 0, mc == 1, [bPTm] + RC, [bPB[bd]])
                    P.S.op("dve", lambda e, bd=bd: e.reciprocal(out=recf2[:, 0:N], in_=PB[bd][:, 0:N]), [bPB[bd]], [brec2])
                    P.tt("dve", omT[:, hd, 0:N], PB[bo][:, 0:N], recf2[:, 0:N], ALU.mult, [bPB[bo], brec2], [bomT])
            steps.append(([wseg(w_mq, 16, 0, 512)], mq_step))
            for cb in range(4):
                def mo_step(slot, bslot, cb=cb):
                    def evac(ti, ps, bps):
                        hs = H[:, ti, cb * 512:(cb + 1) * 512]
                        P.tt("dve", hs, ps[:, 0:512], hs, ALU.add, [bps, bH[ti]], [bH[ti]])
                    proj_tiles(slot, bslot, 4, 512, tiles, lambda kc, ti: omT[:, kc, ti * 128:(ti + 1) * 128],
                               lambda ti: [bomT], evac)
                steps.append(([wseg(w_mo[:, cb * 512:cb * 512 + 512], 4, 0, 512)], mo_step))
            for half in range(2):
                for j in range(8):
                    def up_step(slot, bslot, half=half, j=j):
                        if half == 0 and j == 0:
                            rmsnorm_T_multi([(H[:, ti, :], bH[ti], ti) for ti in tiles], G_MLP)
                        for fb in range(4):
                            b = P.nextps()
                            for kc in range(16):
                                P.mm(PB[b][:, 0:N], slot[:, kc, fb * 128:(fb + 1) * 128], U[:, kc, 0:N], kc == 0, kc == 15,
                                     [bU[t] for t in tiles] + bslot, [bPB[b]])
                            i = P.ev_ctr % 2
                            P.ev_ctr += 1
                            P.act(tmpA[i][:, 0:N], PB[b][:, 0:N], AF.Relu, [bPB[b]], [btmpA[i]])
                            P.tt("dve", aT[:, j * 4 + fb, 0:N], tmpA[i][:, 0:N], tmpA[i][:, 0:N], ALU.mult, [btmpA[i]], [baT])
                    c0 = half * 4096 + j * 512
                    steps.append(([wseg(w_up[:, c0:c0 + 512], 16, 0, 512)], up_step))
                for cb in range(4):
                    for qq in range(2):
                        def dn_step(slot, bslot, half=half, cb=cb, qq=qq):
                            if qq == 0:
                                P._acc = []
                                for ti in tiles:
                                    b = P.nextps()
                                    P.pinned.add(b)
                                    P._acc.append(b)
                            for ti in tiles:
                                b = P._acc[ti]
                                for kc in range(16):
                                    P.mm(PB[b][:], aT[:, qq * 16 + kc, ti * 128:(ti + 1) * 128], slot[:, kc, :],
                                         qq == 0 and kc == 0, qq == 1 and kc == 15, [baT] + bslot, [bPB[b]])
                            if qq == 1:
                                for ti in tiles:
                                    b = P._acc[ti]
                                    hs = H[:, ti, cb * 512:(cb + 1) * 512]
                                    P.tt("dve", hs, PB[b][:], hs, ALU.add, [bPB[b], bH[ti]], [bH[ti]])
                                    P.pinned.discard(b)
                        r0 = half * 4096 + qq * 2048
                        steps.append(([wseg(w_dn[r0:r0 + 2048, cb * 512:cb * 512 + 512], 16, 0, 512)], dn_step))

            def fin_step(slot, bslot):
                S.dma("sp", gfin, gfin_d, writes=[baT])
                for ti, inf in enumerate(tiles_info):
                    stt_, bst = new_stat()
                    P.memset("dve", stt_[:, 0:1], 0.0, [bst])
                    P.act(xn4[ti], H[:, ti, :], AF.Square, [bH[ti], bst], [bxn4[ti], bst], accum=stt_[:, 0:1])
                    P.rsqrt(stt_[:, 1:2], stt_[:, 0:1], 1.0 / D, bst)
                    P.stt("dve", H[:, ti, :], H[:, ti, :], stt_[:, 1:2], gfin, ALU.mult, ALU.mult, [bH[ti], bst, baT], [bH[ti]])
                    S.dma("sp", inf["y_out"], H[:, ti, :], reads=[bH[ti]])
            steps.append(([], fin_step))
            return steps

        steps = []
        steps += phase_mem()
        if self.stop_after == "mem":
            run_steps(steps)
            S.emit_all()
            return nc
        for g in range(self.n_pre_groups):
            steps += prefix_group(g, g == self.n_pre_groups - 1)

        def zero_init():
            for hg in range(2):
                P.memset("dve", S32[:, hg * 4:(hg + 1) * 4, :], 0.0, [bS32[hg]])
                P.memset("dve", Sbf[:, hg * 4:(hg + 1) * 4, :], 0.0, [bSbf[hg]])
            P.memset("dve", KT[:], 0.0, bKT)
            P.memset("dve", Vb[:], 0.0, bVb)
            P.memset("dve", ikT[:], 0.0, bikT)
        for og in range(self.n_own_groups):
            infos = []
            for t in range(4):
                it = og * 4 + t
                r0 = it * 128
                infos.append(dict(kind="own", it=it, x=x_own[r0:r0 + 128, :], tab=tab_own[r0:r0 + 128, :],
                                  k_out=k_own[r0:r0 + 128, :], v_out=v_own[r0:r0 + 128, :], ik_out=ik_own[r0:r0 + 128, :],
                                  y_out=y_own[r0:r0 + 128, :]))
            steps += own_group(infos, pre=zero_init if (og == 0 and self.n_pre_groups == 0) else None)

        def st_out_step(slot, bslot):
            for hg in range(2):
                S.dma("sp", st_own[hg * 4:(hg + 1) * 4].rearrange("h d e -> d h e"), S32[:, hg * 4:(hg + 1) * 4, :], reads=[bS32[hg]])
        if self.n_own_groups > 0:
            steps.append(([], st_out_step))
        if self.do_sample:
            def smp_pre():
                for hg in range(2):
                    S.dma("sp", S32s[:, hg * 4:(hg + 1) * 4, :], st_in[hg * 4:(hg + 1) * 4].rearrange("h d e -> d h e"), writes=[bS32s[hg]])
                    P.cp("dve", Sbfs[:, hg * 4:(hg + 1) * 4, :], S32s[:, hg * 4:(hg + 1) * 4, :], [bS32s[hg]], [bSbfs[hg]])
                load_sample_mem()
                smp_index_prep()
            infos = [dict(kind="smp", it=0, x=x_smp[:, :], tab=tab_smp[:, :], k_out=k_smp[:, :], v_out=v_smp[:, :],
                          ik_out=ik_smp[:, :], y_out=y_smp[:, :])]
            steps += own_group(infos, pre=smp_pre)

            def st_out_s(slot, bslot):
                for hg in range(2):
                    S.dma("sp", st_smp[hg * 4:(hg + 1) * 4].rearrange("h d e -> d h e"), S32s[:, hg * 4:(hg + 1) * 4, :], reads=[bS32s[hg]])
            steps.append(([], st_out_s))
        run_steps(steps)
        S.emit_all()
        return nc


def _rope_tab(pos, n_half, theta):
    inv = (np.float32(theta) ** (-(np.arange(n_half, dtype=np.float32) / np.float32(n_half)))).astype(np.float32)
    ang = pos.astype(np.float32)[:, None] * inv[None, :]
    return np.cos(ang).astype(np.float32), np.sin(ang).astype(np.float32)


def _gammas():
    return np.log1p(-np.exp2(-5.0 - np.arange(8, dtype=np.float64)))


def _tab(pos, kdec=None):
    n = pos.shape[0]
    t = np.zeros((n, 192), np.float32)
    c, s = _rope_tab(pos, 64, 10000.0)
    t[:, 0:64], t[:, 64:128] = c, s
    c, s = _rope_tab(pos, 16, 500000.0)
    t[:, 128:144], t[:, 144:160] = c, s
    c, s = _rope_tab(pos, 8, 500000.0)
    t[:, 160:168], t[:, 168:176] = c, s
    if kdec is not None:
        t[:, 176:184] = kdec
    return t


def _consts(p):
    lg = _gammas()
    c = np.zeros((128, 64), np.float32)
    cb = np.zeros((128, 2304), np.float32)
    cb[:, 0:128] = np.eye(128, dtype=np.float32)
    j = np.arange(128)
    cb[:, 128:256] = np.where(j[None, :] <= j[:, None], 0.0, NEG)
    diff = (j[None, :] - j[:, None]).astype(np.float64)
    for h in range(8):
        cb[:, 256 + h * 128:256 + (h + 1) * 128] = np.where(diff >= 0, np.exp(lg[h] * np.maximum(diff, 0.0)), 0.0)
        qd = np.exp(lg[h] * (j + 1.0))
        cb[:, 1280 + h * 128:1280 + (h + 1) * 128] = np.diag(qd)
        c[:, 0 + h] = qd
        c[:, 8 + h] = np.exp(lg[h] * (127.0 - j))
        c[:, 16 + h] = np.exp(lg[h] * (3.0 - j))
        c[:, 24 + h] = np.exp(lg[h] * 128.0)
        c[:, 32 + h] = np.exp(lg[h] * 4.0)
    for v in range(3):
        c[:, 40 + v] = 0.0 if v < p else NEG
    for t in range(4):
        c[:, 44 + t] = np.where(j <= t, 0.0, NEG)
    return c, cb


_PROG_CACHE = {}


def _get_prog(**kw):
    key = tuple(sorted(kw.items()))
    if key not in _PROG_CACHE:
        _PROG_CACHE[key] = Prog(**kw).build()
    return _PROG_CACHE[key]


def make_in_maps(inp, cores=range(8)):
    f = lambda a: np.ascontiguousarray(np.asarray(a, dtype=np.float32))
    xp = f(inp["x_prompt"]); xs = f(inp["x_sample"]); memp = f(inp["mem_prompt"])
    lg = _gammas()
    gvec = np.zeros((128, 96), np.float32)
    for off, name in ((0, "g_mix"), (16, "g_mem"), (32, "g_memkv"), (48, "g_mlp")):
        gvec[:, off:off + 16] = f(inp[name])[0].reshape(16, 128).T
    gvec[:, 64:72] = f(inp["gn_ret"])[0].reshape(8, 128).T
    gfin = np.ascontiguousarray(np.broadcast_to(f(inp["g_final"])[None, :], (128, D)))
    shared = dict(
        cbf=_consts(0)[1], gvec=gvec, gfin=gfin,
        w_in=f(inp["w_in"])[0], w_ret_out=f(inp["w_ret_out"])[0], w_dsa_out=f(inp["w_dsa_out"])[0], w_o=f(inp["w_o"])[0],
        w_mq=f(inp["w_mq"])[0], w_mk=f(inp["w_mk"])[0], w_mv=f(inp["w_mv"])[0], w_mo=f(inp["w_mo"])[0],
        w_up=f(inp["w_up"])[0], w_down=f(inp["w_down"])[0],
    )
    if USE_POOLS:
        shared.update(pool_k=f(inp["cache_k"])[0].reshape(1280 * 16, 2048), pool_v=f(inp["cache_v"])[0].reshape(1280 * 16, 2048),
                      pool_ik=f(inp["cache_idx_k"])[0].reshape(1280 * 8, 1024))
    pt = np.asarray(inp["page_table"]).astype(np.int32)
    maps = []
    for c in cores:
        b, p = c // 4, c % 4
        m = dict(shared)
        m["x_own"] = np.ascontiguousarray(xp[b, 1024 * p:1024 * (p + 1)])
        xpre = np.zeros((3072, D), np.float32)
        xpre[:1024 * p] = xp[b, :1024 * p]
        m["x_pre"] = xpre
        xsm = np.zeros((128, D), np.float32)
        xsm[:4] = xs[c]
        m["x_smp"] = xsm
        m["mem"] = np.ascontiguousarray(memp[b])
        pos_own = np.arange(1024 * p, 1024 * (p + 1))
        m["tab_own"] = _tab(pos_own)
        pos_pre = np.arange(3072)
        kd = np.zeros((3072, 8), np.float64)
        valid = pos_pre < 1024 * p
        ex = (1024 * p - 1 - pos_pre).astype(np.float64)
        for h in range(8):
            kd[:, h] = np.where(valid, np.exp(lg[h] * np.maximum(ex, 0.0)), 0.0)
        m["tab_pre"] = _tab(pos_pre, kd.astype(np.float32))
        pos_s = PAST + np.arange(128)
        m["tab_smp"] = _tab(pos_s)
        m["cst"] = _consts(p)[0]
        m["state_smp"] = np.ascontiguousarray(f(inp["state_ret"])[0, c])
        m["cmk"] = np.ascontiguousarray(f(inp["cache_mem_k"])[0, c].reshape(256, 512))
        m["cmv"] = np.ascontiguousarray(f(inp["cache_mem_v"])[0, c].reshape(256, 512))
        if USE_POOLS:
            m["ptab"] = np.ascontiguousarray(pt[c].reshape(128, 1))
        maps.append(m)
    return maps


def assemble(res):
    y_p = np.zeros((2, 4096, D), np.float32)
    y_s = np.zeros((8, 4, D), np.float32)
    st_p = np.zeros((1, 2, 8, 128, 128), np.float32)
    k_p = np.zeros((1, 2, 4096, 2, 128), np.float32)
    v_p = np.zeros((1, 2, 4096, 2, 128), np.float32)
    ik_p = np.zeros((1, 2, 4096, 64), np.float32)
    mk_p = np.zeros((1, 2, 256, 4, 128), np.float32)
    mv_p = np.zeros((1, 2, 256, 4, 128), np.float32)
    st_s = np.zeros((1, 8, 8, 128, 128), np.float32)
    k_s = np.zeros((1, 8, 4, 2, 128), np.float32)
    v_s = np.zeros((1, 8, 4, 2, 128), np.float32)
    ik_s = np.zeros((1, 8, 4, 64), np.float32)
    for c, r in enumerate(res):
        b, p = c // 4, c % 4
        sl = slice(1024 * p, 1024 * (p + 1))
        y_p[b, sl] = r["y_own"]
        k_p[0, b, sl] = r["k_own"].reshape(1024, 2, 128)
        v_p[0, b, sl] = r["v_own"].reshape(1024, 2, 128)
        ik_p[0, b, sl] = r["ik_own"]
        if p == 3:
            st_p[0, b] = r["st_own"]
        if p == 0:
            mk_p[0, b] = r["mk_o"].reshape(256, 4, 128)
            mv_p[0, b] = r["mv_o"].reshape(256, 4, 128)
        y_s[c] = r["y_smp"][:4]
        st_s[0, c] = r["st_smp"]
        k_s[0, c] = r["k_smp"][:4].reshape(4, 2, 128)
        v_s[0, c] = r["v_smp"][:4].reshape(4, 2, 128)
        ik_s[0, c] = r["ik_smp"][:4]
    return (y_p, y_s, st_p, k_p, v_p, ik_p, mk_p, mv_p, st_s, k_s, v_s, ik_s)


def kernel(**inputs):
    nc = _get_prog()
    in_maps = make_in_maps(inputs)
    res = run_bass_kernel_spmd(nc, in_maps, core_ids=list(range(8)))
    return assemble(res.results)
```

```python
import numpy as np
from contextlib import ExitStack
import concourse.bass as bass
import concourse.mybir as mybir
from concourse.bass_utils import run_bass_kernel_spmd

F32 = mybir.dt.float32
BF16 = mybir.dt.bfloat16
I32 = mybir.dt.int32
ALU = mybir.AluOpType
AF = mybir.ActivationFunctionType
AX = mybir.AxisListType

D = 2048
NIN = 10832
C_RQ, C_RK, C_RV, C_RG, C_DQ, C_DK, C_DV, C_IQ, C_IK, C_IW, C_GA, C_GB = (
    0, 1024, 2048, 3072, 4096, 5120, 5376, 5632, 6656, 6720, 6736, 8784)
EPS = 1e-6
PAST = 16384
NEG = -30000.0
NIT = 16
TOPK = 256
ENGS = ("pe", "act", "dve", "pool", "sp")
USE_POOLS = True


class Buf:
    __slots__ = ("name", "last_w", "readers", "excl", "last_any")

    def __init__(self, name="", excl=False):
        self.name = name
        self.last_w = None
        self.readers = []
        self.excl = excl
        self.last_any = None


class Op:
    __slots__ = ("eng", "emit", "deps", "is_dma", "sem", "val", "signal")

    def __init__(self, eng, emit, is_dma):
        self.eng = eng
        self.emit = emit
        self.deps = []
        self.is_dma = is_dma
        self.sem = None
        self.val = None
        self.signal = is_dma


class Sched:
    def __init__(self, nc, n_dma_sems=24):
        self.nc = nc
        self.ops = {e: [] for e in ENGS}
        self.n_dma_sems = n_dma_sems

    def op(self, eng, emit, reads=(), writes=(), dma=False):
        o = Op(eng, emit, dma)
        deps = []
        for b in reads:
            if b.last_w is not None:
                deps.append(b.last_w)
        for b in writes:
            if b.last_w is not None:
                deps.append(b.last_w)
            deps.extend(b.readers)
        for b in list(reads) + list(writes):
            if b.excl:
                if b.last_any is not None and b.last_any.eng != eng:
                    deps.append(b.last_any)
                b.last_any = o
        seen = set()
        for d in deps:
            if id(d) in seen or d is o:
                continue
            seen.add(id(d))
            if (not d.is_dma) and (not dma) and d.eng == "pe" and eng == "pe":
                continue
            o.deps.append(d)
            d.signal = True
        for b in reads:
            b.readers.append(o)
        for b in writes:
            b.last_w = o
            b.readers = []
        self.ops[eng].append(o)
        return o

    def dma(self, q, out, in_, reads=(), writes=()):
        return self.op(q, lambda e: e.dma_start(out=out, in_=in_), reads, writes, dma=True)

    def emit_all(self):
        nc = self.nc
        with ExitStack() as st:
            esem = {e: st.enter_context(nc.semaphore(f"s_{e}")) for e in ENGS}
            dsem = {q: [st.enter_context(nc.semaphore(f"d_{q}{i}")) for i in range(self.n_dma_sems)]
                    for q in ("sp", "pool", "act")}
            for e in ENGS:
                cnt = 0
                k = 0
                dcount = [0] * self.n_dma_sems
                for o in self.ops[e]:
                    if o.is_dma:
                        i = k % self.n_dma_sems
                        k += 1
                        dcount[i] += 16
                        o.sem = dsem[e][i]
                        o.val = dcount[i]
                    elif o.signal:
                        cnt += 1
                        o.sem = esem[e]
                        o.val = cnt
            block = st.enter_context(nc.Block())
            ops = self.ops

            def gen(e, engobj):
                clock = {}

                def wait(sem, val):
                    if clock.get(id(sem), 0) >= val:
                        return
                    engobj.wait_ge(sem, val)
                    clock[id(sem)] = val
                for o in ops[e]:
                    need = {}
                    for d in o.deps:
                        k = id(d.sem)
                        if k not in need or need[k][1] < d.val:
                            need[k] = (d.sem, d.val)
                    if o.is_dma and o.val > 16:
                        k = id(o.sem)
                        if k not in need or need[k][1] < o.val - 16:
                            need[k] = (o.sem, o.val - 16)
                    for sem, val in need.values():
                        wait(sem, val)
                    ins = o.emit(engobj)
                    if o.signal:
                        ins.then_inc(o.sem, 16 if o.is_dma else 1)
                if e == "sp":
                    for q in ("sp", "pool", "act"):
                        last = {}
                        for o in ops[q]:
                            if o.is_dma:
                                last[id(o.sem)] = (o.sem, o.val)
                        for sem, val in last.values():
                            wait(sem, val)

            @block.tensor
            def _(eng):
                gen("pe", eng)

            @block.scalar
            def _(eng):
                gen("act", eng)

            @block.vector
            def _(eng):
                gen("dve", eng)

            @block.gpsimd
            def _(eng):
                gen("pool", eng)

            @block.sync
            def _(eng):
                gen("sp", eng)


class Prog:
    def __init__(self, do_sample=True, n_own_groups=2, n_pre_groups=6, stop_after=None):
        self.nc = bass.Bass("TRN2", target_bir_lowering=False)
        self.S = Sched(self.nc)
        self.st = ExitStack()
        self.do_sample = do_sample
        self.n_own_groups = n_own_groups
        self.n_pre_groups = n_pre_groups
        self.stop_after = stop_after
        self.use_pools = False
        self.ps_ctr = 0
        self.pinned = set()
        self.ev_ctr = 0
        self.stat_ctr = 0

    def dram_in(self, name, shape, dt=F32):
        return self.nc.dram_tensor(name, list(shape), dt, kind="ExternalInput").ap()

    def dram_out(self, name, shape, dt=F32):
        return self.nc.dram_tensor(name, list(shape), dt, kind="ExternalOutput").ap()

    def sb(self, name, shape, dt):
        return self.st.enter_context(self.nc.sbuf_tensor(name, list(shape), dt))

    def mm(self, out, lhsT, rhs, st, sp, R, W, skip=False):
        if skip:
            self.S.op("pe", lambda e: e.matmul(out, lhsT=lhsT, rhs=rhs, start=st, stop=sp, skip_group_check=True), R, W)
        else:
            self.S.op("pe", lambda e: e.matmul(out, lhsT=lhsT, rhs=rhs, start=st, stop=sp), R, W)

    def act(self, out, in_, func, R, W, scale=1.0, bias=0.0, accum=None):
        if accum is None:
            self.S.op("act", lambda e: e.activation(out=out, in_=in_, func=func, bias=bias, scale=scale), R, W)
        else:
            self.S.op("act", lambda e: e.activation(out=out, in_=in_, func=func, bias=bias, scale=scale,
                                                    accum_out=accum), R, W)

    def tt(self, eng, out, in0, in1, op, R, W):
        self.S.op(eng, lambda e: e.tensor_tensor(out=out, in0=in0, in1=in1, op=op), R, W)

    def ts(self, eng, out, in0, s1, s2, op0, op1, R, W, accum=None):
        if op1 is None:
            self.S.op(eng, lambda e: e.tensor_scalar(out=out, in0=in0, scalar1=s1, scalar2=None, op0=op0), R, W)
        elif accum is None:
            self.S.op(eng, lambda e: e.tensor_scalar(out=out, in0=in0, scalar1=s1, scalar2=s2, op0=op0, op1=op1), R, W)
        else:
            self.S.op(eng, lambda e: e.tensor_scalar(out=out, in0=in0, scalar1=s1, scalar2=s2, op0=op0, op1=op1,
                                                     accum_out=accum), R, W)

    def rsqrt(self, out, in_, scale, b):
        self.ts("dve", out, in_, scale, EPS, ALU.mult, ALU.add, [b], [b])
        self.S.op("act", lambda e: e.activation(out=out, in_=out, func=AF.Sqrt), [b], [b])
        self.S.op("dve", lambda e: e.reciprocal(out=out, in_=out), [b], [b])

    def stt(self, eng, out, in0, scalar, in1, op0, op1, R, W):
        self.S.op(eng, lambda e: e.scalar_tensor_tensor(out=out, in0=in0, scalar=scalar, in1=in1, op0=op0, op1=op1), R, W)

    def cp(self, eng, out, in_, R, W):
        if eng == "act":
            self.act(out, in_, AF.Copy, R, W)
        else:
            self.S.op(eng, lambda e: e.tensor_copy(out=out, in_=in_), R, W)

    def red(self, eng, out, in_, op, R, W):
        self.S.op(eng, lambda e: e.tensor_reduce(out=out, in_=in_, axis=AX.X, op=op), R, W)

    def memset(self, eng, ap, v, W):
        self.S.op(eng, lambda e: e.memset(ap, v), [], W)

    def ev_eng(self):
        self.ev_ctr += 1
        return "act" if self.ev_ctr % 2 else "dve"

    def nextps(self):
        while True:
            b = self.ps_ctr % 8
            self.ps_ctr += 1
            if b not in self.pinned:
                return b

    def build(self):
        nc, S = self.nc, self.S
        P = self
        x_own = P.dram_in("x_own", [1024, D])
        x_pre = P.dram_in("x_pre", [3072, D])
        x_smp = P.dram_in("x_smp", [128, D])
        mem = P.dram_in("mem", [256, D])
        tab_own = P.dram_in("tab_own", [1024, 192])
        tab_pre = P.dram_in("tab_pre", [3072, 192])
        tab_smp = P.dram_in("tab_smp", [128, 192])
        cst = P.dram_in("cst", [128, 64])
        cbf_d = P.dram_in("cbf", [128, 2304])
        gvec = P.dram_in("gvec", [128, 96])
        gfin_d = P.dram_in("gfin", [128, D])
        w_in = P.dram_in("w_in", [D, NIN])
        w_ro = P.dram_in("w_ret_out", [1024, D])
        w_do = P.dram_in("w_dsa_out", [1024, D])
        w_o = P.dram_in("w_o", [D, D])
        w_mq = P.dram_in("w_mq", [D, 512])
        w_mk = P.dram_in("w_mk", [D, 512])
        w_mv = P.dram_in("w_mv", [D, 512])
        w_mo = P.dram_in("w_mo", [512, D])
        w_up = P.dram_in("w_up", [D, 8192])
        w_dn = P.dram_in("w_down", [8192, D])
        st_in = P.dram_in("state_smp", [8, 128, 128])
        cmk = P.dram_in("cmk", [256, 512])
        cmv = P.dram_in("cmv", [256, 512])
        pool_k2 = P.dram_in("pool_k", [1280 * 16, 2048])
        pool_v2 = P.dram_in("pool_v", [1280 * 16, 2048])
        pool_ik2 = P.dram_in("pool_ik", [1280 * 8, 1024])
        ptab = P.dram_in("ptab", [128, 1], I32)

        y_own = P.dram_out("y_own", [1024, D])
        y_smp = P.dram_out("y_smp", [128, D])
        st_own = P.dram_out("st_own", [8, 128, 128])
        st_smp = P.dram_out("st_smp", [8, 128, 128])
        k_own = P.dram_out("k_own", [1024, 256])
        v_own = P.dram_out("v_own", [1024, 256])
        ik_own = P.dram_out("ik_own", [1024, 64])
        k_smp = P.dram_out("k_smp", [128, 256])
        v_smp = P.dram_out("v_smp", [128, 256])
        ik_smp = P.dram_out("ik_smp", [128, 64])
        mk_o = P.dram_out("mk_o", [256, 512])
        mv_o = P.dram_out("mv_o", [256, 512])

        sb = P.sb
        KT = sb("KT", [128, 2, 4096], BF16); bKT = [Buf() for _ in range(33)]
        Vb = sb("Vb", [128, 32, 256], BF16); bVb = [Buf() for _ in range(33)]
        ikT = sb("ikT", [128, 4096], BF16); bikT = [Buf() for _ in range(33)]
        S32 = sb("S32", [128, 8, 128], F32); bS32 = [Buf(), Buf()]
        Sbf = sb("Sbf", [128, 8, 128], BF16); bSbf = [Buf(), Buf()]
        S32s, Sbfs, bS32s, bSbfs = S32, Sbf, bS32, bSbf
        cstf = sb("cstf_sb", [128, 64], F32); bcst = Buf()
        cbf = sb("cbf_sb", [128, 2304], BF16)
        ident = cbf[:, 0:128]
        tri = cbf[:, 128:256]
        intraT = cbf[:, 256:1280].rearrange("p (h i) -> p h i", h=8)
        diagq = cbf[:, 1280:2304].rearrange("p (h i) -> p h i", h=8)
        I4 = sb("I4", [128, 512], BF16)
        ones = sb("ones", [128, 128], BF16)
        gv = sb("gv", [128, 96], F32); bgv = Buf()
        kdec_own = cstf[:, 8:16]
        kdec_smp = cstf[:, 16:24]
        cdec_own = cstf[:, 24:32]
        cdec_smp = cstf[:, 32:40]
        slotb = cstf[:, 40:43]

        NW = 2
        Wr = [sb(f"Wr{i}", [128, 16, 512], BF16) for i in range(NW)]
        bW = [[Buf() for _ in range(4)] for _ in range(NW)]
        U = sb("U", [128, 16, 512], BF16); bU = [Buf() for _ in range(4)]
        H = sb("H", [128, 4, D], F32); bH = [Buf() for _ in range(4)]
        R = sb("R", [128, 32, 512], BF16)
        retyT = R[:, 0:8, :]; bretyT = [Buf() for _ in range(4)]
        dsaoT = R[:, 8:16, :]; bdsaoT = [Buf() for _ in range(4)]
        mrgT = R[:, 16:32, :]; bmrgT = [Buf() for _ in range(4)]
        aT = R; baT = Buf()
        gfin = R[:, 0:8, :].rearrange("p a b -> p (a b)").bitcast(F32)
        mbs = R[:, 16:24, :].rearrange("p a b -> p (a b)"); bmb = Buf()
        diagw = R[:, 24:28, :].rearrange("p a (b c) -> p (a b) c", c=128); bdiagw = Buf()
        rl = [R[:, 28 + i, :] for i in range(3)]; brl = [Buf() for _ in range(3)]
        xn4 = [R[:, 16 + 4 * i:20 + 4 * i, :].rearrange("p a b -> p (a b)") for i in range(4)]; bxn4 = [Buf() for _ in range(4)]
        stat = sb("stat", [128, 28, 8], F32); bstat = [Buf() for _ in range(28)]
        tab = [sb(f"tab{i}", [128, 192], F32) for i in range(4)]; btab = [Buf() for _ in range(4)]
        tabB = [sb(f"tabB{i}", [128, 192], F32) for i in range(4)]; btabB = [Buf() for _ in range(4)]
        U2 = R[:, 0:16, :]; bU2 = [Buf() for _ in range(4)]
        tmpA = [sb(f"tmpA{i}", [128, 512], F32) for i in range(2)]; btmpA = [Buf() for _ in range(2)]
        rtmp = [sb(f"rtmp{i}", [128, 512], F32) for i in range(2)]; brtmp = Buf()
        bfA = [sb(f"bfA{i}", [128, 512], BF16) for i in range(2)]; bbfA = [Buf() for _ in range(2)]
        PTb = [sb(f"PT{i}", [128, 512], BF16) for i in range(3)]; bPT = [Buf() for _ in range(3)]
        AR = sb("AR", [128, 9216], BF16)
        q4 = AR[:, 0:2048].rearrange("p (a b) -> p a b", a=4); bq4 = [Buf() for _ in range(4)]
        k4 = AR[:, 2048:4096].rearrange("p (a b) -> p a b", a=4); bk4 = [Buf() for _ in range(4)]
        v4 = AR[:, 4096:6144].rearrange("p (a b) -> p a b", a=4); bv4 = [Buf() for _ in range(4)]
        kd4 = AR[:, 6144:6656]; bkd4 = Buf()
        tr3 = [AR[:, 6656 + i * 512:6656 + (i + 1) * 512] for i in range(3)]; btr3 = [Buf() for _ in range(3)]
        smT = AR[:, 8192:8704]; bsmT = Buf()
        rpre = AR[:, 8704:9216]; brpre = Buf()
        QT = AR[:, 0:4096].rearrange("p (a h t) -> p a h t", a=4, h=8); bQT = [Buf() for _ in range(4)]
        iqT = AR[:, 4096:8192].rearrange("p (a h t) -> p a h t", a=4, h=8); biqT = [Buf() for _ in range(4)]
        kbf = AR[:, 8192:8448]; bkbf = Buf()
        kbf4 = [AR[:, 8192 + 256 * i:8448 + 256 * i] for i in range(4)]; bkbf4 = [Buf() for _ in range(4)]
        ik2 = sb("ik2", [128, 128], BF16); bik2 = Buf()
        qmT = AR[:, 0:2048].rearrange("p (a b) -> p a b", a=4); bqmT = Buf()
        PTm = AR[:, 2048:3072].rearrange("p (a b) -> p a b", a=2); bPTm = Buf()
        omT = AR[:, 3072:5120].rearrange("p (a b) -> p a b", a=4); bomT = Buf()
        recf2 = AR[:, 5120:6144].bitcast(F32); brec2 = Buf()
        sg4 = H[:, 0, :].bitcast(BF16)[:, 0:2048].rearrange("p (a b) -> p a b", a=4); bsg4 = [bH[0]] * 4
        iwf = sb("iwf", [128, 4, 16], F32); biw = [Buf() for _ in range(4)]
        ikf = sb("ikf", [128, 64], F32); bikf = Buf()
        mkT = sb("mkT", [128, 4, 256], BF16); bmkT = Buf()
        mvb = sb("mvb", [128, 2, 512], BF16); bmvb = Buf()
        mkTs, mvbs, bmkTs, bmvbs = mkT, mvb, bmkT, bmvb
        orow = H[:, 2, 0:512]; borow = bH[2]
        kvf = H[:, 2, 512:1024]; bkvf = bH[2]
        recf = H[:, 2, 1024:1536]; brec = bH[2]
        m1 = H[:, 3, :].rearrange("p (a b) -> p a b", a=4); bm1 = [bH[3]] * 4
        Isc = H[:, 0:2, :].rearrange("p a b -> p (a b)")
        bIsc = [bH[0], bH[1]]
        Isc2 = H[:, 2:4, :].rearrange("p a b -> p (a b)")
        bIsc2 = [bH[2], bH[3]]

        MR = R[:, 16:32, :].rearrange("p a b -> p (a b)")
        sIT = MR[:, 0:1032].bitcast(F32); bsIT = Buf()
        sMask = MR[:, 1040:1556]; bsMask = Buf()
        sG = MR[:, 1560:2076]; bsG = Buf()
        sRl = MR[:, 2080:3104].bitcast(F32); bsRl = Buf()
        sTmp = MR[:, 3104:4128].bitcast(F32); bsTmp = Buf()
        sE = MR[:, 4128:4640].bitcast(F32); bsE = Buf()
        sPT = MR[:, 4640:4896]; bsPT = Buf()
        sKTr = [MR[:, 4896:5408], MR[:, 5408:5920]]; bsKTr = [Buf(), Buf()]
        sikTr = [MR[:, 5920:6432], MR[:, 6432:6944]]; bsikTr = [Buf(), Buf()]
        sKTo = MR[:, 6944:7200]; bsKTo = Buf()
        sVo = MR[:, 7200:7456]; bsVo = Buf()
        sikTo = MR[:, 7456:7584]; bsikTo = Buf()
        sWb = MR[:, 7584:7712].bitcast(F32); bsWb = Buf()
        sSt = MR[:, 7712:7840].bitcast(F32); bsSt = Buf()
        sX = MR[:, 7840:7904]; bsX = Buf()
        sPcb = MR[:, 7904:7912]; bsPcb = Buf()
        sIdxF = MR[:, 7912:7976].bitcast(F32)
        sIdxI = MR[:, 7976:8040].bitcast(I32); bsIdx = Buf()
        sPt = MR[:, 8040:8042].bitcast(I32)
        Kst = [KT[:, i, :].bitcast(F32) for i in range(2)]; bKst = [Buf(), Buf()]
        Vst = [Vb[:, 16 * i:16 * (i + 1), :].rearrange("p a b -> p (a b)").bitcast(F32) for i in range(2)]; bVst = [Buf(), Buf()]
        Ist = [ikT[:, 2048 * i:2048 * (i + 1)].bitcast(F32) for i in range(2)]; bIst = [Buf(), Buf()]
        H1bf = H[:, 1, :].bitcast(BF16)
        sKbf = H1bf[:, 0:2048]; sVbf = H1bf[:, 2048:4096]
        sIbf = H[:, 3, 512:1536].bitcast(BF16)
        triT4 = cstf[:, 44:48]

        PB = [self.st.enter_context(nc.psum_tensor(f"pb{i}", [128, 512], F32)) for i in range(8)]
        bPB = [Buf(excl=True) for _ in range(8)]

        S.dma("sp", cstf[:], cst, writes=[bcst])
        S.dma("sp", gv[:], gvec, writes=[bgv])
        bc = Buf()
        S.dma("pool", cbf[:], cbf_d, writes=[bc])
        for j in range(4):
            P.cp("dve", I4[:, j * 128:(j + 1) * 128], ident, [bc], [bc])
        P.memset("dve", ones[:], 1.0, [bc])
        G_MIX, G_MEM, G_MEMKV, G_MLP, G_GN = 0, 16, 32, 48, 64
        RC = [bc, bcst, bgv]

        def new_stat():
            i = P.stat_ctr % 28
            P.stat_ctr += 1
            return stat[:, i, :], bstat[i]

        def rms_a(items):
            sts = []
            for (xap, bx, ti) in items:
                stt_, bst = new_stat()
                sts.append((stt_, bst))
                P.memset("dve", stt_[:, 0:1], 0.0, [bst])
                P.act(xn4[ti], xap, AF.Square, [bx, bst], [bxn4[ti], bst], accum=stt_[:, 0:1])
            for (xap, bx, ti), (stt_, bst) in zip(items, sts):
                P.ts("dve", stt_[:, 1:2], stt_[:, 0:1], 1.0 / D, EPS, ALU.mult, ALU.add, [bst], [bst])
            for (xap, bx, ti), (stt_, bst) in zip(items, sts):
                P.S.op("act", lambda e, o=stt_[:, 1:2]: e.activation(out=o, in_=o, func=AF.Sqrt), [bst], [bst])
            for (xap, bx, ti), (stt_, bst) in zip(items, sts):
                P.S.op("dve", lambda e, o=stt_[:, 1:2]: e.reciprocal(out=o, in_=o), [bst], [bst])
            for (xap, bx, ti), (stt_, bst) in zip(items, sts):
                P.ts("dve", xn4[ti], xap, stt_[:, 1:2], None, ALU.mult, None, [bx, bst], [bxn4[ti]])

        def rms_b(items, goff, Ud=None, bUd=None):
            Ud = U if Ud is None else Ud
            bUd = bU if bUd is None else bUd
            pend = []
            for (xap, bx, ti) in items:
                for q in range(4):
                    bk = P.nextps()
                    for j in range(4):
                        kc = q * 4 + j
                        P.mm(PB[bk][:, j * 128:(j + 1) * 128], xn4[ti][:, kc * 128:(kc + 1) * 128], ident, True, True,
                             [bxn4[ti]] + RC, [bPB[bk]])
                    pend.append((bk, q, ti))
                    if len(pend) >= 4:
                        evac_T(pend.pop(0), goff, Ud, bUd)
            while pend:
                evac_T(pend.pop(0), goff, Ud, bUd)

        def rmsnorm_T_multi(items, goff):
            rms_a(items)
            rms_b(items, goff)

        def evac_T(item, goff, Ud, bUd):
            bk, q, ti = item
            dst = Ud[:, 4 * q:4 * q + 4, ti * 128:(ti + 1) * 128]
            src = PB[bk][:].rearrange("p (j t) -> p j t", j=4)
            gb_ = gv[:, goff + 4 * q:goff + 4 * q + 4].unsqueeze(2).to_broadcast([128, 4, 128])
            P.tt("dve", dst, src, gb_, ALU.mult, [bPB[bk]] + RC, [bUd[ti]])

        def rmsnorm_T(xap, bx, goff, ti, ncols_tile=128):
            rmsnorm_T_multi([(xap, bx, ti)], goff)

        brt4 = [Buf() for _ in range(4)]

        def rope(eng, src3, dst3, cos2, sin2, Hh, half, dh, Rr, Ww, inplace=False):
            cb_ = cos2.unsqueeze(1).to_broadcast([128, Hh, half])
            sb_ = sin2.unsqueeze(1).to_broadcast([128, Hh, half])
            x1 = src3[:, :, 0:half]
            x2 = src3[:, :, half:2 * half]
            n = Hh * half
            tA = rtmp[0][:, 0:n].rearrange("p (h d) -> p h d", h=Hh)
            tB = rtmp[0][:, 256:256 + n].rearrange("p (h d) -> p h d", h=Hh)
            tC = rtmp[1][:, 0:n].rearrange("p (h d) -> p h d", h=Hh)
            tD = rtmp[1][:, 256:256 + n].rearrange("p (h d) -> p h d", h=Hh)
            Rr = list(Rr)
            P.tt(eng, tA, x1, cb_, ALU.mult, Rr, [brt4[0]])
            P.tt(eng, tB, x2, sb_, ALU.mult, Rr, [brt4[1]])
            P.tt(eng, tC, x1, sb_, ALU.mult, Rr, [brt4[2]])
            P.tt(eng, tD, x2, cb_, ALU.mult, Rr, [brt4[3]])
            P.tt(eng, dst3[:, :, 0:half], tA, tB, ALU.subtract, [brt4[0], brt4[1], brt4[2], brt4[3]] + Rr, Ww)
            P.tt(eng, dst3[:, :, half:2 * half], tC, tD, ALU.add, [brt4[2], brt4[3]] + Rr, Ww)
            if 2 * half < dh and not inplace:
                P.cp(eng, dst3[:, :, 2 * half:dh], src3[:, :, 2 * half:dh], Rr, Ww)

        def load_tab(src_rows):
            i = P.stat_ctr % 4
            P.stat_ctr += 1
            S.dma("sp", tab[i][:], src_rows, writes=[btab[i]])
            return tab[i], btab[i]

        def proj_tiles(slot, bslot, nkc, ncol, tiles, lhs_of, lhs_bufs_of, evac):
            banks = []
            for ti in tiles:
                b = P.nextps()
                P.pinned.add(b)
                banks.append(b)
            nq = (nkc + 3) // 4
            for q in range(nq):
                for i, ti in enumerate(tiles):
                    b = banks[i]
                    for kc in range(q * 4, min(nkc, q * 4 + 4)):
                        P.mm(PB[b][:, 0:ncol], lhs_of(kc, ti), slot[:, kc, 0:ncol], kc == 0, kc == nkc - 1,
                             lhs_bufs_of(ti) + [bslot[kc // 4]], [bPB[b]])
            for b in banks:
                P.pinned.discard(b)
            for i, ti in enumerate(tiles):
                evac(ti, PB[banks[i]], bPB[banks[i]])

        def u_lhs(kc, ti):
            return U[:, kc, ti * 128:(ti + 1) * 128]

        def u_bufs(ti):
            return [bU[ti]]

        def transposes_to(src_bf, bsrc, n, dst_of, bdst_of, scale_of=None, part=128):
            b = P.nextps()
            for j in range(n):
                P.mm(PB[b][:, j * 128:(j + 1) * 128], src_bf[:, j * 128:(j + 1) * 128], ident, True, True,
                     [bsrc] + RC, [bPB[b]])
            eng = P.ev_eng()
            for j in range(n):
                src = PB[b][:, j * 128:(j + 1) * 128]
                if scale_of is not None:
                    P.ts("dve", dst_of(j), src, scale_of(j), None, ALU.mult, None, [bPB[b]] + RC, [bdst_of(j)])
                else:
                    P.cp(eng, dst_of(j), src, [bPB[b]], [bdst_of(j)])

        def evac_dkdv(ps, bps, tb, btb, chunk, outk=None, outv=None, to_keys=True):
            P.cp("act", kvf, ps[:, 0:512], [bps], [bkvf])
            rope("dve", kvf[:, 0:256].rearrange("p (h d) -> p h d", h=2), kvf[:, 0:256].rearrange("p (h d) -> p h d", h=2),
                 tb[:, 128:144], tb[:, 144:160], 2, 16, 128, [bkvf, btb], [bkvf], inplace=True)
            if outk is not None:
                S.dma("sp", outk, kvf[:, 0:256], reads=[bkvf])
                S.dma("sp", outv, kvf[:, 256:512], reads=[bkvf])
            if to_keys:
                kb_ = kbf4[chunk % 4]
                bkb2 = bkbf4[chunk % 4]
                P.cp("dve", kb_, kvf[:, 0:256], [bkvf], [bkb2])
                P.cp("dve", Vb[:, chunk, :], kvf[:, 256:512], [bkvf], [bVb[chunk]])
                defer(lambda: transposes_to(kb_, bkb2, 2, lambda j: KT[:, j, chunk * 128:(chunk + 1) * 128], lambda j: bKT[chunk]), 2)

        def evac_ik(ps, bps, tb, btb, chunk, outik=None, to_keys=True):
            P.cp("act", ikf[:], ps[:, 0:64], [bps], [bikf])
            v3 = ikf[:].rearrange("p (h d) -> p h d", h=1)
            rope("dve", v3, v3, tb[:, 160:168], tb[:, 168:176], 1, 8, 64, [bikf, btb], [bikf], inplace=True)
            if self.stop_after == "projB":
                return
            if outik is not None:
                S.dma("sp", outik, ikf[:], reads=[bikf])
            if self.stop_after == "projC":
                return
            if to_keys:
                i2 = PTb[0][:, (chunk % 4) * 128:(chunk % 4 + 1) * 128]
                P.cp("dve", i2[:, 0:64], ikf[:], [bikf], [bPT[0]])
                P.cp("dve", i2[:, 64:128], ikf[:], [bikf], [bPT[0]])
                defer(lambda: transposes_to(i2, bPT[0], 1, lambda j: ikT[:, chunk * 128:(chunk + 1) * 128], lambda j: bikT[chunk]), 2)

        def evac_rk(ps, bps, tb, btb, dst_bf, bdst):
            i = P.ev_ctr % 2
            P.ev_ctr += 1
            P.act(tmpA[i][:], ps[:, 0:512], AF.Copy, [bps], [btmpA[i]], scale=128.0 ** -0.5)
            rope("dve", tmpA[i][:].rearrange("p (h d) -> p h d", h=4), dst_bf.rearrange("p (h d) -> p h d", h=4),
                 tb[:, 0:64], tb[:, 64:128], 4, 64, 128, [btmpA[i], btb], [bdst])

        def wseg(src2d, nkc, coff, ncol):
            return (src2d, nkc, coff, ncol)

        P.deferred = []

        def defer(fn, delay=1):
            P.deferred.append([delay, fn])

        def run_deferred(flush=False):
            keep = []
            todo = []
            for it in P.deferred:
                it[0] -= 1
                if flush or it[0] <= 0:
                    todo.append(it[1])
                else:
                    keep.append(it)
            P.deferred = keep
            for fn in todo:
                fn()

        wcache = {}

        def run_steps(steps):
            def issue(i):
                slot = i % NW
                for (src, nkc, coff, ncol) in steps[i][0]:
                    key = str(src)
                    first = key not in wcache
                    if first:
                        scr = nc.dram_tensor(f"wc{len(wcache)}", [128, nkc, ncol], BF16).ap()
                        wcache[key] = (scr, [Buf() for _ in range(4)])
                    scr, bscr = wcache[key]
                    for k0 in range(0, nkc, 4):
                        k1 = min(nkc, k0 + 4)
                        q = k0 // 4
                        dst = Wr[slot][:, k0:k1, coff:coff + ncol]
                        if first:
                            S.dma("pool", dst, src[k0 * 128:k1 * 128, :].rearrange("(k p) n -> p k n", p=128), writes=[bW[slot][q]])
                            S.dma("sp", scr[:, k0:k1, :], dst, reads=[bW[slot][q]], writes=[bscr[q]])
                        else:
                            S.dma("sp", dst, scr[:, k0:k1, :], reads=[bscr[q]], writes=[bW[slot][q]])
            if not steps:
                return
            issue(0)
            for i in range(len(steps)):
                if i + 1 < len(steps):
                    issue(i + 1)
                steps[i][1](Wr[i % NW], bW[i % NW])
                run_deferred()

        def phase_mem():
            steps = []

            def mk_step(slot, bslot):
                for mt in range(2):
                    S.dma("sp", H[:, mt, :], mem[mt * 128:(mt + 1) * 128, :], writes=[bH[mt]])
                rmsnorm_T_multi([(H[:, mt, :], bH[mt], mt) for mt in range(2)], G_MEMKV)
                def evac(ti, ps, bps):
                    i = ti % 2
                    P.cp("act", tmpA[i][:], ps[:, 0:512], [bps], [btmpA[i]])
                    S.dma("sp", mk_o[ti * 128:(ti + 1) * 128, :], tmpA[i][:], reads=[btmpA[i]])
                    P.cp("dve", bfA[i][:], tmpA[i][:], [btmpA[i]], [bbfA[i]])
                    transposes_to(bfA[i], bbfA[i], 4, lambda j: mkT[:, j, ti * 128:(ti + 1) * 128], lambda j: bmkT)
                proj_tiles(slot, bslot, 16, 512, [0, 1], u_lhs, u_bufs, evac)

            def mv_step(slot, bslot):
                def evac(ti, ps, bps):
                    i = ti % 2
                    P.cp("act", tmpA[i][:], ps[:, 0:512], [bps], [btmpA[i]])
                    S.dma("sp", mv_o[ti * 128:(ti + 1) * 128, :], tmpA[i][:], reads=[btmpA[i]])
                    P.cp("dve", mvb[:, ti, :], tmpA[i][:], [btmpA[i]], [bmvb])
                proj_tiles(slot, bslot, 16, 512, [0, 1], u_lhs, u_bufs, evac)
            steps.append(([wseg(w_mk, 16, 0, 512)], mk_step))
            steps.append(([wseg(w_mv, 16, 0, 512)], mv_step))
            return steps

        def load_sample_mem():
            for mt in range(2):
                i = mt
                S.dma("sp", tmpA[i][:], cmk[mt * 128:(mt + 1) * 128, :], writes=[btmpA[i]])
                P.cp("dve", bfA[i][:], tmpA[i][:], [btmpA[i]], [bbfA[i]])
                transposes_to(bfA[i], bbfA[i], 4, lambda j, mt=mt: mkTs[:, j, mt * 128:(mt + 1) * 128], lambda j: bmkTs)
            S.dma("pool", mvbs[:], cmv.rearrange("(c p) n -> p c n", p=128), writes=[bmvbs])

        pre_state_first = [True, True]

        pre_loads = {}

        def prefix_group(g, last):
            tiles = list(range(4))

            Ug, bUg = (U, bU) if g % 2 == 0 else (U2, bU2)
            tabs, btabs = (tab, btab) if g % 2 == 0 else (tabB, btabB)
            items = [(H[:, ti, :], bH[ti], ti) for ti in tiles]

            def loads_a():
                if g == 0:
                    P.pinned.add(6)
                    P.pinned.add(7)
                    P.memset("dve", PB[6][:], 0.0, [bPB[6]])
                    P.memset("dve", PB[7][:], 0.0, [bPB[7]])
                for ti in tiles:
                    row0 = (g * 4 + ti) * 128
                    S.dma("sp", H[:, ti, :], x_pre[row0:row0 + 128, :], writes=[bH[ti]])
                    S.dma("sp", tabs[ti][:], tab_pre[row0:row0 + 128, :], writes=[btabs[ti]])
                rms_a(items)

            def loads_b():
                rms_b(items, G_MIX, Ug, bUg)
            pre_loads[g] = (loads_a, loads_b)

            def ug_lhs(kc, ti):
                return Ug[:, kc, ti * 128:(ti + 1) * 128]

            def ug_bufs(ti):
                return [bUg[ti]]
            steps = []
            kbuf = [k4, AR[:, 0:2048].rearrange("p (a b) -> p a b", a=4)]
            vbuf = [v4, AR[:, 6144:8192].rearrange("p (a b) -> p a b", a=4)]
            bkb_ = [bk4, bq4]
            bvb_ = [bv4, [bkd4, btr3[0], btr3[1], btr3[2]]]

            def state_mm(hg):
                bank = 6 + hg
                for ti in tiles:
                    for h in range(4):
                        lastf = last and ti == 3
                        P.mm(PB[bank][:, h * 128:(h + 1) * 128], kbuf[hg][:, ti, h * 128:(h + 1) * 128],
                             vbuf[hg][:, ti, h * 128:(h + 1) * 128], False, lastf, [bkb_[hg][ti], bvb_[hg][ti]], [bPB[bank]], skip=True)

            for hg in range(2):
                def rk_step(slot, bslot, hg=hg):
                    if hg == 0 and g == 0:
                        loads_a()
                        loads_b()

                    def evac(ti, ps, bps):
                        evac_rk(ps, bps, tabs[ti], btabs[ti], kbuf[hg][:, ti, :], bkb_[hg][ti])
                        kd3 = kbuf[hg][:, ti, :].rearrange("p (h d) -> p h d", h=4)
                        dec = tabs[ti][:, 176 + hg * 4:176 + hg * 4 + 4].unsqueeze(2).to_broadcast([128, 4, 128])
                        P.tt("dve", kd3, kd3, dec, ALU.mult, [bkb_[hg][ti], btabs[ti]], [bkb_[hg][ti]])
                    proj_tiles(slot, bslot, 16, 512, tiles, ug_lhs, ug_bufs, evac)
                    if hg == 1:
                        state_mm(0)

                def rv_step(slot, bslot, hg=hg):
                    def evac(ti, ps, bps):
                        P.cp(P.ev_eng(), vbuf[hg][:, ti, :], ps[:, 0:512], [bps], [bvb_[hg][ti]])
                    proj_tiles(slot, bslot, 16, 512, tiles, ug_lhs, ug_bufs, evac)
                    if (g + 1) in pre_loads:
                        pre_loads[g + 1][hg]()
                steps.append(([wseg(w_in[:, C_RK + hg * 512:C_RK + hg * 512 + 512], 16, 0, 512)], rk_step))
                steps.append(([wseg(w_in[:, C_RV + hg * 512:C_RV + hg * 512 + 512], 16, 0, 512)], rv_step))

            def kv_step(slot, bslot):
                def evac(ti, ps, bps):
                    evac_dkdv(ps, bps, tabs[ti], btabs[ti], g * 4 + ti)
                proj_tiles(slot, bslot, 16, 512, tiles, ug_lhs, ug_bufs, evac)
                state_mm(1)

            def ik_step(slot, bslot):
                def evac(ti, ps, bps):
                    evac_ik(ps, bps, tabs[ti], btabs[ti], g * 4 + ti)
                proj_tiles(slot, bslot, 16, 64, tiles, ug_lhs, ug_bufs, evac)
                if last:
                    prefix_finish()
                    P.pinned.discard(6)
                    P.pinned.discard(7)
            steps.append(([wseg(w_in[:, C_DK:C_DK + 512], 16, 0, 512)], kv_step))
            steps.append(([wseg(w_in[:, C_IK:C_IK + 64], 16, 0, 64)], ik_step))
            return steps

        def prefix_finish():
            for hg in range(2):
                P.cp("dve", S32[:, hg * 4:(hg + 1) * 4, :].rearrange("p h e -> p (h e)"), PB[6 + hg][:], [bPB[6 + hg]], [bS32[hg]])
                P.cp("act", Sbf[:, hg * 4:(hg + 1) * 4, :].rearrange("p h e -> p (h e)"), PB[6 + hg][:], [bPB[6 + hg]], [bSbf[hg]])

        def retention(ti, hg, smp):
            S32_, Sbf_, bS32_, bSbf_ = (S32s, Sbfs, bS32s, bSbfs) if smp else (S32, Sbf, bS32, bSbf)
            kdec = kdec_smp if smp else kdec_own
            cdec = cdec_smp if smp else cdec_own
            qsrc = q4[:, ti, :]
            ksrc = k4[:, ti, :]
            vsrc = v4[:, ti, :]
            P.tt("dve", kd4.rearrange("p (h d) -> p h d", h=4), ksrc.rearrange("p (h d) -> p h d", h=4),
                 kdec[:, hg * 4:hg * 4 + 4].unsqueeze(2).to_broadcast([128, 4, 128]), ALU.mult, [bk4[ti]] + RC, [bkd4])
            for which, (src, bsrc, rhs_of) in enumerate((
                    (qsrc, bq4[ti], lambda h: ident),
                    (ksrc, bk4[ti], lambda h: ident),
                    (qsrc, bq4[ti], lambda h: diagq[:, hg * 4 + h, :]))):
                b = P.nextps()
                for h in range(4):
                    P.mm(PB[b][:, h * 128:(h + 1) * 128], src[:, h * 128:(h + 1) * 128], rhs_of(h), True, True,
                         [bsrc] + RC, [bPB[b]])
                P.cp(P.ev_eng(), tr3[which], PB[b][:], [bPB[b]], [btr3[which]])
            qT, kT, qdT = tr3
            b = P.nextps()
            for h in range(4):
                P.mm(PB[b][:, h * 128:(h + 1) * 128], kT[:, h * 128:(h + 1) * 128], qT[:, h * 128:(h + 1) * 128], True, True,
                     [btr3[0], btr3[1]], [bPB[b]])
            P.tt("dve", smT, PB[b][:], intraT[:, hg * 4:(hg + 1) * 4, :].rearrange("p h i -> p (h i)"), ALU.mult,
                 [bPB[b]] + RC, [bsmT])
            bo = P.nextps()
            for h in range(4):
                P.mm(PB[bo][:, h * 128:(h + 1) * 128], smT[:, h * 128:(h + 1) * 128], vsrc[:, h * 128:(h + 1) * 128], True, False,
                     [bsmT, bv4[ti]], [bPB[bo]])
                P.mm(PB[bo][:, h * 128:(h + 1) * 128], qdT[:, h * 128:(h + 1) * 128], Sbf_[:, hg * 4 + h, :], False, True,
                     [btr3[2], bSbf_[hg]], [bPB[bo]])
            P.cp("act", orow, PB[bo][:], [bPB[bo]], [borow])
            bs = P.nextps()
            for h in range(4):
                P.mm(PB[bs][:, h * 128:(h + 1) * 128], kd4[:, h * 128:(h + 1) * 128], vsrc[:, h * 128:(h + 1) * 128], True, True,
                     [bkd4, bv4[ti]], [bPB[bs]])
            Sv = S32_[:, hg * 4:(hg + 1) * 4, :]
            P.tt("dve", Sv, Sv, cdec[:, hg * 4:hg * 4 + 4].unsqueeze(2).to_broadcast([128, 4, 128]), ALU.mult, [bS32_[hg]] + RC, [bS32_[hg]])
            P.tt("dve", Sv, Sv, PB[bs][:].rearrange("p (h e) -> p h e", h=4), ALU.add, [bPB[bs], bS32_[hg]], [bS32_[hg]])
            P.cp("act", Sbf_[:, hg * 4:(hg + 1) * 4, :], S32_[:, hg * 4:(hg + 1) * 4, :], [bS32_[hg]], [bSbf_[hg]])
            stt_, bst = new_stat()
            o3 = orow.rearrange("p (h e) -> p h e", h=4)
            P.red("dve", stt_[:, 0:4], o3, ALU.add, [borow], [bst])
            sq = tmpA[1]
            P.tt("dve", sq[:], orow, orow, ALU.mult, [borow], [btmpA[1]])
            P.red("dve", stt_[:, 4:8], sq[:].rearrange("p (h e) -> p h e", h=4), ALU.add, [btmpA[1]], [bst])
            stt2, bst2 = new_stat()
            P.ts("dve", stt2[:, 0:4], stt_[:, 0:4], 1.0 / 128, None, ALU.mult, None, [bst], [bst2])
            P.tt("dve", stt2[:, 4:8], stt2[:, 0:4], stt2[:, 0:4], ALU.mult, [bst2], [bst2])
            P.stt("dve", stt2[:, 4:8], stt_[:, 4:8], 1.0 / 128, stt2[:, 4:8], ALU.mult, ALU.subtract, [bst, bst2], [bst2])
            P.rsqrt(stt2[:, 4:8], stt2[:, 4:8], 1.0, bst2)
            P.tt("dve", o3, o3, stt2[:, 0:4].unsqueeze(2).to_broadcast([128, 4, 128]), ALU.subtract, [borow, bst2], [borow])
            P.tt("dve", o3, o3, stt2[:, 4:8].unsqueeze(2).to_broadcast([128, 4, 128]), ALU.mult, [borow, bst2], [borow])
            P.tt("dve", rpre, orow, sg4[:, ti, :], ALU.mult, [borow, bsg4[ti]], [brpre])
            transposes_to(rpre, brpre, 4, lambda j: retyT[:, hg * 4 + j, ti * 128:(ti + 1) * 128], lambda j: bretyT[ti],
                          scale_of=lambda j: gv[:, G_GN + hg * 4 + j:G_GN + hg * 4 + j + 1])

        def dsa_idx(ti, it, par):
            Isc_, bIsc_ = (Isc, bIsc) if par == 0 else (Isc2, bIsc2)
            nkc = 24 + it + 1
            Sk = nkc * 128
            nblk = (Sk + 511) // 512
            keyb = [bKT[c] for c in range(nkc)]
            for h in range(16):
                P.act(diagw[:, h, :], ident, AF.Copy, [biw[ti]] + RC, [bdiagw], scale=iwf[:, ti, h:h + 1])
            stt_, bst = new_stat()
            stt2, bst2 = new_stat()
            bIs = [P.nextps(), None]
            P.pinned.add(bIs[0])
            bIs[1] = P.nextps()
            P.pinned.add(bIs[1])
            items = [(kb, h) for kb in range(nblk) for h in range(16)]

            def stage_a(kb, h, i):
                c0 = kb * 512
                n = min(512, Sk - c0)
                b = P.nextps()
                hp = (h % 2) * 64
                P.mm(PB[b][:, 0:n], iqT[hp:hp + 64, ti, h // 2, :], ikT[hp:hp + 64, c0:c0 + n], True, True,
                     [biqT[ti]] + [bikT[c] for c in range(c0 // 128, (c0 + n) // 128)], [bPB[b]])
                r = i % 3
                P.act(rl[r][:, 0:n], PB[b][:, 0:n], AF.Relu, [bPB[b]], [brl[r]])

            def stage_b(kb, h, i):
                c0 = kb * 512
                n = min(512, Sk - c0)
                bI = bIs[kb % 2]
                r = i % 3
                P.mm(PB[bI][:, 0:n], diagw[:, h, :], rl[r][:, 0:n], h == 0, h == 15, [bdiagw, brl[r]], [bPB[bI]])
                if h == 15:
                    P.cp("act", Isc_[:, c0:c0 + n], PB[bI][:, 0:n], [bPB[bI]], bIsc_)
                    P.red("dve", stt_[:, kb:kb + 1], Isc_[:, c0:c0 + n], ALU.max, bIsc_, [bst])
                    P.red("dve", stt2[:, kb:kb + 1], Isc_[:, c0:c0 + n], ALU.min, bIsc_, [bst2])
                    if c0 < 3072:
                        P.ts("dve", Isc_[:, c0:c0 + n], Isc_[:, c0:c0 + n], slotb[:, c0 // 1024:c0 // 1024 + 1], None, ALU.add, None, bIsc_ + RC, bIsc_)
                    else:
                        d0 = 3072 + it * 128
                        if d0 >= c0 and d0 < c0 + n:
                            P.tt("dve", Isc_[:, d0:d0 + 128], Isc_[:, d0:d0 + 128], tri, ALU.add, bIsc_ + RC, bIsc_)
            LAG = 2
            for i, (kb, h) in enumerate(items):
                stage_a(kb, h, i)
                if i >= LAG:
                    stage_b(items[i - LAG][0], items[i - LAG][1], i - LAG)
            for i in range(max(0, len(items) - LAG), len(items)):
                stage_b(items[i][0], items[i][1], i)
            P.pinned.discard(bIs[0])
            P.pinned.discard(bIs[1])
            return dict(ti=ti, it=it, nkc=nkc, Sk=Sk, nblk=nblk, Isc=Isc_, bIsc=bIsc_, stt=stt_, bst=bst, stt2=stt2, bst2=bst2)

        def dsa_bis(c):
            ti, it, nkc, Sk, nblk = c["ti"], c["it"], c["nkc"], c["Sk"], c["nblk"]
            Isc_, bIsc_, stt_, bst, stt2, bst2 = c["Isc"], c["bIsc"], c["stt"], c["bst"], c["stt2"], c["bst2"]
            st3, bst3 = new_stat()
            P.red("dve", st3[:, 0:1], stt_[:, 0:nblk], ALU.max, [bst], [bst3])
            P.red("dve", st3[:, 1:2], stt2[:, 0:nblk], ALU.min, [bst2], [bst3])
            P.stt("dve", st3[:, 2:3], st3[:, 0:1], 1.0, st3[:, 1:2], ALU.add, ALU.subtract, [bst3], [bst3])
            P.ts("dve", st3[:, 2:3], st3[:, 2:3], 0.5, None, ALU.mult, None, [bst3], [bst3])
            cnts, bcn = new_stat()
            cnts2, bcn2 = new_stat()
            cnts3, bcn3 = new_stat()
            P.memset("dve", cnts[:], 0.0, [bcn])
            P.memset("dve", cnts2[:], 0.0, [bcn2])
            P.memset("dve", cnts3[:], 0.0, [bcn3])
            for itn in range(NIT):
                cc, bcc = (cnts, bcn) if itn < 8 else ((cnts2, bcn2) if itn < 16 else (cnts3, bcn3))
                col = itn % 8
                P.tt("dve", st3[:, 3:4], st3[:, 1:2], st3[:, 2:3], ALU.add, [bst3], [bst3])
                P.ts("dve", mbs[:, 0:Sk], Isc_[:, 0:Sk], st3[:, 3:4], 0.0, ALU.is_ge, ALU.add, bIsc_ + [bst3, bcc], [bmb, bcc],
                     accum=cc[:, col:col + 1])
                P.ts("dve", st3[:, 4:5], cc[:, col:col + 1], float(TOPK), st3[:, 2:3], ALU.is_ge, ALU.mult, [bcc, bst3], [bst3])
                P.tt("dve", st3[:, 1:2], st3[:, 1:2], st3[:, 4:5], ALU.add, [bst3], [bst3])
                P.ts("dve", st3[:, 2:3], st3[:, 2:3], 0.5, None, ALU.mult, None, [bst3], [bst3])
            P.ts("dve", mbs[:, 0:Sk], Isc_[:, 0:Sk], st3[:, 1:2], NEG, ALU.is_lt, ALU.mult, bIsc_ + [bst3], [bmb])

        def dsa_att(c):
            ti, it, nkc, Sk = c["ti"], c["it"], c["nkc"], c["Sk"]
            for g in range(2):
                bo = P.nextps()
                P.pinned.add(bo)
                bd = P.nextps()
                P.pinned.add(bd)

                def att_a(kc):
                    b = P.nextps()
                    P.mm(PB[b][:], KT[:, g, kc * 128:(kc + 1) * 128], QT[:, ti, g * 4:(g + 1) * 4, :].rearrange("p h t -> p (h t)"),
                         True, False, [bKT[kc], bQT[ti]], [bPB[b]])
                    P.mm(PB[b][:], mbs[:, kc * 128:(kc + 1) * 128], I4[:], False, True, [bmb] + RC, [bPB[b]])
                    r = kc % 3
                    P.act(PTb[r][:], PB[b][:], AF.Exp, [bPB[b]], [bPT[r]], scale=128.0 ** -0.5)

                def att_b(kc):
                    r = kc % 3
                    P.mm(PB[bo][:], Vb[:, kc, g * 128:(g + 1) * 128], PTb[r][:], kc == 0, kc == nkc - 1, [bVb[kc], bPT[r]], [bPB[bo]])
                    P.mm(PB[bd][:], ones[:], PTb[r][:], kc == 0, kc == nkc - 1, [bPT[r]] + RC, [bPB[bd]])
                LAG2 = 2
                for kc in range(nkc):
                    att_a(kc)
                    if kc >= LAG2:
                        att_b(kc - LAG2)
                for kc in range(max(0, nkc - LAG2), nkc):
                    att_b(kc)
                P.S.op("dve", lambda e, bd=bd: e.reciprocal(out=tmpA[0][:], in_=PB[bd][:]), [bPB[bd]], [btmpA[0]])
                for hh in range(4):
                    P.tt("dve", dsaoT[:, g * 4 + hh, ti * 128:(ti + 1) * 128], PB[bo][:, hh * 128:(hh + 1) * 128],
                         tmpA[0][:, hh * 128:(hh + 1) * 128], ALU.mult, [bPB[bo], btmpA[0]], [bdsaoT[ti]])
                P.pinned.discard(bo)
                P.pinned.discard(bd)

        def smp_index_prep():
            S.dma("sp", sPt, ptab, writes=[bsIdx])
            P.cp("dve", sIdxF[:, 31:32], sPt, [bsIdx], [bsIdx])
            for rb in range(16):
                P.ts("dve", sIdxF[:, rb:rb + 1], sIdxF[:, 31:32], 16.0, float(rb), ALU.mult, ALU.add, [bsIdx], [bsIdx])
            for rb in range(8):
                P.ts("dve", sIdxF[:, 16 + rb:17 + rb], sIdxF[:, 31:32], 8.0, float(rb), ALU.mult, ALU.add, [bsIdx], [bsIdx])
            P.cp("dve", sIdxI[:, 0:24], sIdxF[:, 0:24], [bsIdx], [bsIdx])

        def gather(dst, bdst, src2d, col, extra_w=()):
            S.op("pool", lambda e: e.indirect_dma_start(out=dst, out_offset=None, in_=src2d,
                                                        in_offset=bass.IndirectOffsetOnAxis(ap=sIdxI[:, col:col + 1], axis=0)),
                 [bsIdx], [bdst] + list(extra_w), dma=True)

        def dsa_sample(ti):
            SC = 128.0 ** -0.5
            P.memset("dve", dsaoT[:, :, ti * 128:(ti + 1) * 128], 0.0, [bdsaoT[ti]])
            X4 = sX[0:4, 0:64].rearrange("p (q t j) -> p t q j", t=4, q=2)
            iw4 = iwf[0:4, ti, :].rearrange("p (j q) -> p q j", q=2).unsqueeze(1).to_broadcast([4, 4, 2, 8])
            id4 = ident[0:4, 0:4].unsqueeze(2).unsqueeze(3).to_broadcast([4, 4, 2, 8])
            P.tt("dve", X4, iw4, id4, ALU.mult, [biw[ti]] + RC, [bsX])
            b = P.nextps()
            P.mm(PB[b][:, 0:64], ones[0:4, 0:128], sX[0:4, 0:64], True, True, [bsX] + RC, [bPB[b]])
            P.cp("dve", sWb, PB[b][:, 0:64], [bPB[b]], [bsWb])

            def score_batch(bDs, nr, dst_cols):
                n = nr * 32
                for par in range(2):
                    o0 = par * 256
                    P.act(sRl[:, o0:o0 + n], PB[bDs[par]][:, 0:n], AF.Relu, [bPB[bDs[par]]], [bsRl])
                    P.tt("dve", sTmp[:, o0:o0 + n].rearrange("p (r c) -> p r c", c=32), sRl[:, o0:o0 + n].rearrange("p (r c) -> p r c", c=32),
                         sWb[:, par * 32:(par + 1) * 32].unsqueeze(1).to_broadcast([128, nr, 32]), ALU.mult, [bsRl, bsWb], [bsTmp])
                    P.red("dve", sRl[:, o0:o0 + nr * 4], sTmp[:, o0:o0 + n].rearrange("p (a j) -> p a j", j=8), ALU.add, [bsTmp, bsRl], [bsRl])
                P.tt("dve", sIT[:, dst_cols], sRl[:, 0:nr * 4], sRl[:, 256:256 + nr * 4], ALU.add, [bsRl], [bsIT])

            def dots_mm(bDs, col0, ikT_ap, bik):
                for par in range(2):
                    outv = PB[bDs[par]][:, col0:col0 + 32]
                    rhs = iqT[par * 64:(par + 1) * 64, ti, :, 0:4].rearrange("p j t -> p t j")
                    P.mm(outv, ikT_ap[par * 64:(par + 1) * 64, :], rhs, True, True, [biqT[ti], bik], [bPB[bDs[par]]])

            gather(Ist[0], bIst[0], pool_ik2, 16, extra_w=bikT)
            for ib in range(8):
                if ib + 1 < 8:
                    gather(Ist[(ib + 1) % 2], bIst[(ib + 1) % 2], pool_ik2, 16 + ib + 1, extra_w=bikT if ib == 0 else ())
                src3 = Ist[ib % 2].rearrange("p (r d) -> p r d", d=64)
                dst3 = sIbf.rearrange("p (r d) -> p r d", d=128)
                P.cp("dve", dst3[:, :, 0:64], src3, [bIst[ib % 2]], [bH[3]])
                P.cp("act", dst3[:, :, 64:128], src3, [bIst[ib % 2]], [bH[3]])
                for half in range(2):
                    bDs = [P.nextps(), None]
                    P.pinned.add(bDs[0])
                    bDs[1] = P.nextps()
                    P.pinned.add(bDs[1])
                    for q in range(2):
                        qq = half * 2 + q
                        bt = P.nextps()
                        for rr in range(4):
                            r = qq * 4 + rr
                            P.mm(PB[bt][:, rr * 128:(rr + 1) * 128], dst3[:, r, :], ident, True, True, [bH[3]] + RC, [bPB[bt]])
                        P.cp(P.ev_eng(), sikTr[qq % 2], PB[bt][:], [bPB[bt]], [bsikTr[qq % 2]])
                        for rr in range(4):
                            dots_mm(bDs, (q * 4 + rr) * 32, sikTr[qq % 2][:, rr * 128:(rr + 1) * 128], bsikTr[qq % 2])
                    c0 = (ib * 16 + half * 8) * 4
                    score_batch(bDs, 8, slice(c0, c0 + 32))
                    P.pinned.discard(bDs[0])
                    P.pinned.discard(bDs[1])
            bDs = [P.nextps(), None]
            P.pinned.add(bDs[0])
            bDs[1] = P.nextps()
            P.pinned.discard(bDs[0])
            dots_mm(bDs, 0, sikTo, bsikTo)
            score_batch(bDs, 1, slice(512, 516))
            P.tt("dve", sIT[:, 512:516], sIT[:, 512:516], triT4, ALU.add, [bsIT] + RC, [bsIT])
            IT3 = sIT[:, 0:516].rearrange("p (c t) -> p t c", t=4)
            P.red("dve", sSt[:, 0:4], IT3, ALU.max, [bsIT], [bsSt])
            P.red("dve", sSt[:, 4:8], sIT[:, 0:512].rearrange("p (c t) -> p t c", t=4), ALU.min, [bsIT], [bsSt])
            P.cp("dve", sPcb[:, 0:8], sSt[:, 0:8], [bsSt], [bsPcb])
            b = P.nextps()
            P.mm(PB[b][0:4, 0:128], sPcb[:, 0:4], ident, True, True, [bsPcb] + RC, [bPB[b]])
            P.mm(PB[b][0:4, 128:256], sPcb[:, 4:8], ident, True, True, [bsPcb] + RC, [bPB[b]])
            g4 = sSt[0:4, 8:16]
            P.red("dve", g4[:, 0:1], PB[b][0:4, 0:128], ALU.max, [bPB[b]], [bsSt])
            P.red("dve", g4[:, 1:2], PB[b][0:4, 128:256], ALU.min, [bPB[b]], [bsSt])
            P.ts("dve", g4[:, 2:3], g4[:, 0:1], 1.02, None, ALU.mult, None, [bsSt], [bsSt])
            P.ts("dve", g4[:, 4:5], g4[:, 0:1], 0.98, None, ALU.mult, None, [bsSt], [bsSt])
            P.tt("dve", g4[:, 3:4], g4[:, 2:3], g4[:, 4:5], ALU.max, [bsSt], [bsSt])
            P.ts("dve", g4[:, 3:4], g4[:, 3:4], 1.0, None, ALU.add, None, [bsSt], [bsSt])
            P.ts("dve", g4[:, 2:3], g4[:, 1:2], 1.02, None, ALU.mult, None, [bsSt], [bsSt])
            P.ts("dve", g4[:, 4:5], g4[:, 1:2], 0.98, None, ALU.mult, None, [bsSt], [bsSt])
            P.tt("dve", g4[:, 5:6], g4[:, 2:3], g4[:, 4:5], ALU.min, [bsSt], [bsSt])
            P.ts("dve", g4[:, 5:6], g4[:, 5:6], -1.0, None, ALU.add, None, [bsSt], [bsSt])
            D8 = sX[0:4, 0:8]
            P.ts("dve", D8[:, 0:4], ident[0:4, 0:4], g4[:, 3:4], None, ALU.mult, None, [bsSt] + RC, [bsX])
            P.ts("dve", D8[:, 4:8], ident[0:4, 0:4], g4[:, 5:6], None, ALU.mult, None, [bsSt] + RC, [bsX])
            b = P.nextps()
            P.mm(PB[b][:, 0:8], ones[0:4, 0:128], D8, True, True, [bsX] + RC, [bPB[b]])
            hi_b = sSt[:, 16:20]; thr = sSt[:, 20:24]; step = sSt[:, 24:28]; cand = sSt[:, 28:32]; mt = sSt[:, 32:36]
            P.cp("dve", sSt[:, 16:24], PB[b][:, 0:8], [bPB[b]], [bsSt])
            P.tt("dve", step, hi_b, thr, ALU.subtract, [bsSt], [bsSt])
            P.ts("dve", step, step, 0.5, None, ALU.mult, None, [bsSt], [bsSt])
            IT3n = sIT[:, 0:516].rearrange("p (c t) -> p c t", t=4)
            G3n = sG[:, 0:516].rearrange("p (c t) -> p c t", t=4)
            G3 = sG[:, 0:516].rearrange("p (c t) -> p t c", t=4)
            for itn in range(20):
                P.tt("dve", cand, thr, step, ALU.add, [bsSt], [bsSt])
                P.tt("dve", G3n, IT3n, cand.unsqueeze(1).to_broadcast([128, 129, 4]), ALU.is_ge, [bsIT, bsSt], [bsG])
                P.red("dve", sSt[:, 36:40], G3, ALU.add, [bsG], [bsSt])
                P.cp("dve", sPcb[:, 0:4], sSt[:, 36:40], [bsSt], [bsPcb])
                b = P.nextps()
                P.mm(PB[b][:, 0:4], ones[:], sPcb[:, 0:4], True, True, [bsPcb] + RC, [bPB[b]])
                P.ts("dve", mt, PB[b][:, 0:4], float(TOPK), None, ALU.is_ge, None, [bPB[b]], [bsSt])
                P.tt("dve", mt, mt, step, ALU.mult, [bsSt], [bsSt])
                P.tt("dve", thr, thr, mt, ALU.add, [bsSt], [bsSt])
                P.ts("dve", step, step, 0.5, None, ALU.mult, None, [bsSt], [bsSt])
            M3n = sMask[:, 0:516].rearrange("p (c t) -> p c t", t=4)
            P.tt("dve", M3n, IT3n, thr.unsqueeze(1).to_broadcast([128, 129, 4]), ALU.is_ge, [bsIT, bsSt], [bsMask])
            bO = [P.nextps(), None]
            P.pinned.add(bO[0])
            bO[1] = P.nextps()
            P.pinned.add(bO[1])
            bDn = P.nextps()
            P.pinned.add(bDn)
            Kb3 = sKbf.rearrange("p (r c) -> p r c", c=256)
            Vb3 = sVbf.rearrange("p (r c) -> p r c", c=256)
            PT3 = sPT.rearrange("p (r c) -> p r c", c=32)

            def qrhs(g):
                return QT[:, ti, g * 4:(g + 1) * 4, 0:4]

            def softmax_pv(bL, nr, mcol0, vsrc_of, bv, first, last):
                n = nr * 32
                P.act(sE[:, 0:n], PB[bL][:, 0:n], AF.Exp, [bPB[bL]], [bsE], scale=SC)
                P.tt("dve", sPT[:, 0:n].rearrange("p (r a t) -> p r a t", a=8, t=4),
                     sE[:, 0:n].rearrange("p (r a t) -> p r a t", a=8, t=4),
                     sMask[:, mcol0:mcol0 + nr * 4].rearrange("p (r t) -> p r t", t=4).unsqueeze(2).to_broadcast([128, nr, 8, 4]),
                     ALU.mult, [bsE, bsMask], [bsPT])
                for r8 in range(nr):
                    st_ = first and r8 == 0
                    sp_ = last and r8 == nr - 1
                    for g in range(2):
                        P.mm(PB[bO[g]][:, 0:16], vsrc_of(r8, g), PT3[:, r8, g * 16:(g + 1) * 16], st_, sp_, [bv, bsPT], [bPB[bO[g]]])
                    P.mm(PB[bDn][:, 0:32], ones[:], PT3[:, r8, :], st_, sp_, [bsPT] + RC, [bPB[bDn]])

            gather(Kst[0], bKst[0], pool_k2, 0, extra_w=bKT)
            gather(Vst[0], bVst[0], pool_v2, 0, extra_w=bVb)
            H2bf = H[:, 2, :].bitcast(BF16)
            Kbf2 = [sKbf, H2bf[:, 0:2048]]
            Vbf2 = [sVbf, H2bf[:, 2048:4096]]
            bKV2 = [bH[1], bH[2]]
            KTr4 = [MR[:, 4896 + 512 * i:5408 + 512 * i] for i in range(4)]
            bKTr4 = [bsKTr[0], bsKTr[1], bsikTr[0], bsikTr[1]]
            for kb in range(16):
                if kb + 1 < 16:
                    gather(Kst[(kb + 1) % 2], bKst[(kb + 1) % 2], pool_k2, kb + 1, extra_w=bKT if kb == 0 else ())
                    gather(Vst[(kb + 1) % 2], bVst[(kb + 1) % 2], pool_v2, kb + 1, extra_w=bVb if kb == 0 else ())
                pz = kb % 2
                P.cp("dve", Kbf2[pz], Kst[kb % 2], [bKst[kb % 2]], [bKV2[pz]])
                P.cp("act", Vbf2[pz], Vst[kb % 2], [bVst[kb % 2]], [bKV2[pz]])
                Kb3 = Kbf2[pz].rearrange("p (r c) -> p r c", c=256)
                Vb3 = Vbf2[pz].rearrange("p (r c) -> p r c", c=256)
                bL = P.nextps()
                P.pinned.add(bL)
                bts = []
                for q in range(4):
                    bt = P.nextps()
                    P.pinned.add(bt)
                    bts.append(bt)
                    for rr in range(2):
                        for g in range(2):
                            P.mm(PB[bt][:, (rr * 2 + g) * 128:(rr * 2 + g + 1) * 128], Kb3[:, q * 2 + rr, g * 128:(g + 1) * 128], ident,
                                 True, True, [bKV2[pz]] + RC, [bPB[bt]])
                for q in range(4):
                    P.cp("act" if q % 2 == 0 else "dve", KTr4[q], PB[bts[q]][:], [bPB[bts[q]]], [bKTr4[q]])
                    P.pinned.discard(bts[q])
                for q in range(4):
                    for rr in range(2):
                        for g in range(2):
                            r8 = q * 2 + rr
                            P.mm(PB[bL][:, r8 * 32 + g * 16:r8 * 32 + g * 16 + 16], KTr4[q][:, (rr * 2 + g) * 128:(rr * 2 + g + 1) * 128],
                                 qrhs(g), True, True, [bKTr4[q], bQT[ti]], [bPB[bL]])
                softmax_pv(bL, 8, kb * 32, lambda r8, g, Vb3=Vb3: Vb3[:, r8, g * 128:(g + 1) * 128], bKV2[pz], kb == 0, False)
                P.pinned.discard(bL)
            bL = P.nextps()
            for g in range(2):
                P.mm(PB[bL][:, g * 16:(g + 1) * 16], sKTo[:, g * 128:(g + 1) * 128], qrhs(g), True, True, [bsKTo, bQT[ti]], [bPB[bL]])
            softmax_pv(bL, 1, 512, lambda r8, g: sVo[:, g * 128:(g + 1) * 128], bsVo, False, True)
            P.S.op("dve", lambda e: e.reciprocal(out=sSt[:, 0:32], in_=PB[bDn][:, 0:32]), [bPB[bDn]], [bsSt])
            for g in range(2):
                P.tt("dve", dsaoT[:, g * 4:(g + 1) * 4, ti * 128:ti * 128 + 4], PB[bO[g]][:, 0:16].rearrange("p (a t) -> p a t", t=4),
                     sSt[:, g * 16:(g + 1) * 16].rearrange("p (a t) -> p a t", t=4), ALU.mult, [bPB[bO[g]], bsSt], [bdsaoT[ti]])
            for bb in (bO[0], bO[1], bDn):
                P.pinned.discard(bb)

        def own_group(tiles_info, pre=None):
            T = len(tiles_info)
            tiles = list(range(T))
            N = T * 128

            def loads():
                if pre is not None:
                    pre()
                for ti, inf in enumerate(tiles_info):
                    S.dma("sp", H[:, ti, :], inf["x"], writes=[bH[ti]])
                    S.dma("sp", tab[ti][:], inf["tab"], writes=[btab[ti]])
                rmsnorm_T_multi([(H[:, ti, :], bH[ti], ti) for ti in tiles], G_MIX)
            steps = []
            for hg in range(2):
                def rq_step(slot, bslot, hg=hg):
                    if hg == 0:
                        loads()

                    def evac(ti, ps, bps):
                        i = P.ev_ctr % 2
                        P.ev_ctr += 1
                        P.cp("act", tmpA[i][:], ps[:, 0:512], [bps], [btmpA[i]])
                        rope("dve", tmpA[i][:].rearrange("p (h d) -> p h d", h=4), q4[:, ti, :].rearrange("p (h d) -> p h d", h=4),
                             tab[ti][:, 0:64], tab[ti][:, 64:128], 4, 64, 128, [btmpA[i], btab[ti]], [bq4[ti]])
                    proj_tiles(slot, bslot, 16, 512, tiles, u_lhs, u_bufs, evac)

                def rk_step(slot, bslot, hg=hg):
                    def evac(ti, ps, bps):
                        evac_rk(ps, bps, tab[ti], btab[ti], k4[:, ti, :], bk4[ti])
                    proj_tiles(slot, bslot, 16, 512, tiles, u_lhs, u_bufs, evac)

                def rv_step(slot, bslot, hg=hg):
                    def evac(ti, ps, bps):
                        P.cp(P.ev_eng(), v4[:, ti, :], ps[:, 0:512], [bps], [bv4[ti]])
                    proj_tiles(slot, bslot, 16, 512, tiles, u_lhs, u_bufs, evac)

                def rg_step(slot, bslot, hg=hg):
                    def evac(ti, ps, bps):
                        P.act(sg4[:, ti, :], ps[:, 0:512], AF.Silu, [bps], [bsg4[ti]])
                    proj_tiles(slot, bslot, 16, 512, tiles, u_lhs, u_bufs, evac)
                    for ti, inf in enumerate(tiles_info):
                        retention(ti, hg, inf["kind"] == "smp")
                for c0, fn in ((C_RQ, rq_step), (C_RK, rk_step), (C_RV, rv_step), (C_RG, rg_step)):
                    steps.append(([wseg(w_in[:, c0 + hg * 512:c0 + hg * 512 + 512], 16, 0, 512)], fn))
            if self.stop_after == "ret":
                return steps
            for blk in range(2):
                def dq_step(slot, bslot, blk=blk):
                    def evac(ti, ps, bps):
                        i = P.ev_ctr % 2
                        P.ev_ctr += 1
                        P.cp("act", tmpA[i][:], ps[:, 0:512], [bps], [btmpA[i]])
                        rope("dve", tmpA[i][:].rearrange("p (h d) -> p h d", h=4), bfA[i][:].rearrange("p (h d) -> p h d", h=4),
                             tab[ti][:, 128:144], tab[ti][:, 144:160], 4, 16, 128, [btmpA[i], btab[ti]], [bbfA[i]])
                        transposes_to(bfA[i], bbfA[i], 4, lambda j: QT[:, ti, blk * 4 + j, :], lambda j: bQT[ti])
                    proj_tiles(slot, bslot, 16, 512, tiles, u_lhs, u_bufs, evac)
                steps.append(([wseg(w_in[:, C_DQ + blk * 512:C_DQ + blk * 512 + 512], 16, 0, 512)], dq_step))

            if self.stop_after == "dq":
                return steps

            def kv_step(slot, bslot):
                def evac(ti, ps, bps):
                    inf = tiles_info[ti]
                    if inf["kind"] == "own":
                        evac_dkdv(ps, bps, tab[ti], btab[ti], 24 + inf["it"], inf["k_out"], inf["v_out"], True)
                    else:
                        evac_dkdv(ps, bps, tab[ti], btab[ti], 32, inf["k_out"], inf["v_out"], False)
                        P.cp("dve", kbf, kvf[:, 0:256], [bkvf], [bkbf])
                        P.cp("dve", sVo, kvf[:, 256:512], [bkvf], [bsVo])
                        transposes_to(kbf, bkbf, 2, lambda j: sKTo[:, j * 128:(j + 1) * 128], lambda j: bsKTo)
                proj_tiles(slot, bslot, 16, 512, tiles, u_lhs, u_bufs, evac)
            steps.append(([wseg(w_in[:, C_DK:C_DK + 512], 16, 0, 512)], kv_step))
            if self.stop_after == "kv":
                return steps
            for blk in range(2):
                def iq_step(slot, bslot, blk=blk):
                    def evac(ti, ps, bps):
                        i = P.ev_ctr % 2
                        P.ev_ctr += 1
                        P.cp("act", tmpA[i][:], ps[:, 0:512], [bps], [btmpA[i]])
                        rope("dve", tmpA[i][:].rearrange("p (h d) -> p h d", h=8), bfA[i][:].rearrange("p (h d) -> p h d", h=8),
                             tab[ti][:, 160:168], tab[ti][:, 168:176], 8, 8, 64, [btmpA[i], btab[ti]], [bbfA[i]])
                        transposes_to(bfA[i], bbfA[i], 4, lambda j: iqT[:, ti, blk * 4 + j, :], lambda j: biqT[ti])
                    proj_tiles(slot, bslot, 16, 512, tiles, u_lhs, u_bufs, evac)
                steps.append(([wseg(w_in[:, C_IQ + blk * 512:C_IQ + blk * 512 + 512], 16, 0, 512)], iq_step))

            if self.stop_after == "iq":
                return steps

            def ikw_step(slot, bslot):
                def evac(ti, ps, bps):
                    inf = tiles_info[ti]
                    P.cp("dve", iwf[:, ti, :], ps[:, 64:80], [bps], [biw[ti]])
                    if self.stop_after == "projA":
                        return
                    if inf["kind"] == "own":
                        evac_ik(ps, bps, tab[ti], btab[ti], 24 + inf["it"], inf["ik_out"], True)
                    else:
                        evac_ik(ps, bps, tab[ti], btab[ti], 32, inf["ik_out"], False)
                        P.cp("dve", ik2[:, 0:64], ikf[:], [bikf], [bik2])
                        P.cp("dve", ik2[:, 64:128], ikf[:], [bikf], [bik2])
                        transposes_to(ik2, bik2, 1, lambda j: sikTo, lambda j: bsikTo)
                proj_tiles(slot, bslot, 16, 80, tiles, u_lhs, u_bufs, evac)
                if self.stop_after in ("proj", "projA", "projB", "projC", "projD"):
                    return
                run_deferred(flush=True)
                if tiles_info[0]["kind"] == "own":
                    ctx = [None] * T
                    ctx[0] = dsa_idx(0, tiles_info[0]["it"], 0)
                    for ti in range(T):
                        dsa_bis(ctx[ti])
                        if ti + 1 < T:
                            ctx[ti + 1] = dsa_idx(ti + 1, tiles_info[ti + 1]["it"], (ti + 1) % 2)
                            dsa_att(ctx[ti])
                        else:
                            defer(lambda c=ctx[ti]: dsa_att(c), 3)
                else:
                    dsa_sample(0)
            steps.append(([wseg(w_in[:, C_IK:C_IK + 128], 16, 0, 128)], ikw_step))
            if self.stop_after in ("proj", "projA", "projB", "projC", "projD"):
                return steps
            for cb in range(4):
                def ga_step(slot, bslot, cb=cb):
                    def evac(ti, ps, bps):
                        P.act(sg4[:, ti, :], ps[:, 0:512], AF.Sigmoid, [bps], [bsg4[ti]])
                    proj_tiles(slot, bslot, 16, 512, tiles, u_lhs, u_bufs, evac)

                def ro_step(slot, bslot, cb=cb):
                    def evac(ti, ps, bps):
                        P.tt("dve", m1[:, ti, :], ps[:, 0:512], sg4[:, ti, :], ALU.mult, [bps, bsg4[ti]], [bm1[ti]])
                    proj_tiles(slot, bslot, 8, 512, tiles, lambda kc, ti: retyT[:, kc, ti * 128:(ti + 1) * 128],
                               lambda ti: [bretyT[ti]], evac)

                def gb_step(slot, bslot, cb=cb):
                    def evac(ti, ps, bps):
                        P.act(sg4[:, ti, :], ps[:, 0:512], AF.Sigmoid, [bps], [bsg4[ti]])
                    proj_tiles(slot, bslot, 16, 512, tiles, u_lhs, u_bufs, evac)

                def do_step(slot, bslot, cb=cb):
                    def evac(ti, ps, bps):
                        i = P.ev_ctr % 2
                        P.ev_ctr += 1
                        P.tt("dve", tmpA[i][:], ps[:, 0:512], sg4[:, ti, :], ALU.mult, [bps, bsg4[ti]], [btmpA[i]])
                        P.tt("dve", bfA[i][:], tmpA[i][:], m1[:, ti, :], ALU.add, [btmpA[i], bm1[ti]], [bbfA[i]])
                        transposes_to(bfA[i], bbfA[i], 4, lambda j: mrgT[:, cb * 4 + j, ti * 128:(ti + 1) * 128], lambda j: bmrgT[ti])
                    proj_tiles(slot, bslot, 8, 512, tiles, lambda kc, ti: dsaoT[:, kc, ti * 128:(ti + 1) * 128],
                               lambda ti: [bdsaoT[ti]], evac)
                steps.append(([wseg(w_in[:, C_GA + cb * 512:C_GA + cb * 512 + 512], 16, 0, 512)], ga_step))
                steps.append(([wseg(w_ro[:, cb * 512:cb * 512 + 512], 8, 0, 512)], ro_step))
                steps.append(([wseg(w_in[:, C_GB + cb * 512:C_GB + cb * 512 + 512], 16, 0, 512)], gb_step))
                steps.append(([wseg(w_do[:, cb * 512:cb * 512 + 512], 8, 0, 512)], do_step))
            for cb in range(4):
                def wo_step(slot, bslot, cb=cb):
                    if cb == 0:
                        for ti, inf in enumerate(tiles_info):
                            S.dma("sp", H[:, ti, :], inf["x"], writes=[bH[ti]])

                    def evac(ti, ps, bps):
                        hs = H[:, ti, cb * 512:(cb + 1) * 512]
                        P.tt("dve", hs, ps[:, 0:512], hs, ALU.add, [bps, bH[ti]], [bH[ti]])
                    proj_tiles(slot, bslot, 16, 512, tiles, lambda kc, ti: mrgT[:, kc, ti * 128:(ti + 1) * 128],
                               lambda ti: [bmrgT[ti]], evac)
                steps.append(([wseg(w_o[:, cb * 512:cb * 512 + 512], 16, 0, 512)], wo_step))
            smp_group = tiles_info[0]["kind"] == "smp"
            mkT_, mvb_, bmkT_, bmvb_ = (mkTs, mvbs, bmkTs, bmvbs) if smp_group else (mkT, mvb, bmkT, bmvb)

            def mq_step(slot, bslot):
                rmsnorm_T_multi([(H[:, ti, :], bH[ti], ti) for ti in tiles], G_MEM)
                for hd in range(4):
                    b = P.nextps()
                    for kc in range(16):
                        P.mm(PB[b][:, 0:N], slot[:, kc, hd * 128:(hd + 1) * 128], U[:, kc, 0:N], kc == 0, kc == 15,
                             [bU[t] for t in tiles] + bslot, [bPB[b]])
                    P.cp(P.ev_eng(), qmT[:, hd, 0:N], PB[b][:, 0:N], [bPB[b]], [bqmT])
                for hd in range(4):
                    for mc in range(2):
                        b = P.nextps()
                        P.mm(PB[b][:, 0:N], mkT_[:, hd, mc * 128:(mc + 1) * 128], qmT[:, hd, 0:N], True, True, [bmkT_, bqmT], [bPB[b]])
                        P.act(PTm[:, mc, 0:N], PB[b][:, 0:N], AF.Exp, [bPB[b]], [bPTm], scale=128.0 ** -0.5)
                    bo = P.nextps()
                    bd = P.nextps()
                    for mc in range(2):
                        P.mm(PB[bo][:, 0:N], mvb_[:, mc, hd * 128:(hd + 1) * 128], PTm[:, mc, 0:N], mc == 0, mc == 1, [bmvb_, bPTm], [bPB[bo]])
                    for mc in range(2):
                        P.mm(PB[bd][:, 0:N], ones[:], PTm[:, mc, 0:N], mc == 0, mc == 1, [bPTm] + RC, [bPB[bd]])
                    P.S.op("dve", lambda e, bd=bd: e.reciprocal(out=recf2[:, 0:N], in_=PB[bd][:, 0:N]), [bPB[bd]], [brec2])
                    P.tt("dve", omT[:, hd, 0:N], PB[bo][:, 0:N], recf2[:, 0:N], ALU.mult, [bPB[bo], brec2], [bomT])
            steps.append(([wseg(w_mq, 16, 0, 512)], mq_step))
            for cb in range(4):
                def mo_step(slot, bslot, cb=cb):
                    def evac(ti, ps, bps):
                        hs = H[:, ti, cb * 512:(cb + 1) * 512]
                        P.tt("dve", hs, ps[:, 0:512], hs, ALU.add, [bps, bH[ti]], [bH[ti]])
                    proj_tiles(slot, bslot, 4, 512, tiles, lambda kc, ti: omT[:, kc, ti * 128:(ti + 1) * 128],
                               lambda ti: [bomT], evac)
                steps.append(([wseg(w_mo[:, cb * 512:cb * 512 + 512], 4, 0, 512)], mo_step))
            for half in range(2):
                for j in range(8):
                    def up_step(slot, bslot, half=half, j=j):
                        if half == 0 and j == 0:
                            rmsnorm_T_multi([(H[:, ti, :], bH[ti], ti) for ti in tiles], G_MLP)
                        for fb in range(4):
                            b = P.nextps()
                            for kc in range(16):
                                P.mm(PB[b][:, 0:N], slot[:, kc, fb * 128:(fb + 1) * 128], U[:, kc, 0:N], kc == 0, kc == 15,
                                     [bU[t] for t in tiles] + bslot, [bPB[b]])
                            i = P.ev_ctr % 2
                            P.ev_ctr += 1
                            P.act(tmpA[i][:, 0:N], PB[b][:, 0:N], AF.Relu, [bPB[b]], [btmpA[i]])
                            P.tt("dve", aT[:, j * 4 + fb, 0:N], tmpA[i][:, 0:N], tmpA[i][:, 0:N], ALU.mult, [btmpA[i]], [baT])
                    c0 = half * 4096 + j * 512
                    steps.append(([wseg(w_up[:, c0:c0 + 512], 16, 0, 512)], up_step))
                for cb in range(4):
                    for qq in range(2):
                        def dn_step(slot, bslot, half=half, cb=cb, qq=qq):
                            if qq == 0:
                                P._acc = []
                                for ti in tiles:
                                    b = P.nextps()
                                    P.pinned.add(b)
                                    P._acc.append(b)
                            for ti in tiles:
                                b = P._acc[ti]
                                for kc in range(16):
                                    P.mm(PB[b][:], aT[:, qq * 16 + kc, ti * 128:(ti + 1) * 128], slot[:, kc, :],
                                         qq == 0 and kc == 0, qq == 1 and kc == 15, [baT] + bslot, [bPB[b]])
                            if qq == 1:
                                for ti in tiles:
                                    b = P._acc[ti]
                                    hs = H[:, ti, cb * 512:(cb + 1) * 512]
                                    P.tt("dve", hs, PB[b][:], hs, ALU.add, [bPB[b], bH[ti]], [bH[ti]])
                                    P.pinned.discard(b)
                        r0 = half * 4096 + qq * 2048
                        steps.append(([wseg(w_dn[r0:r0 + 2048, cb * 512:cb * 512 + 512], 16, 0, 512)], dn_step))

            def fin_step(slot, bslot):
                S.dma("sp", gfin, gfin_d, writes=[baT])
                for ti, inf in enumerate(tiles_info):
                    stt_, bst = new_stat()
                    P.memset("dve", stt_[:, 0:1], 0.0, [bst])
                    P.act(xn4[ti], H[:, ti, :], AF.Square, [bH[ti], bst], [bxn4[ti], bst], accum=stt_[:, 0:1])
                    P.rsqrt(stt_[:, 1:2], stt_[:, 0:1], 1.0 / D, bst)
                    P.stt("dve", H[:, ti, :], H[:, ti, :], stt_[:, 1:2], gfin, ALU.mult, ALU.mult, [bH[ti], bst, baT], [bH[ti]])
                    S.dma("sp", inf["y_out"], H[:, ti, :], reads=[bH[ti]])
            steps.append(([], fin_step))
            return steps

        steps = []
        steps += phase_mem()
        if self.stop_after == "mem":
            run_steps(steps)
            S.emit_all()
            return nc
        for g in range(self.n_pre_groups):
            steps += prefix_group(g, g == self.n_pre_groups - 1)

        def zero_init():
            for hg in range(2):
                P.memset("dve", S32[:, hg * 4:(hg + 1) * 4, :], 0.0, [bS32[hg]])
                P.memset("dve", Sbf[:, hg * 4:(hg + 1) * 4, :], 0.0, [bSbf[hg]])
            P.memset("dve", KT[:], 0.0, bKT)
            P.memset("dve", Vb[:], 0.0, bVb)
            P.memset("dve", ikT[:], 0.0, bikT)
        for og in range(self.n_own_groups):
            infos = []
            for t in range(4):
                it = og * 4 + t
                r0 = it * 128
                infos.append(dict(kind="own", it=it, x=x_own[r0:r0 + 128, :], tab=tab_own[r0:r0 + 128, :],
                                  k_out=k_own[r0:r0 + 128, :], v_out=v_own[r0:r0 + 128, :], ik_out=ik_own[r0:r0 + 128, :],
                                  y_out=y_own[r0:r0 + 128, :]))
            steps += own_group(infos, pre=zero_init if (og == 0 and self.n_pre_groups == 0) else None)

        def st_out_step(slot, bslot):
            for hg in range(2):
                S.dma("sp", st_own[hg * 4:(hg + 1) * 4].rearrange("h d e -> d h e"), S32[:, hg * 4:(hg + 1) * 4, :], reads=[bS32[hg]])
        if self.n_own_groups > 0:
            steps.append(([], st_out_step))
        if self.do_sample:
            def smp_pre():
                for hg in range(2):
                    S.dma("sp", S32s[:, hg * 4:(hg + 1) * 4, :], st_in[hg * 4:(hg + 1) * 4].rearrange("h d e -> d h e"), writes=[bS32s[hg]])
                    P.cp("dve", Sbfs[:, hg * 4:(hg + 1) * 4, :], S32s[:, hg * 4:(hg + 1) * 4, :], [bS32s[hg]], [bSbfs[hg]])
                load_sample_mem()
                smp_index_prep()
            infos = [dict(kind="smp", it=0, x=x_smp[:, :], tab=tab_smp[:, :], k_out=k_smp[:, :], v_out=v_smp[:, :],
                          ik_out=ik_smp[:, :], y_out=y_smp[:, :])]
            steps += own_group(infos, pre=smp_pre)

            def st_out_s(slot, bslot):
                for hg in range(2):
                    S.dma("sp", st_smp[hg * 4:(hg + 1) * 4].rearrange("h d e -> d h e"), S32s[:, hg * 4:(hg + 1) * 4, :], reads=[bS32s[hg]])
            steps.append(([], st_out_s))
        run_steps(steps)
        S.emit_all()
        return nc


def _rope_tab(pos, n_half, theta):
    inv = (np.float32(theta) ** (-(np.arange(n_half, dtype=np.float32) / np.float32(n_half)))).astype(np.float32)
    ang = pos.astype(np.float32)[:, None] * inv[None, :]
    return np.cos(ang).astype(np.float32), np.sin(ang).astype(np.float32)


def _gammas():
    return np.log1p(-np.exp2(-5.0 - np.arange(8, dtype=np.float64)))


def _tab(pos, kdec=None):
    n = pos.shape[0]
    t = np.zeros((n, 192), np.float32)
    c, s = _rope_tab(pos, 64, 10000.0)
    t[:, 0:64], t[:, 64:128] = c, s
    c, s = _rope_tab(pos, 16, 500000.0)
    t[:, 128:144], t[:, 144:160] = c, s
    c, s = _rope_tab(pos, 8, 500000.0)
    t[:, 160:168], t[:, 168:176] = c, s
    if kdec is not None:
        t[:, 176:184] = kdec
    return t


def _consts(p):
    lg = _gammas()
    c = np.zeros((128, 64), np.float32)
    cb = np.zeros((128, 2304), np.float32)
    cb[:, 0:128] = np.eye(128, dtype=np.float32)
    j = np.arange(128)
    cb[:, 128:256] = np.where(j[None, :] <= j[:, None], 0.0, NEG)
    diff = (j[None, :] - j[:, None]).astype(np.float64)
    for h in range(8):
        cb[:, 256 + h * 128:256 + (h + 1) * 128] = np.where(diff >= 0, np.exp(lg[h] * np.maximum(diff, 0.0)), 0.0)
        qd = np.exp(lg[h] * (j + 1.0))
        cb[:, 1280 + h * 128:1280 + (h + 1) * 128] = np.diag(qd)
        c[:, 0 + h] = qd
        c[:, 8 + h] = np.exp(lg[h] * (127.0 - j))
        c[:, 16 + h] = np.exp(lg[h] * (3.0 - j))
        c[:, 24 + h] = np.exp(lg[h] * 128.0)
        c[:, 32 + h] = np.exp(lg[h] * 4.0)
    for v in range(3):
        c[:, 40 + v] = 0.0 if v < p else NEG
    for t in range(4):
        c[:, 44 + t] = np.where(j <= t, 0.0, NEG)
    return c, cb


_PROG_CACHE = {}


def _get_prog(**kw):
    key = tuple(sorted(kw.items()))
    if key not in _PROG_CACHE:
        _PROG_CACHE[key] = Prog(**kw).build()
    return _PROG_CACHE[key]


def make_in_maps(inp, cores=range(8)):
    f = lambda a: np.ascontiguousarray(np.asarray(a, dtype=np.float32))
    xp = f(inp["x_prompt"]); xs = f(inp["x_sample"]); memp = f(inp["mem_prompt"])
    lg = _gammas()
    gvec = np.zeros((128, 96), np.float32)
    for off, name in ((0, "g_mix"), (16, "g_mem"), (32, "g_memkv"), (48, "g_mlp")):
        gvec[:, off:off + 16] = f(inp[name])[0].reshape(16, 128).T
    gvec[:, 64:72] = f(inp["gn_ret"])[0].reshape(8, 128).T
    gfin = np.ascontiguousarray(np.broadcast_to(f(inp["g_final"])[None, :], (128, D)))
    shared = dict(
        cbf=_consts(0)[1], gvec=gvec, gfin=gfin,
        w_in=f(inp["w_in"])[0], w_ret_out=f(inp["w_ret_out"])[0], w_dsa_out=f(inp["w_dsa_out"])[0], w_o=f(inp["w_o"])[0],
        w_mq=f(inp["w_mq"])[0], w_mk=f(inp["w_mk"])[0], w_mv=f(inp["w_mv"])[0], w_mo=f(inp["w_mo"])[0],
        w_up=f(inp["w_up"])[0], w_down=f(inp["w_down"])[0],
    )
    if USE_POOLS:
        shared.update(pool_k=f(inp["cache_k"])[0].reshape(1280 * 16, 2048), pool_v=f(inp["cache_v"])[0].reshape(1280 * 16, 2048),
                      pool_ik=f(inp["cache_idx_k"])[0].reshape(1280 * 8, 1024))
    pt = np.asarray(inp["page_table"]).astype(np.int32)
    maps = []
    for c in cores:
        b, p = c // 4, c % 4
        m = dict(shared)
        m["x_own"] = np.ascontiguousarray(xp[b, 1024 * p:1024 * (p + 1)])
        xpre = np.zeros((3072, D), np.float32)
        xpre[:1024 * p] = xp[b, :1024 * p]
        m["x_pre"] = xpre
        xsm = np.zeros((128, D), np.float32)
        xsm[:4] = xs[c]
        m["x_smp"] = xsm
        m["mem"] = np.ascontiguousarray(memp[b])
        pos_own = np.arange(1024 * p, 1024 * (p + 1))
        m["tab_own"] = _tab(pos_own)
        pos_pre = np.arange(3072)
        kd = np.zeros((3072, 8), np.float64)
        valid = pos_pre < 1024 * p
        ex = (1024 * p - 1 - pos_pre).astype(np.float64)
        for h in range(8):
            kd[:, h] = np.where(valid, np.exp(lg[h] * np.maximum(ex, 0.0)), 0.0)
        m["tab_pre"] = _tab(pos_pre, kd.astype(np.float32))
        pos_s = PAST + np.arange(128)
        m["tab_smp"] = _tab(pos_s)
        m["cst"] = _consts(p)[0]
        m["state_smp"] = np.ascontiguousarray(f(inp["state_ret"])[0, c])
        m["cmk"] = np.ascontiguousarray(f(inp["cache_mem_k"])[0, c].reshape(256, 512))
        m["cmv"] = np.ascontiguousarray(f(inp["cache_mem_v"])[0, c].reshape(256, 512))
        if USE_POOLS:
            m["ptab"] = np.ascontiguousarray(pt[c].reshape(128, 1))
        maps.append(m)
    return maps


def assemble(res):
    y_p = np.zeros((2, 4096, D), np.float32)
    y_s = np.zeros((8, 4, D), np.float32)
    st_p = np.zeros((1, 2, 8, 128, 128), np.float32)
    k_p = np.zeros((1, 2, 4096, 2, 128), np.float32)
    v_p = np.zeros((1, 2, 4096, 2, 128), np.float32)
    ik_p = np.zeros((1, 2, 4096, 64), np.float32)
    mk_p = np.zeros((1, 2, 256, 4, 128), np.float32)
    mv_p = np.zeros((1, 2, 256, 4, 128), np.float32)
    st_s = np.zeros((1, 8, 8, 128, 128), np.float32)
    k_s = np.zeros((1, 8, 4, 2, 128), np.float32)
    v_s = np.zeros((1, 8, 4, 2, 128), np.float32)
    ik_s = np.zeros((1, 8, 4, 64), np.float32)
    for c, r in enumerate(res):
        b, p = c // 4, c % 4
        sl = slice(1024 * p, 1024 * (p + 1))
        y_p[b, sl] = r["y_own"]
        k_p[0, b, sl] = r["k_own"].reshape(1024, 2, 128)
        v_p[0, b, sl] = r["v_own"].reshape(1024, 2, 128)
        ik_p[0, b, sl] = r["ik_own"]
        if p == 3:
            st_p[0, b] = r["st_own"]
        if p == 0:
            mk_p[0, b] = r["mk_o"].reshape(256, 4, 128)
            mv_p[0, b] = r["mv_o"].reshape(256, 4, 128)
        y_s[c] = r["y_smp"][:4]
        st_s[0, c] = r["st_smp"]
        k_s[0, c] = r["k_smp"][:4].reshape(4, 2, 128)
        v_s[0, c] = r["v_smp"][:4].reshape(4, 2, 128)
        ik_s[0, c] = r["ik_smp"][:4]
    return (y_p, y_s, st_p, k_p, v_p, ik_p, mk_p, mv_p, st_s, k_s, v_s, ik_s)


def kernel(**inputs):
    nc = _get_prog()
    in_maps = make_in_maps(inputs)
    res = run_bass_kernel_spmd(nc, in_maps, core_ids=list(range(8)))
    return assemble(res.results)
```

```python
import numpy as np
from contextlib import ExitStack
import concourse.bass as bass
import concourse.mybir as mybir
from concourse.bass_utils import run_bass_kernel_spmd

F32 = mybir.dt.float32
BF16 = mybir.dt.bfloat16
I32 = mybir.dt.int32
ALU = mybir.AluOpType
AF = mybir.ActivationFunctionType
AX = mybir.AxisListType

D = 2048
NIN = 10832
C_RQ, C_RK, C_RV, C_RG, C_DQ, C_DK, C_DV, C_IQ, C_IK, C_IW, C_GA, C_GB = (
    0, 1024, 2048, 3072, 4096, 5120, 5376, 5632, 6656, 6720, 6736, 8784)
EPS = 1e-6
PAST = 16384
NEG = -30000.0
NIT = 16
TOPK = 256
ENGS = ("pe", "act", "dve", "pool", "sp")
USE_POOLS = True


class Buf:
    __slots__ = ("name", "last_w", "readers", "excl", "last_any")

    def __init__(self, name="", excl=False):
        self.name = name
        self.last_w = None
        self.readers = []
        self.excl = excl
        self.last_any = None


class Op:
    __slots__ = ("eng", "emit", "deps", "is_dma", "sem", "val", "signal")

    def __init__(self, eng, emit, is_dma):
        self.eng = eng
        self.emit = emit
        self.deps = []
        self.is_dma = is_dma
        self.sem = None
        self.val = None
        self.signal = is_dma


class Sched:
    def __init__(self, nc, n_dma_sems=24):
        self.nc = nc
        self.ops = {e: [] for e in ENGS}
        self.n_dma_sems = n_dma_sems

    def op(self, eng, emit, reads=(), writes=(), dma=False):
        o = Op(eng, emit, dma)
        deps = []
        for b in reads:
            if b.last_w is not None:
                deps.append(b.last_w)
        for b in writes:
            if b.last_w is not None:
                deps.append(b.last_w)
            deps.extend(b.readers)
        for b in list(reads) + list(writes):
            if b.excl:
                if b.last_any is not None and b.last_any.eng != eng:
                    deps.append(b.last_any)
                b.last_any = o
        seen = set()
        for d in deps:
            if id(d) in seen or d is o:
                continue
            seen.add(id(d))
            if (not d.is_dma) and (not dma) and d.eng == "pe" and eng == "pe":
                continue
            o.deps.append(d)
            d.signal = True
        for b in reads:
            b.readers.append(o)
        for b in writes:
            b.last_w = o
            b.readers = []
        self.ops[eng].append(o)
        return o

    def dma(self, q, out, in_, reads=(), writes=()):
        return self.op(q, lambda e: e.dma_start(out=out, in_=in_), reads, writes, dma=True)

    def emit_all(self):
        nc = self.nc
        with ExitStack() as st:
            esem = {e: st.enter_context(nc.semaphore(f"s_{e}")) for e in ENGS}
            dsem = {q: [st.enter_context(nc.semaphore(f"d_{q}{i}")) for i in range(self.n_dma_sems)]
                    for q in ("sp", "pool", "act")}
            for e in ENGS:
                cnt = 0
                k = 0
                dcount = [0] * self.n_dma_sems
                for o in self.ops[e]:
                    if o.is_dma:
                        i = k % self.n_dma_sems
                        k += 1
                        dcount[i] += 16
                        o.sem = dsem[e][i]
                        o.val = dcount[i]
                    elif o.signal:
                        cnt += 1
                        o.sem = esem[e]
                        o.val = cnt
            block = st.enter_context(nc.Block())
            ops = self.ops

            def gen(e, engobj):
                clock = {}

                def wait(sem, val):
                    if clock.get(id(sem), 0) >= val:
                        return
                    engobj.wait_ge(sem, val)
                    clock[id(sem)] = val
                for o in ops[e]:
                    need = {}
                    for d in o.deps:
                        k = id(d.sem)
                        if k not in need or need[k][1] < d.val:
                            need[k] = (d.sem, d.val)
                    if o.is_dma and o.val > 16:
                        k = id(o.sem)
                        if k not in need or need[k][1] < o.val - 16:
                            need[k] = (o.sem, o.val - 16)
                    for sem, val in need.values():
                        wait(sem, val)
                    ins = o.emit(engobj)
                    if o.signal:
                        ins.then_inc(o.sem, 16 if o.is_dma else 1)
                if e == "sp":
                    for q in ("sp", "pool", "act"):
                        last = {}
                        for o in ops[q]:
                            if o.is_dma:
                                last[id(o.sem)] = (o.sem, o.val)
                        for sem, val in last.values():
                            wait(sem, val)

            @block.tensor
            def _(eng):
                gen("pe", eng)

            @block.scalar
            def _(eng):
                gen("act", eng)

            @block.vector
            def _(eng):
                gen("dve", eng)

            @block.gpsimd
            def _(eng):
                gen("pool", eng)

            @block.sync
            def _(eng):
                gen("sp", eng)


class Prog:
    def __init__(self, do_sample=True, n_own_groups=2, n_pre_groups=6, stop_after=None):
        self.nc = bass.Bass("TRN2", target_bir_lowering=False)
        self.S = Sched(self.nc)
        self.st = ExitStack()
        self.do_sample = do_sample
        self.n_own_groups = n_own_groups
        self.n_pre_groups = n_pre_groups
        self.stop_after = stop_after
        self.use_pools = False
        self.ps_ctr = 0
        self.pinned = set()
        self.ev_ctr = 0
        self.stat_ctr = 0

    def dram_in(self, name, shape, dt=F32):
        return self.nc.dram_tensor(name, list(shape), dt, kind="ExternalInput").ap()

    def dram_out(self, name, shape, dt=F32):
        return self.nc.dram_tensor(name, list(shape), dt, kind="ExternalOutput").ap()

    def sb(self, name, shape, dt):
        return self.st.enter_context(self.nc.sbuf_tensor(name, list(shape), dt))

    def mm(self, out, lhsT, rhs, st, sp, R, W, skip=False):
        if skip:
            self.S.op("pe", lambda e: e.matmul(out, lhsT=lhsT, rhs=rhs, start=st, stop=sp, skip_group_check=True), R, W)
        else:
            self.S.op("pe", lambda e: e.matmul(out, lhsT=lhsT, rhs=rhs, start=st, stop=sp), R, W)

    def act(self, out, in_, func, R, W, scale=1.0, bias=0.0, accum=None):
        if accum is None:
            self.S.op("act", lambda e: e.activation(out=out, in_=in_, func=func, bias=bias, scale=scale), R, W)
        else:
            self.S.op("act", lambda e: e.activation(out=out, in_=in_, func=func, bias=bias, scale=scale,
                                                    accum_out=accum), R, W)

    def tt(self, eng, out, in0, in1, op, R, W):
        self.S.op(eng, lambda e: e.tensor_tensor(out=out, in0=in0, in1=in1, op=op), R, W)

    def ts(self, eng, out, in0, s1, s2, op0, op1, R, W, accum=None):
        if op1 is None:
            self.S.op(eng, lambda e: e.tensor_scalar(out=out, in0=in0, scalar1=s1, scalar2=None, op0=op0), R, W)
        elif accum is None:
            self.S.op(eng, lambda e: e.tensor_scalar(out=out, in0=in0, scalar1=s1, scalar2=s2, op0=op0, op1=op1), R, W)
        else:
            self.S.op(eng, lambda e: e.tensor_scalar(out=out, in0=in0, scalar1=s1, scalar2=s2, op0=op0, op1=op1,
                                                     accum_out=accum), R, W)

    def rsqrt(self, out, in_, scale, b):
        self.ts("dve", out, in_, scale, EPS, ALU.mult, ALU.add, [b], [b])
        self.S.op("act", lambda e: e.activation(out=out, in_=out, func=AF.Sqrt), [b], [b])
        self.S.op("dve", lambda e: e.reciprocal(out=out, in_=out), [b], [b])

    def stt(self, eng, out, in0, scalar, in1, op0, op1, R, W):
        self.S.op(eng, lambda e: e.scalar_tensor_tensor(out=out, in0=in0, scalar=scalar, in1=in1, op0=op0, op1=op1), R, W)

    def cp(self, eng, out, in_, R, W):
        if eng == "act":
            self.act(out, in_, AF.Copy, R, W)
        else:
            self.S.op(eng, lambda e: e.tensor_copy(out=out, in_=in_), R, W)

    def red(self, eng, out, in_, op, R, W):
        self.S.op(eng, lambda e: e.tensor_reduce(out=out, in_=in_, axis=AX.X, op=op), R, W)

    def memset(self, eng, ap, v, W):
        self.S.op(eng, lambda e: e.memset(ap, v), [], W)

    def ev_eng(self):
        self.ev_ctr += 1
        return "act" if self.ev_ctr % 2 else "dve"

    def nextps(self):
        while True:
            b = self.ps_ctr % 8
            self.ps_ctr += 1
            if b not in self.pinned:
                return b

    def build(self):
        nc, S = self.nc, self.S
        P = self
        x_own = P.dram_in("x_own", [1024, D])
        x_pre = P.dram_in("x_pre", [3072, D])
        x_smp = P.dram_in("x_smp", [128, D])
        mem = P.dram_in("mem", [256, D])
        tab_own = P.dram_in("tab_own", [1024, 192])
        tab_pre = P.dram_in("tab_pre", [3072, 192])
        tab_smp = P.dram_in("tab_smp", [128, 192])
        cst = P.dram_in("cst", [128, 64])
        cbf_d = P.dram_in("cbf", [128, 2304])
        gvec = P.dram_in("gvec", [128, 96])
        gfin_d = P.dram_in("gfin", [128, D])
        w_in = P.dram_in("w_in", [D, NIN])
        w_ro = P.dram_in("w_ret_out", [1024, D])
        w_do = P.dram_in("w_dsa_out", [1024, D])
        w_o = P.dram_in("w_o", [D, D])
        w_mq = P.dram_in("w_mq", [D, 512])
        w_mk = P.dram_in("w_mk", [D, 512])
        w_mv = P.dram_in("w_mv", [D, 512])
        w_mo = P.dram_in("w_mo", [512, D])
        w_up = P.dram_in("w_up", [D, 8192])
        w_dn = P.dram_in("w_down", [8192, D])
        st_in = P.dram_in("state_smp", [8, 128, 128])
        cmk = P.dram_in("cmk", [256, 512])
        cmv = P.dram_in("cmv", [256, 512])
        pool_k2 = P.dram_in("pool_k", [1280 * 16, 2048])
        pool_v2 = P.dram_in("pool_v", [1280 * 16, 2048])
        pool_ik2 = P.dram_in("pool_ik", [1280 * 8, 1024])
        ptab = P.dram_in("ptab", [128, 1], I32)

        y_own = P.dram_out("y_own", [1024, D])
        y_smp = P.dram_out("y_smp", [128, D])
        st_own = P.dram_out("st_own", [8, 128, 128])
        st_smp = P.dram_out("st_smp", [8, 128, 128])
        k_own = P.dram_out("k_own", [1024, 256])
        v_own = P.dram_out("v_own", [1024, 256])
        ik_own = P.dram_out("ik_own", [1024, 64])
        k_smp = P.dram_out("k_smp", [128, 256])
        v_smp = P.dram_out("v_smp", [128, 256])
        ik_smp = P.dram_out("ik_smp", [128, 64])
        mk_o = P.dram_out("mk_o", [256, 512])
        mv_o = P.dram_out("mv_o", [256, 512])

        sb = P.sb
        KT = sb("KT", [128, 2, 4096], BF16); bKT = [Buf() for _ in range(33)]
        Vb = sb("Vb", [128, 32, 256], BF16); bVb = [Buf() for _ in range(33)]
        ikT = sb("ikT", [128, 4096], BF16); bikT = [Buf() for _ in range(33)]
        S32 = sb("S32", [128, 8, 128], F32); bS32 = [Buf(), Buf()]
        Sbf = sb("Sbf", [128, 8, 128], BF16); bSbf = [Buf(), Buf()]
        S32s, Sbfs, bS32s, bSbfs = S32, Sbf, bS32, bSbf
        cstf = sb("cstf_sb", [128, 64], F32); bcst = Buf()
        cbf = sb("cbf_sb", [128, 2304], BF16)
        ident = cbf[:, 0:128]
        tri = cbf[:, 128:256]
        intraT = cbf[:, 256:1280].rearrange("p (h i) -> p h i", h=8)
        diagq = cbf[:, 1280:2304].rearrange("p (h i) -> p h i", h=8)
        I4 = sb("I4", [128, 512], BF16)
        ones = sb("ones", [128, 128], BF16)
        gv = sb("gv", [128, 96], F32); bgv = Buf()
        kdec_own = cstf[:, 8:16]
        kdec_smp = cstf[:, 16:24]
        cdec_own = cstf[:, 24:32]
        cdec_smp = cstf[:, 32:40]
        slotb = cstf[:, 40:43]

        NW = 2
        Wr = [sb(f"Wr{i}", [128, 16, 512], BF16) for i in range(NW)]
        bW = [[Buf() for _ in range(4)] for _ in range(NW)]
        U = sb("U", [128, 16, 512], BF16); bU = [Buf() for _ in range(4)]
        H = sb("H", [128, 4, D], F32); bH = [Buf() for _ in range(4)]
        R = sb("R", [128, 32, 512], BF16)
        retyT = R[:, 0:8, :]; bretyT = [Buf() for _ in range(4)]
        dsaoT = R[:, 8:16, :]; bdsaoT = [Buf() for _ in range(4)]
        mrgT = R[:, 16:32, :]; bmrgT = [Buf() for _ in range(4)]
        aT = R; baT = Buf()
        gfin = R[:, 0:8, :].rearrange("p a b -> p (a b)").bitcast(F32)
        mbs = R[:, 16:24, :].rearrange("p a b -> p (a b)"); bmb = Buf()
        diagw = R[:, 24:28, :].rearrange("p a (b c) -> p (a b) c", c=128); bdiagw = Buf()
        rl = [R[:, 28 + i, :] for i in range(3)]; brl = [Buf() for _ in range(3)]
        xn4 = [R[:, 16 + 4 * i:20 + 4 * i, :].rearrange("p a b -> p (a b)") for i in range(4)]; bxn4 = [Buf() for _ in range(4)]
        stat = sb("stat", [128, 28, 8], F32); bstat = [Buf() for _ in range(28)]
        tab = [sb(f"tab{i}", [128, 192], F32) for i in range(4)]; btab = [Buf() for _ in range(4)]
        tabB = [sb(f"tabB{i}", [128, 192], F32) for i in range(4)]; btabB = [Buf() for _ in range(4)]
        U2 = R[:, 0:16, :]; bU2 = [Buf() for _ in range(4)]
        tmpA = [sb(f"tmpA{i}", [128, 512], F32) for i in range(2)]; btmpA = [Buf() for _ in range(2)]
        rtmp = [sb(f"rtmp{i}", [128, 512], F32) for i in range(2)]; brtmp = Buf()
        bfA = [sb(f"bfA{i}", [128, 512], BF16) for i in range(2)]; bbfA = [Buf() for _ in range(2)]
        PTb = [sb(f"PT{i}", [128, 512], BF16) for i in range(3)]; bPT = [Buf() for _ in range(3)]
        AR = sb("AR", [128, 9216], BF16)
        q4 = AR[:, 0:2048].rearrange("p (a b) -> p a b", a=4); bq4 = [Buf() for _ in range(4)]
        k4 = AR[:, 2048:4096].rearrange("p (a b) -> p a b", a=4); bk4 = [Buf() for _ in range(4)]
        v4 = AR[:, 4096:6144].rearrange("p (a b) -> p a b", a=4); bv4 = [Buf() for _ in range(4)]
        kd4 = AR[:, 6144:6656]; bkd4 = Buf()
        tr3 = [AR[:, 6656 + i * 512:6656 + (i + 1) * 512] for i in range(3)]; btr3 = [Buf() for _ in range(3)]
        smT = AR[:, 8192:8704]; bsmT = Buf()
        rpre = AR[:, 8704:9216]; brpre = Buf()
        QT = AR[:, 0:4096].rearrange("p (a h t) -> p a h t", a=4, h=8); bQT = [Buf() for _ in range(4)]
        iqT = AR[:, 4096:8192].rearrange("p (a h t) -> p a h t", a=4, h=8); biqT = [Buf() for _ in range(4)]
        kbf = AR[:, 8192:8448]; bkbf = Buf()
        kbf4 = [AR[:, 8192 + 256 * i:8448 + 256 * i] for i in range(4)]; bkbf4 = [Buf() for _ in range(4)]
        ik2 = sb("ik2", [128, 128], BF16); bik2 = Buf()
        qmT = AR[:, 0:2048].rearrange("p (a b) -> p a b", a=4); bqmT = Buf()
        PTm = AR[:, 2048:3072].rearrange("p (a b) -> p a b", a=2); bPTm = Buf()
        omT = AR[:, 3072:5120].rearrange("p (a b) -> p a b", a=4); bomT = Buf()
        recf2 = AR[:, 5120:6144].bitcast(F32); brec2 = Buf()
        sg4 = H[:, 0, :].bitcast(BF16)[:, 0:2048].rearrange("p (a b) -> p a b", a=4); bsg4 = [bH[0]] * 4
        iwf = sb("iwf", [128, 4, 16], F32); biw = [Buf() for _ in range(4)]
        ikf = sb("ikf", [128, 64], F32); bikf = Buf()
        mkT = sb("mkT", [128, 4, 256], BF16); bmkT = Buf()
        mvb = sb("mvb", [128, 2, 512], BF16); bmvb = Buf()
        mkTs, mvbs, bmkTs, bmvbs = mkT, mvb, bmkT, bmvb
        orow = H[:, 2, 0:512]; borow = bH[2]
        kvf = H[:, 2, 512:1024]; bkvf = bH[2]
        recf = H[:, 2, 1024:1536]; brec = bH[2]
        m1 = H[:, 3, :].rearrange("p (a b) -> p a b", a=4); bm1 = [bH[3]] * 4
        Isc = H[:, 0:2, :].rearrange("p a b -> p (a b)")
        bIsc = [bH[0], bH[1]]
        Isc2 = H[:, 2:4, :].rearrange("p a b -> p (a b)")
        bIsc2 = [bH[2], bH[3]]

        MR = R[:, 16:32, :].rearrange("p a b -> p (a b)")
        sIT = MR[:, 0:1032].bitcast(F32); bsIT = Buf()
        sMask = MR[:, 1040:1556]; bsMask = Buf()
        sG = MR[:, 1560:2076]; bsG = Buf()
        sRl = MR[:, 2080:3104].bitcast(F32); bsRl = Buf()
        sTmp = MR[:, 3104:4128].bitcast(F32); bsTmp = Buf()
        sE = MR[:, 4128:4640].bitcast(F32); bsE = Buf()
        sPT = MR[:, 4640:4896]; bsPT = Buf()
        sKTr = [MR[:, 4896:5408], MR[:, 5408:5920]]; bsKTr = [Buf(), Buf()]
        sikTr = [MR[:, 5920:6432], MR[:, 6432:6944]]; bsikTr = [Buf(), Buf()]
        sKTo = MR[:, 6944:7200]; bsKTo = Buf()
        sVo = MR[:, 7200:7456]; bsVo = Buf()
        sikTo = MR[:, 7456:7584]; bsikTo = Buf()
        sWb = MR[:, 7584:7712].bitcast(F32); bsWb = Buf()
        sSt = MR[:, 7712:7840].bitcast(F32); bsSt = Buf()
        sX = MR[:, 7840:7904]; bsX = Buf()
        sPcb = MR[:, 7904:7912]; bsPcb = Buf()
        sIdxF = MR[:, 7912:7976].bitcast(F32)
        sIdxI = MR[:, 7976:8040].bitcast(I32); bsIdx = Buf()
        sPt = MR[:, 8040:8042].bitcast(I32)
        Kst = [KT[:, i, :].bitcast(F32) for i in range(2)]; bKst = [Buf(), Buf()]
        Vst = [Vb[:, 16 * i:16 * (i + 1), :].rearrange("p a b -> p (a b)").bitcast(F32) for i in range(2)]; bVst = [Buf(), Buf()]
        Ist = [ikT[:, 2048 * i:2048 * (i + 1)].bitcast(F32) for i in range(2)]; bIst = [Buf(), Buf()]
        H1bf = H[:, 1, :].bitcast(BF16)
        sKbf = H1bf[:, 0:2048]; sVbf = H1bf[:, 2048:4096]
        sIbf = H[:, 3, 512:1536].bitcast(BF16)
        triT4 = cstf[:, 44:48]

        PB = [self.st.enter_context(nc.psum_tensor(f"pb{i}", [128, 512], F32)) for i in range(8)]
        bPB = [Buf(excl=True) for _ in range(8)]

        S.dma("sp", cstf[:], cst, writes=[bcst])
        S.dma("sp", gv[:], gvec, writes=[bgv])
        bc = Buf()
        S.dma("pool", cbf[:], cbf_d, writes=[bc])
        for j in range(4):
            P.cp("dve", I4[:, j * 128:(j + 1) * 128], ident, [bc], [bc])
        P.memset("dve", ones[:], 1.0, [bc])
        G_MIX, G_MEM, G_MEMKV, G_MLP, G_GN = 0, 16, 32, 48, 64
        RC = [bc, bcst, bgv]

        def new_stat():
            i = P.stat_ctr % 28
            P.stat_ctr += 1
            return stat[:, i, :], bstat[i]

        def rms_a(items):
            sts = []
            for (xap, bx, ti) in items:
                stt_, bst = new_stat()
                sts.append((stt_, bst))
                P.memset("dve", stt_[:, 0:1], 0.0, [bst])
                P.act(xn4[ti], xap, AF.Square, [bx, bst], [bxn4[ti], bst], accum=stt_[:, 0:1])
            for (xap, bx, ti), (stt_, bst) in zip(items, sts):
                P.ts("dve", stt_[:, 1:2], stt_[:, 0:1], 1.0 / D, EPS, ALU.mult, ALU.add, [bst], [bst])
            for (xap, bx, ti), (stt_, bst) in zip(items, sts):
                P.S.op("act", lambda e, o=stt_[:, 1:2]: e.activation(out=o, in_=o, func=AF.Sqrt), [bst], [bst])
            for (xap, bx, ti), (stt_, bst) in zip(items, sts):
                P.S.op("dve", lambda e, o=stt_[:, 1:2]: e.reciprocal(out=o, in_=o), [bst], [bst])
            for (xap, bx, ti), (stt_, bst) in zip(items, sts):
                P.ts("dve", xn4[ti], xap, stt_[:, 1:2], None, ALU.mult, None, [bx, bst], [bxn4[ti]])

        def rms_b(items, goff, Ud=None, bUd=None):
            Ud = U if Ud is None else Ud
            bUd = bU if bUd is None else bUd
            pend = []
            for (xap, bx, ti) in items:
                for q in range(4):
                    bk = P.nextps()
                    for j in range(4):
                        kc = q * 4 + j
                        P.mm(PB[bk][:, j * 128:(j + 1) * 128], xn4[ti][:, kc * 128:(kc + 1) * 128], ident, True, True,
                             [bxn4[ti]] + RC, [bPB[bk]])
                    pend.append((bk, q, ti))
                    if len(pend) >= 4:
                        evac_T(pend.pop(0), goff, Ud, bUd)
            while pend:
                evac_T(pend.pop(0), goff, Ud, bUd)

        def rmsnorm_T_multi(items, goff):
            rms_a(items)
            rms_b(items, goff)

        def evac_T(item, goff, Ud, bUd):
            bk, q, ti = item
            dst = Ud[:, 4 * q:4 * q + 4, ti * 128:(ti + 1) * 128]
            src = PB[bk][:].rearrange("p (j t) -> p j t", j=4)
            gb_ = gv[:, goff + 4 * q:goff + 4 * q + 4].unsqueeze(2).to_broadcast([128, 4, 128])
            P.tt("dve", dst, src, gb_, ALU.mult, [bPB[bk]] + RC, [bUd[ti]])

        def rmsnorm_T(xap, bx, goff, ti, ncols_tile=128):
            rmsnorm_T_multi([(xap, bx, ti)], goff)

        brt4 = [Buf() for _ in range(4)]

        def rope(eng, src3, dst3, cos2, sin2, Hh, half, dh, Rr, Ww, inplace=False):
            cb_ = cos2.unsqueeze(1).to_broadcast([128, Hh, half])
            sb_ = sin2.unsqueeze(1).to_broadcast([128, Hh, half])
            x1 = src3[:, :, 0:half]
            x2 = src3[:, :, half:2 * half]
            n = Hh * half
            tA = rtmp[0][:, 0:n].rearrange("p (h d) -> p h d", h=Hh)
            tB = rtmp[0][:, 256:256 + n].rearrange("p (h d) -> p h d", h=Hh)
            tC = rtmp[1][:, 0:n].rearrange("p (h d) -> p h d", h=Hh)
            tD = rtmp[1][:, 256:256 + n].rearrange("p (h d) -> p h d", h=Hh)
            Rr = list(Rr)
            P.tt(eng, tA, x1, cb_, ALU.mult, Rr, [brt4[0]])
            P.tt(eng, tB, x2, sb_, ALU.mult, Rr, [brt4[1]])
            P.tt(eng, tC, x1, sb_, ALU.mult, Rr, [brt4[2]])
            P.tt(eng, tD, x2, cb_, ALU.mult, Rr, [brt4[3]])
            P.tt(eng, dst3[:, :, 0:half], tA, tB, ALU.subtract, [brt4[0], brt4[1], brt4[2], brt4[3]] + Rr, Ww)
            P.tt(eng, dst3[:, :, half:2 * half], tC, tD, ALU.add, [brt4[2], brt4[3]] + Rr, Ww)
            if 2 * half < dh and not inplace:
                P.cp(eng, dst3[:, :, 2 * half:dh], src3[:, :, 2 * half:dh], Rr, Ww)

        def load_tab(src_rows):
            i = P.stat_ctr % 4
            P.stat_ctr += 1
            S.dma("sp", tab[i][:], src_rows, writes=[btab[i]])
            return tab[i], btab[i]

        def proj_tiles(slot, bslot, nkc, ncol, tiles, lhs_of, lhs_bufs_of, evac):
            banks = []
            for ti in tiles:
                b = P.nextps()
                P.pinned.add(b)
                banks.append(b)
            nq = (nkc + 3) // 4
            for q in range(nq):
                for i, ti in enumerate(tiles):
                    b = banks[i]
                    for kc in range(q * 4, min(nkc, q * 4 + 4)):
                        P.mm(PB[b][:, 0:ncol], lhs_of(kc, ti), slot[:, kc, 0:ncol], kc == 0, kc == nkc - 1,
                             lhs_bufs_of(ti) + [bslot[kc // 4]], [bPB[b]])
            for b in banks:
                P.pinned.discard(b)
            for i, ti in enumerate(tiles):
                evac(ti, PB[banks[i]], bPB[banks[i]])

        def u_lhs(kc, ti):
            return U[:, kc, ti * 128:(ti + 1) * 128]

        def u_bufs(ti):
            return [bU[ti]]

        def transposes_to(src_bf, bsrc, n, dst_of, bdst_of, scale_of=None, part=128):
            b = P.nextps()
            for j in range(n):
                P.mm(PB[b][:, j * 128:(j + 1) * 128], src_bf[:, j * 128:(j + 1) * 128], ident, True, True,
                     [bsrc] + RC, [bPB[b]])
            eng = P.ev_eng()
            for j in range(n):
                src = PB[b][:, j * 128:(j + 1) * 128]
                if scale_of is not None:
                    P.ts("dve", dst_of(j), src, scale_of(j), None, ALU.mult, None, [bPB[b]] + RC, [bdst_of(j)])
                else:
                    P.cp(eng, dst_of(j), src, [bPB[b]], [bdst_of(j)])

        def evac_dkdv(ps, bps, tb, btb, chunk, outk=None, outv=None, to_keys=True):
            P.cp("act", kvf, ps[:, 0:512], [bps], [bkvf])
            rope("dve", kvf[:, 0:256].rearrange("p (h d) -> p h d", h=2), kvf[:, 0:256].rearrange("p (h d) -> p h d", h=2),
                 tb[:, 128:144], tb[:, 144:160], 2, 16, 128, [bkvf, btb], [bkvf], inplace=True)
            if outk is not None:
                S.dma("sp", outk, kvf[:, 0:256], reads=[bkvf])
                S.dma("sp", outv, kvf[:, 256:512], reads=[bkvf])
            if to_keys:
                kb_ = kbf4[chunk % 4]
                bkb2 = bkbf4[chunk % 4]
                P.cp("dve", kb_, kvf[:, 0:256], [bkvf], [bkb2])
                P.cp("dve", Vb[:, chunk, :], kvf[:, 256:512], [bkvf], [bVb[chunk]])
                defer(lambda: transposes_to(kb_, bkb2, 2, lambda j: KT[:, j, chunk * 128:(chunk + 1) * 128], lambda j: bKT[chunk]), 2)

        def evac_ik(ps, bps, tb, btb, chunk, outik=None, to_keys=True):
            P.cp("act", ikf[:], ps[:, 0:64], [bps], [bikf])
            v3 = ikf[:].rearrange("p (h d) -> p h d", h=1)
            rope("dve", v3, v3, tb[:, 160:168], tb[:, 168:176], 1, 8, 64, [bikf, btb], [bikf], inplace=True)
            if self.stop_after == "projB":
                return
            if outik is not None:
                S.dma("sp", outik, ikf[:], reads=[bikf])
            if self.stop_after == "projC":
                return
            if to_keys:
                i2 = PTb[0][:, (chunk % 4) * 128:(chunk % 4 + 1) * 128]
                P.cp("dve", i2[:, 0:64], ikf[:], [bikf], [bPT[0]])
                P.cp("dve", i2[:, 64:128], ikf[:], [bikf], [bPT[0]])
                defer(lambda: transposes_to(i2, bPT[0], 1, lambda j: ikT[:, chunk * 128:(chunk + 1) * 128], lambda j: bikT[chunk]), 2)

        def evac_rk(ps, bps, tb, btb, dst_bf, bdst):
            i = P.ev_ctr % 2
            P.ev_ctr += 1
            P.act(tmpA[i][:], ps[:, 0:512], AF.Copy, [bps], [btmpA[i]], scale=128.0 ** -0.5)
            rope("dve", tmpA[i][:].rearrange("p (h d) -> p h d", h=4), dst_bf.rearrange("p (h d) -> p h d", h=4),
                 tb[:, 0:64], tb[:, 64:128], 4, 64, 128, [btmpA[i], btb], [bdst])

        def wseg(src2d, nkc, coff, ncol):
            return (src2d, nkc, coff, ncol)

        P.deferred = []

        def defer(fn, delay=1):
            P.deferred.append([delay, fn])

        def run_deferred(flush=False):
            keep = []
            todo = []
            for it in P.deferred:
                it[0] -= 1
                if flush or it[0] <= 0:
                    todo.append(it[1])
                else:
                    keep.append(it)
            P.deferred = keep
            for fn in todo:
                fn()

        wcache = {}

        def run_steps(steps):
            def issue(i):
                slot = i % NW
                for (src, nkc, coff, ncol) in steps[i][0]:
                    key = str(src)
                    first = key not in wcache
                    if first:
                        scr = nc.dram_tensor(f"wc{len(wcache)}", [128, nkc, ncol], BF16).ap()
                        wcache[key] = (scr, [Buf() for _ in range(4)])
                    scr, bscr = wcache[key]
                    for k0 in range(0, nkc, 4):
                        k1 = min(nkc, k0 + 4)
                        q = k0 // 4
                        dst = Wr[slot][:, k0:k1, coff:coff + ncol]
                        if first:
                            S.dma("pool", dst, src[k0 * 128:k1 * 128, :].rearrange("(k p) n -> p k n", p=128), writes=[bW[slot][q]])
                            S.dma("sp", scr[:, k0:k1, :], dst, reads=[bW[slot][q]], writes=[bscr[q]])
                        else:
                            S.dma("sp", dst, scr[:, k0:k1, :], reads=[bscr[q]], writes=[bW[slot][q]])
            if not steps:
                return
            issue(0)
            for i in range(len(steps)):
                if i + 1 < len(steps):
                    issue(i + 1)
                steps[i][1](Wr[i % NW], bW[i % NW])
                run_deferred()

        def phase_mem():
            steps = []

            def mk_step(slot, bslot):
                for mt in range(2):
                    S.dma("sp", H[:, mt, :], mem[mt * 128:(mt + 1) * 128, :], writes=[bH[mt]])
                rmsnorm_T_multi([(H[:, mt, :], bH[mt], mt) for mt in range(2)], G_MEMKV)
                def evac(ti, ps, bps):
                    i = ti % 2
                    P.cp("act", tmpA[i][:], ps[:, 0:512], [bps], [btmpA[i]])
                    S.dma("sp", mk_o[ti * 128:(ti + 1) * 128, :], tmpA[i][:], reads=[btmpA[i]])
                    P.cp("dve", bfA[i][:], tmpA[i][:], [btmpA[i]], [bbfA[i]])
                    transposes_to(bfA[i], bbfA[i], 4, lambda j: mkT[:, j, ti * 128:(ti + 1) * 128], lambda j: bmkT)
                proj_tiles(slot, bslot, 16, 512, [0, 1], u_lhs, u_bufs, evac)

            def mv_step(slot, bslot):
                def evac(ti, ps, bps):
                    i = ti % 2
                    P.cp("act", tmpA[i][:], ps[:, 0:512], [bps], [btmpA[i]])
                    S.dma("sp", mv_o[ti * 128:(ti + 1) * 128, :], tmpA[i][:], reads=[btmpA[i]])
                    P.cp("dve", mvb[:, ti, :], tmpA[i][:], [btmpA[i]], [bmvb])
                proj_tiles(slot, bslot, 16, 512, [0, 1], u_lhs, u_bufs, evac)
            steps.append(([wseg(w_mk, 16, 0, 512)], mk_step))
            steps.append(([wseg(w_mv, 16, 0, 512)], mv_step))
            return steps

        def load_sample_mem():
            for mt in range(2):
                i = mt
                S.dma("sp", tmpA[i][:], cmk[mt * 128:(mt + 1) * 128, :], writes=[btmpA[i]])
                P.cp("dve", bfA[i][:], tmpA[i][:], [btmpA[i]], [bbfA[i]])
                transposes_to(bfA[i], bbfA[i], 4, lambda j, mt=mt: mkTs[:, j, mt * 128:(mt + 1) * 128], lambda j: bmkTs)
            S.dma("pool", mvbs[:], cmv.rearrange("(c p) n -> p c n", p=128), writes=[bmvbs])

        pre_state_first = [True, True]

        pre_loads = {}

        def prefix_group(g, last):
            tiles = list(range(4))

            Ug, bUg = (U, bU) if g % 2 == 0 else (U2, bU2)
            tabs, btabs = (tab, btab) if g % 2 == 0 else (tabB, btabB)
            items = [(H[:, ti, :], bH[ti], ti) for ti in tiles]

            def loads_a():
                if g == 0:
                    P.pinned.add(6)
                    P.pinned.add(7)
                    P.memset("dve", PB[6][:], 0.0, [bPB[6]])
                    P.memset("dve", PB[7][:], 0.0, [bPB[7]])
                for ti in tiles:
                    row0 = (g * 4 + ti) * 128
                    S.dma("sp", H[:, ti, :], x_pre[row0:row0 + 128, :], writes=[bH[ti]])
                    S.dma("sp", tabs[ti][:], tab_pre[row0:row0 + 128, :], writes=[btabs[ti]])
                rms_a(items)

            def loads_b():
                rms_b(items, G_MIX, Ug, bUg)
            pre_loads[g] = (loads_a, loads_b)

            def ug_lhs(kc, ti):
                return Ug[:, kc, ti * 128:(ti + 1) * 128]

            def ug_bufs(ti):
                return [bUg[ti]]
            steps = []
            kbuf = [k4, AR[:, 0:2048].rearrange("p (a b) -> p a b", a=4)]
            vbuf = [v4, AR[:, 6144:8192].rearrange("p (a b) -> p a b", a=4)]
            bkb_ = [bk4, bq4]
            bvb_ = [bv4, [bkd4, btr3[0], btr3[1], btr3[2]]]

            def state_mm(hg):
                bank = 6 + hg
                for ti in tiles:
                    for h in range(4):
                        lastf = last and ti == 3
                        P.mm(PB[bank][:, h * 128:(h + 1) * 128], kbuf[hg][:, ti, h * 128:(h + 1) * 128],
                             vbuf[hg][:, ti, h * 128:(h + 1) * 128], False, lastf, [bkb_[hg][ti], bvb_[hg][ti]], [bPB[bank]], skip=True)

            for hg in range(2):
                def rk_step(slot, bslot, hg=hg):
                    if hg == 0 and g == 0:
                        loads_a()
                        loads_b()

                    def evac(ti, ps, bps):
                        evac_rk(ps, bps, tabs[ti], btabs[ti], kbuf[hg][:, ti, :], bkb_[hg][ti])
                        kd3 = kbuf[hg][:, ti, :].rearrange("p (h d) -> p h d", h=4)
                        dec = tabs[ti][:, 176 + hg * 4:176 + hg * 4 + 4].unsqueeze(2).to_broadcast([128, 4, 128])
                        P.tt("dve", kd3, kd3, dec, ALU.mult, [bkb_[hg][ti], btabs[ti]], [bkb_[hg][ti]])
                    proj_tiles(slot, bslot, 16, 512, tiles, ug_lhs, ug_bufs, evac)
                    if hg == 1:
                        state_mm(0)

                def rv_step(slot, bslot, hg=hg):
                    def evac(ti, ps, bps):
                        P.cp(P.ev_eng(), vbuf[hg][:, ti, :], ps[:, 0:512], [bps], [bvb_[hg][ti]])
                    proj_tiles(slot, bslot, 16, 512, tiles, ug_lhs, ug_bufs, evac)
                    if (g + 1) in pre_loads:
                        pre_loads[g + 1][hg]()
                steps.append(([wseg(w_in[:, C_RK + hg * 512:C_RK + hg * 512 + 512], 16, 0, 512)], rk_step))
                steps.append(([wseg(w_in[:, C_RV + hg * 512:C_RV + hg * 512 + 512], 16, 0, 512)], rv_step))

            def kv_step(slot, bslot):
                def evac(ti, ps, bps):
                    evac_dkdv(ps, bps, tabs[ti], btabs[ti], g * 4 + ti)
                proj_tiles(slot, bslot, 16, 512, tiles, ug_lhs, ug_bufs, evac)
                state_mm(1)

            def ik_step(slot, bslot):
                def evac(ti, ps, bps):
                    evac_ik(ps, bps, tabs[ti], btabs[ti], g * 4 + ti)
                proj_tiles(slot, bslot, 16, 64, tiles, ug_lhs, ug_bufs, evac)
                if last:
                    prefix_finish()
                    P.pinned.discard(6)
                    P.pinned.discard(7)
            steps.append(([wseg(w_in[:, C_DK:C_DK + 512], 16, 0, 512)], kv_step))
            steps.append(([wseg(w_in[:, C_IK:C_IK + 64], 16, 0, 64)], ik_step))
            return steps

        def prefix_finish():
            for hg in range(2):
                P.cp("dve", S32[:, hg * 4:(hg + 1) * 4, :].rearrange("p h e -> p (h e)"), PB[6 + hg][:], [bPB[6 + hg]], [bS32[hg]])
                P.cp("act", Sbf[:, hg * 4:(hg + 1) * 4, :].rearrange("p h e -> p (h e)"), PB[6 + hg][:], [bPB[6 + hg]], [bSbf[hg]])

        def retention(ti, hg, smp):
            S32_, Sbf_, bS32_, bSbf_ = (S32s, Sbfs, bS32s, bSbfs) if smp else (S32, Sbf, bS32, bSbf)
            kdec = kdec_smp if smp else kdec_own
            cdec = cdec_smp if smp else cdec_own
            qsrc = q4[:, ti, :]
            ksrc = k4[:, ti, :]
            vsrc = v4[:, ti, :]
            P.tt("dve", kd4.rearrange("p (h d) -> p h d", h=4), ksrc.rearrange("p (h d) -> p h d", h=4),
                 kdec[:, hg * 4:hg * 4 + 4].unsqueeze(2).to_broadcast([128, 4, 128]), ALU.mult, [bk4[ti]] + RC, [bkd4])
            for which, (src, bsrc, rhs_of) in enumerate((
                    (qsrc, bq4[ti], lambda h: ident),
                    (ksrc, bk4[ti], lambda h: ident),
                    (qsrc, bq4[ti], lambda h: diagq[:, hg * 4 + h, :]))):
                b = P.nextps()
                for h in range(4):
                    P.mm(PB[b][:, h * 128:(h + 1) * 128], src[:, h * 128:(h + 1) * 128], rhs_of(h), True, True,
                         [bsrc] + RC, [bPB[b]])
                P.cp(P.ev_eng(), tr3[which], PB[b][:], [bPB[b]], [btr3[which]])
            qT, kT, qdT = tr3
            b = P.nextps()
            for h in range(4):
                P.mm(PB[b][:, h * 128:(h + 1) * 128], kT[:, h * 128:(h + 1) * 128], qT[:, h * 128:(h + 1) * 128], True, True,
                     [btr3[0], btr3[1]], [bPB[b]])
            P.tt("dve", smT, PB[b][:], intraT[:, hg * 4:(hg + 1) * 4, :].rearrange("p h i -> p (h i)"), ALU.mult,
                 [bPB[b]] + RC, [bsmT])
            bo = P.nextps()
            for h in range(4):
                P.mm(PB[bo][:, h * 128:(h + 1) * 128], smT[:, h * 128:(h + 1) * 128], vsrc[:, h * 128:(h + 1) * 128], True, False,
                     [bsmT, bv4[ti]], [bPB[bo]])
                P.mm(PB[bo][:, h * 128:(h + 1) * 128], qdT[:, h * 128:(h + 1) * 128], Sbf_[:, hg * 4 + h, :], False, True,
                     [btr3[2], bSbf_[hg]], [bPB[bo]])
            P.cp("act", orow, PB[bo][:], [bPB[bo]], [borow])
            bs = P.nextps()
            for h in range(4):
                P.mm(PB[bs][:, h * 128:(h + 1) * 128], kd4[:, h * 128:(h + 1) * 128], vsrc[:, h * 128:(h + 1) * 128], True, True,
                     [bkd4, bv4[ti]], [bPB[bs]])
            Sv = S32_[:, hg * 4:(hg + 1) * 4, :]
            P.tt("dve", Sv, Sv, cdec[:, hg * 4:hg * 4 + 4].unsqueeze(2).to_broadcast([128, 4, 128]), ALU.mult, [bS32_[hg]] + RC, [bS32_[hg]])
            P.tt("dve", Sv, Sv, PB[bs][:].rearrange("p (h e) -> p h e", h=4), ALU.add, [bPB[bs], bS32_[hg]], [bS32_[hg]])
            P.cp("act", Sbf_[:, hg * 4:(hg + 1) * 4, :], S32_[:, hg * 4:(hg + 1) * 4, :], [bS32_[hg]], [bSbf_[hg]])
            stt_, bst = new_stat()
            o3 = orow.rearrange("p (h e) -> p h e", h=4)
            P.red("dve", stt_[:, 0:4], o3, ALU.add, [borow], [bst])
            sq = tmpA[1]
            P.tt("dve", sq[:], orow, orow, ALU.mult, [borow], [btmpA[1]])
            P.red("dve", stt_[:, 4:8], sq[:].rearrange("p (h e) -> p h e", h=4), ALU.add, [btmpA[1]], [bst])
            stt2, bst2 = new_stat()
            P.ts("dve", stt2[:, 0:4], stt_[:, 0:4], 1.0 / 128, None, ALU.mult, None, [bst], [bst2])
            P.tt("dve", stt2[:, 4:8], stt2[:, 0:4], stt2[:, 0:4], ALU.mult, [bst2], [bst2])
            P.stt("dve", stt2[:, 4:8], stt_[:, 4:8], 1.0 / 128, stt2[:, 4:8], ALU.mult, ALU.subtract, [bst, bst2], [bst2])
            P.rsqrt(stt2[:, 4:8], stt2[:, 4:8], 1.0, bst2)
            P.tt("dve", o3, o3, stt2[:, 0:4].unsqueeze(2).to_broadcast([128, 4, 128]), ALU.subtract, [borow, bst2], [borow])
            P.tt("dve", o3, o3, stt2[:, 4:8].unsqueeze(2).to_broadcast([128, 4, 128]), ALU.mult, [borow, bst2], [borow])
            rp, brp = bfA[ti % 2], bbfA[ti % 2]
            P.tt("dve", rp[:], orow, sg4[:, ti, :], ALU.mult, [borow, bsg4[ti]], [brp])

            def final():
                transposes_to(rp, brp, 4, lambda j: retyT[:, hg * 4 + j, ti * 128:(ti + 1) * 128], lambda j: bretyT[ti],
                              scale_of=lambda j: gv[:, G_GN + hg * 4 + j:G_GN + hg * 4 + j + 1])
            return final

        def dsa_idx(ti, it, par):
            Isc_, bIsc_ = (Isc, bIsc) if par == 0 else (Isc2, bIsc2)
            nkc = 24 + it + 1
            Sk = nkc * 128
            nblk = (Sk + 511) // 512
            keyb = [bKT[c] for c in range(nkc)]
            for h in range(16):
                P.act(diagw[:, h, :], ident, AF.Copy, [biw[ti]] + RC, [bdiagw], scale=iwf[:, ti, h:h + 1])
            stt_, bst = new_stat()
            stt2, bst2 = new_stat()
            bIs = [P.nextps(), None]
            P.pinned.add(bIs[0])
            bIs[1] = P.nextps()
            P.pinned.add(bIs[1])
            items = [(kb, h) for kb in range(nblk) for h in range(16)]

            def stage_a(kb, h, i):
                c0 = kb * 512
                n = min(512, Sk - c0)
                b = P.nextps()
                hp = (h % 2) * 64
                P.mm(PB[b][:, 0:n], iqT[hp:hp + 64, ti, h // 2, :], ikT[hp:hp + 64, c0:c0 + n], True, True,
                     [biqT[ti]] + [bikT[c] for c in range(c0 // 128, (c0 + n) // 128)], [bPB[b]])
                r = i % 3
                P.act(rl[r][:, 0:n], PB[b][:, 0:n], AF.Relu, [bPB[b]], [brl[r]])

            def stage_b(kb, h, i):
                c0 = kb * 512
                n = min(512, Sk - c0)
                bI = bIs[kb % 2]
                r = i % 3
                P.mm(PB[bI][:, 0:n], diagw[:, h, :], rl[r][:, 0:n], h == 0, h == 15, [bdiagw, brl[r]], [bPB[bI]])
                if h == 15:
                    P.cp("act", Isc_[:, c0:c0 + n], PB[bI][:, 0:n], [bPB[bI]], bIsc_)
                    P.red("dve", stt_[:, kb:kb + 1], Isc_[:, c0:c0 + n], ALU.max, bIsc_, [bst])
                    P.red("dve", stt2[:, kb:kb + 1], Isc_[:, c0:c0 + n], ALU.min, bIsc_, [bst2])
                    if c0 < 3072:
                        P.ts("dve", Isc_[:, c0:c0 + n], Isc_[:, c0:c0 + n], slotb[:, c0 // 1024:c0 // 1024 + 1], None, ALU.add, None, bIsc_ + RC, bIsc_)
                    else:
                        d0 = 3072 + it * 128
                        if d0 >= c0 and d0 < c0 + n:
                            P.tt("dve", Isc_[:, d0:d0 + 128], Isc_[:, d0:d0 + 128], tri, ALU.add, bIsc_ + RC, bIsc_)
            LAG = 2
            for i, (kb, h) in enumerate(items):
                stage_a(kb, h, i)
                if i >= LAG:
                    stage_b(items[i - LAG][0], items[i - LAG][1], i - LAG)
            for i in range(max(0, len(items) - LAG), len(items)):
                stage_b(items[i][0], items[i][1], i)
            P.pinned.discard(bIs[0])
            P.pinned.discard(bIs[1])
            return dict(ti=ti, it=it, nkc=nkc, Sk=Sk, nblk=nblk, Isc=Isc_, bIsc=bIsc_, stt=stt_, bst=bst, stt2=stt2, bst2=bst2)

        def dsa_bis(c):
            ti, it, nkc, Sk, nblk = c["ti"], c["it"], c["nkc"], c["Sk"], c["nblk"]
            Isc_, bIsc_, stt_, bst, stt2, bst2 = c["Isc"], c["bIsc"], c["stt"], c["bst"], c["stt2"], c["bst2"]
            st3, bst3 = new_stat()
            P.red("dve", st3[:, 0:1], stt_[:, 0:nblk], ALU.max, [bst], [bst3])
            P.red("dve", st3[:, 1:2], stt2[:, 0:nblk], ALU.min, [bst2], [bst3])
            P.stt("dve", st3[:, 2:3], st3[:, 0:1], 1.0, st3[:, 1:2], ALU.add, ALU.subtract, [bst3], [bst3])
            P.ts("dve", st3[:, 2:3], st3[:, 2:3], 0.5, None, ALU.mult, None, [bst3], [bst3])
            cnts, bcn = new_stat()
            cnts2, bcn2 = new_stat()
            cnts3, bcn3 = new_stat()
            P.memset("dve", cnts[:], 0.0, [bcn])
            P.memset("dve", cnts2[:], 0.0, [bcn2])
            P.memset("dve", cnts3[:], 0.0, [bcn3])
            for itn in range(NIT):
                cc, bcc = (cnts, bcn) if itn < 8 else ((cnts2, bcn2) if itn < 16 else (cnts3, bcn3))
                col = itn % 8
                P.tt("dve", st3[:, 3:4], st3[:, 1:2], st3[:, 2:3], ALU.add, [bst3], [bst3])
                P.ts("dve", mbs[:, 0:Sk], Isc_[:, 0:Sk], st3[:, 3:4], 0.0, ALU.is_ge, ALU.add, bIsc_ + [bst3, bcc], [bmb, bcc],
                     accum=cc[:, col:col + 1])
                P.ts("dve", st3[:, 4:5], cc[:, col:col + 1], float(TOPK), st3[:, 2:3], ALU.is_ge, ALU.mult, [bcc, bst3], [bst3])
                P.tt("dve", st3[:, 1:2], st3[:, 1:2], st3[:, 4:5], ALU.add, [bst3], [bst3])
                P.ts("dve", st3[:, 2:3], st3[:, 2:3], 0.5, None, ALU.mult, None, [bst3], [bst3])
            P.ts("dve", mbs[:, 0:Sk], Isc_[:, 0:Sk], st3[:, 1:2], NEG, ALU.is_lt, ALU.mult, bIsc_ + [bst3], [bmb])

        def dsa_att(c):
            ti, it, nkc, Sk = c["ti"], c["it"], c["nkc"], c["Sk"]
            for g in range(2):
                bo = P.nextps()
                P.pinned.add(bo)
                bd = P.nextps()
                P.pinned.add(bd)

                def att_a(kc):
                    b = P.nextps()
                    P.mm(PB[b][:], KT[:, g, kc * 128:(kc + 1) * 128], QT[:, ti, g * 4:(g + 1) * 4, :].rearrange("p h t -> p (h t)"),
                         True, False, [bKT[kc], bQT[ti]], [bPB[b]])
                    P.mm(PB[b][:], mbs[:, kc * 128:(kc + 1) * 128], I4[:], False, True, [bmb] + RC, [bPB[b]])
                    r = kc % 3
                    P.act(PTb[r][:], PB[b][:], AF.Exp, [bPB[b]], [bPT[r]], scale=128.0 ** -0.5)

                def att_b(kc):
                    r = kc % 3
                    P.mm(PB[bo][:], Vb[:, kc, g * 128:(g + 1) * 128], PTb[r][:], kc == 0, kc == nkc - 1, [bVb[kc], bPT[r]], [bPB[bo]])
                    P.mm(PB[bd][:], ones[:], PTb[r][:], kc == 0, kc == nkc - 1, [bPT[r]] + RC, [bPB[bd]])
                LAG2 = 2
                for kc in range(nkc):
                    att_a(kc)
                    if kc >= LAG2:
                        att_b(kc - LAG2)
                for kc in range(max(0, nkc - LAG2), nkc):
                    att_b(kc)
                P.S.op("dve", lambda e, bd=bd: e.reciprocal(out=tmpA[0][:], in_=PB[bd][:]), [bPB[bd]], [btmpA[0]])
                for hh in range(4):
                    P.tt("dve", dsaoT[:, g * 4 + hh, ti * 128:(ti + 1) * 128], PB[bo][:, hh * 128:(hh + 1) * 128],
                         tmpA[0][:, hh * 128:(hh + 1) * 128], ALU.mult, [bPB[bo], btmpA[0]], [bdsaoT[ti]])
                P.pinned.discard(bo)
                P.pinned.discard(bd)

        def smp_index_prep():
            S.dma("sp", sPt, ptab, writes=[bsIdx])
            P.cp("dve", sIdxF[:, 31:32], sPt, [bsIdx], [bsIdx])
            for rb in range(16):
                P.ts("dve", sIdxF[:, rb:rb + 1], sIdxF[:, 31:32], 16.0, float(rb), ALU.mult, ALU.add, [bsIdx], [bsIdx])
            for rb in range(8):
                P.ts("dve", sIdxF[:, 16 + rb:17 + rb], sIdxF[:, 31:32], 8.0, float(rb), ALU.mult, ALU.add, [bsIdx], [bsIdx])
            P.cp("dve", sIdxI[:, 0:24], sIdxF[:, 0:24], [bsIdx], [bsIdx])

        def gather(dst, bdst, src2d, col, extra_w=()):
            S.op("pool", lambda e: e.indirect_dma_start(out=dst, out_offset=None, in_=src2d,
                                                        in_offset=bass.IndirectOffsetOnAxis(ap=sIdxI[:, col:col + 1], axis=0)),
                 [bsIdx], [bdst] + list(extra_w), dma=True)

        def dsa_sample(ti):
            SC = 128.0 ** -0.5
            P.memset("dve", dsaoT[:, :, ti * 128:(ti + 1) * 128], 0.0, [bdsaoT[ti]])
            X4 = sX[0:4, 0:64].rearrange("p (q t j) -> p t q j", t=4, q=2)
            iw4 = iwf[0:4, ti, :].rearrange("p (j q) -> p q j", q=2).unsqueeze(1).to_broadcast([4, 4, 2, 8])
            id4 = ident[0:4, 0:4].unsqueeze(2).unsqueeze(3).to_broadcast([4, 4, 2, 8])
            P.tt("dve", X4, iw4, id4, ALU.mult, [biw[ti]] + RC, [bsX])
            b = P.nextps()
            P.mm(PB[b][:, 0:64], ones[0:4, 0:128], sX[0:4, 0:64], True, True, [bsX] + RC, [bPB[b]])
            P.cp("dve", sWb, PB[b][:, 0:64], [bPB[b]], [bsWb])

            def score_batch(bDs, nr, dst_cols):
                n = nr * 32
                for par in range(2):
                    o0 = par * 256
                    P.act(sRl[:, o0:o0 + n], PB[bDs[par]][:, 0:n], AF.Relu, [bPB[bDs[par]]], [bsRl])
                    P.tt("dve", sTmp[:, o0:o0 + n].rearrange("p (r c) -> p r c", c=32), sRl[:, o0:o0 + n].rearrange("p (r c) -> p r c", c=32),
                         sWb[:, par * 32:(par + 1) * 32].unsqueeze(1).to_broadcast([128, nr, 32]), ALU.mult, [bsRl, bsWb], [bsTmp])
                    P.red("dve", sRl[:, o0:o0 + nr * 4], sTmp[:, o0:o0 + n].rearrange("p (a j) -> p a j", j=8), ALU.add, [bsTmp, bsRl], [bsRl])
                P.tt("dve", sIT[:, dst_cols], sRl[:, 0:nr * 4], sRl[:, 256:256 + nr * 4], ALU.add, [bsRl], [bsIT])

            def dots_mm(bDs, col0, ikT_ap, bik):
                for par in range(2):
                    outv = PB[bDs[par]][:, col0:col0 + 32]
                    rhs = iqT[par * 64:(par + 1) * 64, ti, :, 0:4].rearrange("p j t -> p t j")
                    P.mm(outv, ikT_ap[par * 64:(par + 1) * 64, :], rhs, True, True, [biqT[ti], bik], [bPB[bDs[par]]])

            gather(Ist[0], bIst[0], pool_ik2, 16, extra_w=bikT)
            for ib in range(8):
                if ib + 1 < 8:
                    gather(Ist[(ib + 1) % 2], bIst[(ib + 1) % 2], pool_ik2, 16 + ib + 1, extra_w=bikT if ib == 0 else ())
                src3 = Ist[ib % 2].rearrange("p (r d) -> p r d", d=64)
                dst3 = sIbf.rearrange("p (r d) -> p r d", d=128)
                P.cp("dve", dst3[:, :, 0:64], src3, [bIst[ib % 2]], [bH[3]])
                P.cp("act", dst3[:, :, 64:128], src3, [bIst[ib % 2]], [bH[3]])
                for half in range(2):
                    bDs = [P.nextps(), None]
                    P.pinned.add(bDs[0])
                    bDs[1] = P.nextps()
                    P.pinned.add(bDs[1])
                    for q in range(2):
                        qq = half * 2 + q
                        bt = P.nextps()
                        for rr in range(4):
                            r = qq * 4 + rr
                            P.mm(PB[bt][:, rr * 128:(rr + 1) * 128], dst3[:, r, :], ident, True, True, [bH[3]] + RC, [bPB[bt]])
                        P.cp(P.ev_eng(), sikTr[qq % 2], PB[bt][:], [bPB[bt]], [bsikTr[qq % 2]])
                        for rr in range(4):
                            dots_mm(bDs, (q * 4 + rr) * 32, sikTr[qq % 2][:, rr * 128:(rr + 1) * 128], bsikTr[qq % 2])
                    c0 = (ib * 16 + half * 8) * 4
                    score_batch(bDs, 8, slice(c0, c0 + 32))
                    P.pinned.discard(bDs[0])
                    P.pinned.discard(bDs[1])
            bDs = [P.nextps(), None]
            P.pinned.add(bDs[0])
            bDs[1] = P.nextps()
            P.pinned.discard(bDs[0])
            dots_mm(bDs, 0, sikTo, bsikTo)
            score_batch(bDs, 1, slice(512, 516))
            P.tt("dve", sIT[:, 512:516], sIT[:, 512:516], triT4, ALU.add, [bsIT] + RC, [bsIT])
            IT3 = sIT[:, 0:516].rearrange("p (c t) -> p t c", t=4)
            P.red("dve", sSt[:, 0:4], IT3, ALU.max, [bsIT], [bsSt])
            P.red("dve", sSt[:, 4:8], sIT[:, 0:512].rearrange("p (c t) -> p t c", t=4), ALU.min, [bsIT], [bsSt])
            P.cp("dve", sPcb[:, 0:8], sSt[:, 0:8], [bsSt], [bsPcb])
            b = P.nextps()
            P.mm(PB[b][0:4, 0:128], sPcb[:, 0:4], ident, True, True, [bsPcb] + RC, [bPB[b]])
            P.mm(PB[b][0:4, 128:256], sPcb[:, 4:8], ident, True, True, [bsPcb] + RC, [bPB[b]])
            g4 = sSt[0:4, 8:16]
            P.red("dve", g4[:, 0:1], PB[b][0:4, 0:128], ALU.max, [bPB[b]], [bsSt])
            P.red("dve", g4[:, 1:2], PB[b][0:4, 128:256], ALU.min, [bPB[b]], [bsSt])
            P.ts("dve", g4[:, 2:3], g4[:, 0:1], 1.02, None, ALU.mult, None, [bsSt], [bsSt])
            P.ts("dve", g4[:, 4:5], g4[:, 0:1], 0.98, None, ALU.mult, None, [bsSt], [bsSt])
            P.tt("dve", g4[:, 3:4], g4[:, 2:3], g4[:, 4:5], ALU.max, [bsSt], [bsSt])
            P.ts("dve", g4[:, 3:4], g4[:, 3:4], 1.0, None, ALU.add, None, [bsSt], [bsSt])
            P.ts("dve", g4[:, 2:3], g4[:, 1:2], 1.02, None, ALU.mult, None, [bsSt], [bsSt])
            P.ts("dve", g4[:, 4:5], g4[:, 1:2], 0.98, None, ALU.mult, None, [bsSt], [bsSt])
            P.tt("dve", g4[:, 5:6], g4[:, 2:3], g4[:, 4:5], ALU.min, [bsSt], [bsSt])
            P.ts("dve", g4[:, 5:6], g4[:, 5:6], -1.0, None, ALU.add, None, [bsSt], [bsSt])
            D8 = sX[0:4, 0:8]
            P.ts("dve", D8[:, 0:4], ident[0:4, 0:4], g4[:, 3:4], None, ALU.mult, None, [bsSt] + RC, [bsX])
            P.ts("dve", D8[:, 4:8], ident[0:4, 0:4], g4[:, 5:6], None, ALU.mult, None, [bsSt] + RC, [bsX])
            b = P.nextps()
            P.mm(PB[b][:, 0:8], ones[0:4, 0:128], D8, True, True, [bsX] + RC, [bPB[b]])
            hi_b = sSt[:, 16:20]; thr = sSt[:, 20:24]; step = sSt[:, 24:28]; cand = sSt[:, 28:32]; mt = sSt[:, 32:36]
            P.cp("dve", sSt[:, 16:24], PB[b][:, 0:8], [bPB[b]], [bsSt])
            P.tt("dve", step, hi_b, thr, ALU.subtract, [bsSt], [bsSt])
            P.ts("dve", step, step, 0.5, None, ALU.mult, None, [bsSt], [bsSt])
            IT3n = sIT[:, 0:516].rearrange("p (c t) -> p c t", t=4)
            G3n = sG[:, 0:516].rearrange("p (c t) -> p c t", t=4)
            G3 = sG[:, 0:516].rearrange("p (c t) -> p t c", t=4)
            for itn in range(20):
                P.tt("dve", cand, thr, step, ALU.add, [bsSt], [bsSt])
                P.tt("dve", G3n, IT3n, cand.unsqueeze(1).to_broadcast([128, 129, 4]), ALU.is_ge, [bsIT, bsSt], [bsG])
                P.red("dve", sSt[:, 36:40], G3, ALU.add, [bsG], [bsSt])
                P.cp("dve", sPcb[:, 0:4], sSt[:, 36:40], [bsSt], [bsPcb])
                b = P.nextps()
                P.mm(PB[b][:, 0:4], ones[:], sPcb[:, 0:4], True, True, [bsPcb] + RC, [bPB[b]])
                P.ts("dve", mt, PB[b][:, 0:4], float(TOPK), None, ALU.is_ge, None, [bPB[b]], [bsSt])
                P.tt("dve", mt, mt, step, ALU.mult, [bsSt], [bsSt])
                P.tt("dve", thr, thr, mt, ALU.add, [bsSt], [bsSt])
                P.ts("dve", step, step, 0.5, None, ALU.mult, None, [bsSt], [bsSt])
            M3n = sMask[:, 0:516].rearrange("p (c t) -> p c t", t=4)
            P.tt("dve", M3n, IT3n, thr.unsqueeze(1).to_broadcast([128, 129, 4]), ALU.is_ge, [bsIT, bsSt], [bsMask])
            bO = [P.nextps(), None]
            P.pinned.add(bO[0])
            bO[1] = P.nextps()
            P.pinned.add(bO[1])
            bDn = P.nextps()
            P.pinned.add(bDn)
            Kb3 = sKbf.rearrange("p (r c) -> p r c", c=256)
            Vb3 = sVbf.rearrange("p (r c) -> p r c", c=256)
            PT3 = sPT.rearrange("p (r c) -> p r c", c=32)

            def qrhs(g):
                return QT[:, ti, g * 4:(g + 1) * 4, 0:4]

            def softmax_pv(bL, nr, mcol0, vsrc_of, bv, first, last):
                n = nr * 32
                P.act(sE[:, 0:n], PB[bL][:, 0:n], AF.Exp, [bPB[bL]], [bsE], scale=SC)
                P.tt("dve", sPT[:, 0:n].rearrange("p (r a t) -> p r a t", a=8, t=4),
                     sE[:, 0:n].rearrange("p (r a t) -> p r a t", a=8, t=4),
                     sMask[:, mcol0:mcol0 + nr * 4].rearrange("p (r t) -> p r t", t=4).unsqueeze(2).to_broadcast([128, nr, 8, 4]),
                     ALU.mult, [bsE, bsMask], [bsPT])
                for r8 in range(nr):
                    st_ = first and r8 == 0
                    sp_ = last and r8 == nr - 1
                    for g in range(2):
                        P.mm(PB[bO[g]][:, 0:16], vsrc_of(r8, g), PT3[:, r8, g * 16:(g + 1) * 16], st_, sp_, [bv, bsPT], [bPB[bO[g]]])
                    P.mm(PB[bDn][:, 0:32], ones[:], PT3[:, r8, :], st_, sp_, [bsPT] + RC, [bPB[bDn]])

            gather(Kst[0], bKst[0], pool_k2, 0, extra_w=bKT)
            gather(Vst[0], bVst[0], pool_v2, 0, extra_w=bVb)
            H2bf = H[:, 2, :].bitcast(BF16)
            Kbf2 = [sKbf, H2bf[:, 0:2048]]
            Vbf2 = [sVbf, H2bf[:, 2048:4096]]
            bKV2 = [bH[1], bH[2]]
            KTr4 = [MR[:, 4896 + 512 * i:5408 + 512 * i] for i in range(4)]
            bKTr4 = [bsKTr[0], bsKTr[1], bsikTr[0], bsikTr[1]]
            for kb in range(16):
                if kb + 1 < 16:
                    gather(Kst[(kb + 1) % 2], bKst[(kb + 1) % 2], pool_k2, kb + 1, extra_w=bKT if kb == 0 else ())
                    gather(Vst[(kb + 1) % 2], bVst[(kb + 1) % 2], pool_v2, kb + 1, extra_w=bVb if kb == 0 else ())
                pz = kb % 2
                P.cp("dve", Kbf2[pz], Kst[kb % 2], [bKst[kb % 2]], [bKV2[pz]])
                P.cp("act", Vbf2[pz], Vst[kb % 2], [bVst[kb % 2]], [bKV2[pz]])
                Kb3 = Kbf2[pz].rearrange("p (r c) -> p r c", c=256)
                Vb3 = Vbf2[pz].rearrange("p (r c) -> p r c", c=256)
                bL = P.nextps()
                P.pinned.add(bL)
                bts = []
                for q in range(4):
                    bt = P.nextps()
                    P.pinned.add(bt)
                    bts.append(bt)
                    for rr in range(2):
                        for g in range(2):
                            P.mm(PB[bt][:, (rr * 2 + g) * 128:(rr * 2 + g + 1) * 128], Kb3[:, q * 2 + rr, g * 128:(g + 1) * 128], ident,
                                 True, True, [bKV2[pz]] + RC, [bPB[bt]])
                for q in range(4):
                    P.cp("act" if q % 2 == 0 else "dve", KTr4[q], PB[bts[q]][:], [bPB[bts[q]]], [bKTr4[q]])
                    P.pinned.discard(bts[q])
                for q in range(4):
                    for rr in range(2):
                        for g in range(2):
                            r8 = q * 2 + rr
                            P.mm(PB[bL][:, r8 * 32 + g * 16:r8 * 32 + g * 16 + 16], KTr4[q][:, (rr * 2 + g) * 128:(rr * 2 + g + 1) * 128],
                                 qrhs(g), True, True, [bKTr4[q], bQT[ti]], [bPB[bL]])
                softmax_pv(bL, 8, kb * 32, lambda r8, g, Vb3=Vb3: Vb3[:, r8, g * 128:(g + 1) * 128], bKV2[pz], kb == 0, False)
                P.pinned.discard(bL)
            bL = P.nextps()
            for g in range(2):
                P.mm(PB[bL][:, g * 16:(g + 1) * 16], sKTo[:, g * 128:(g + 1) * 128], qrhs(g), True, True, [bsKTo, bQT[ti]], [bPB[bL]])
            softmax_pv(bL, 1, 512, lambda r8, g: sVo[:, g * 128:(g + 1) * 128], bsVo, False, True)
            P.S.op("dve", lambda e: e.reciprocal(out=sSt[:, 0:32], in_=PB[bDn][:, 0:32]), [bPB[bDn]], [bsSt])
            for g in range(2):
                P.tt("dve", dsaoT[:, g * 4:(g + 1) * 4, ti * 128:ti * 128 + 4], PB[bO[g]][:, 0:16].rearrange("p (a t) -> p a t", t=4),
                     sSt[:, g * 16:(g + 1) * 16].rearrange("p (a t) -> p a t", t=4), ALU.mult, [bPB[bO[g]], bsSt], [bdsaoT[ti]])
            for bb in (bO[0], bO[1], bDn):
                P.pinned.discard(bb)

        def own_group(tiles_info, pre=None):
            T = len(tiles_info)
            tiles = list(range(T))
            N = T * 128

            def loads():
                if pre is not None:
                    pre()
                for ti, inf in enumerate(tiles_info):
                    S.dma("sp", H[:, ti, :], inf["x"], writes=[bH[ti]])
                    S.dma("sp", tab[ti][:], inf["tab"], writes=[btab[ti]])
                rmsnorm_T_multi([(H[:, ti, :], bH[ti], ti) for ti in tiles], G_MIX)
            steps = []
            for hg in range(2):
                def rq_step(slot, bslot, hg=hg):
                    if hg == 0:
                        loads()

                    def evac(ti, ps, bps):
                        i = P.ev_ctr % 2
                        P.ev_ctr += 1
                        P.cp("act", tmpA[i][:], ps[:, 0:512], [bps], [btmpA[i]])
                        rope("dve", tmpA[i][:].rearrange("p (h d) -> p h d", h=4), q4[:, ti, :].rearrange("p (h d) -> p h d", h=4),
                             tab[ti][:, 0:64], tab[ti][:, 64:128], 4, 64, 128, [btmpA[i], btab[ti]], [bq4[ti]])
                    proj_tiles(slot, bslot, 16, 512, tiles, u_lhs, u_bufs, evac)

                def rk_step(slot, bslot, hg=hg):
                    def evac(ti, ps, bps):
                        evac_rk(ps, bps, tab[ti], btab[ti], k4[:, ti, :], bk4[ti])
                    proj_tiles(slot, bslot, 16, 512, tiles, u_lhs, u_bufs, evac)

                def rv_step(slot, bslot, hg=hg):
                    def evac(ti, ps, bps):
                        P.cp(P.ev_eng(), v4[:, ti, :], ps[:, 0:512], [bps], [bv4[ti]])
                    proj_tiles(slot, bslot, 16, 512, tiles, u_lhs, u_bufs, evac)

                def rg_step(slot, bslot, hg=hg):
                    def evac(ti, ps, bps):
                        P.act(sg4[:, ti, :], ps[:, 0:512], AF.Silu, [bps], [bsg4[ti]])
                    proj_tiles(slot, bslot, 16, 512, tiles, u_lhs, u_bufs, evac)
                    prev_final = None
                    for ti, inf in enumerate(tiles_info):
                        fin_ = retention(ti, hg, inf["kind"] == "smp")
                        if prev_final is not None:
                            prev_final()
                        prev_final = fin_
                    defer(prev_final, 1)
                for c0, fn in ((C_RQ, rq_step), (C_RK, rk_step), (C_RV, rv_step), (C_RG, rg_step)):
                    steps.append(([wseg(w_in[:, c0 + hg * 512:c0 + hg * 512 + 512], 16, 0, 512)], fn))
            if self.stop_after == "ret":
                return steps
            for blk in range(2):
                def dq_step(slot, bslot, blk=blk):
                    def evac(ti, ps, bps):
                        i = P.ev_ctr % 2
                        P.ev_ctr += 1
                        P.cp("act", tmpA[i][:], ps[:, 0:512], [bps], [btmpA[i]])
                        rope("dve", tmpA[i][:].rearrange("p (h d) -> p h d", h=4), bfA[i][:].rearrange("p (h d) -> p h d", h=4),
                             tab[ti][:, 128:144], tab[ti][:, 144:160], 4, 16, 128, [btmpA[i], btab[ti]], [bbfA[i]])
                        transposes_to(bfA[i], bbfA[i], 4, lambda j: QT[:, ti, blk * 4 + j, :], lambda j: bQT[ti])
                    proj_tiles(slot, bslot, 16, 512, tiles, u_lhs, u_bufs, evac)
                steps.append(([wseg(w_in[:, C_DQ + blk * 512:C_DQ + blk * 512 + 512], 16, 0, 512)], dq_step))

            if self.stop_after == "dq":
                return steps

            def kv_step(slot, bslot):
                def evac(ti, ps, bps):
                    inf = tiles_info[ti]
                    if inf["kind"] == "own":
                        evac_dkdv(ps, bps, tab[ti], btab[ti], 24 + inf["it"], inf["k_out"], inf["v_out"], True)
                    else:
                        evac_dkdv(ps, bps, tab[ti], btab[ti], 32, inf["k_out"], inf["v_out"], False)
                        P.cp("dve", kbf, kvf[:, 0:256], [bkvf], [bkbf])
                        P.cp("dve", sVo, kvf[:, 256:512], [bkvf], [bsVo])
                        transposes_to(kbf, bkbf, 2, lambda j: sKTo[:, j * 128:(j + 1) * 128], lambda j: bsKTo)
                proj_tiles(slot, bslot, 16, 512, tiles, u_lhs, u_bufs, evac)
            steps.append(([wseg(w_in[:, C_DK:C_DK + 512], 16, 0, 512)], kv_step))
            if self.stop_after == "kv":
                return steps
            for blk in range(2):
                def iq_step(slot, bslot, blk=blk):
                    def evac(ti, ps, bps):
                        i = P.ev_ctr % 2
                        P.ev_ctr += 1
                        P.cp("act", tmpA[i][:], ps[:, 0:512], [bps], [btmpA[i]])
                        rope("dve", tmpA[i][:].rearrange("p (h d) -> p h d", h=8), bfA[i][:].rearrange("p (h d) -> p h d", h=8),
                             tab[ti][:, 160:168], tab[ti][:, 168:176], 8, 8, 64, [btmpA[i], btab[ti]], [bbfA[i]])
                        transposes_to(bfA[i], bbfA[i], 4, lambda j: iqT[:, ti, blk * 4 + j, :], lambda j: biqT[ti])
                    proj_tiles(slot, bslot, 16, 512, tiles, u_lhs, u_bufs, evac)
                steps.append(([wseg(w_in[:, C_IQ + blk * 512:C_IQ + blk * 512 + 512], 16, 0, 512)], iq_step))

            if self.stop_after == "iq":
                return steps

            def ikw_step(slot, bslot):
                def evac(ti, ps, bps):
                    inf = tiles_info[ti]
                    P.cp("dve", iwf[:, ti, :], ps[:, 64:80], [bps], [biw[ti]])
                    if self.stop_after == "projA":
                        return
                    if inf["kind"] == "own":
                        evac_ik(ps, bps, tab[ti], btab[ti], 24 + inf["it"], inf["ik_out"], True)
                    else:
                        evac_ik(ps, bps, tab[ti], btab[ti], 32, inf["ik_out"], False)
                        P.cp("dve", ik2[:, 0:64], ikf[:], [bikf], [bik2])
                        P.cp("dve", ik2[:, 64:128], ikf[:], [bikf], [bik2])
                        transposes_to(ik2, bik2, 1, lambda j: sikTo, lambda j: bsikTo)
                proj_tiles(slot, bslot, 16, 80, tiles, u_lhs, u_bufs, evac)
                if self.stop_after in ("proj", "projA", "projB", "projC", "projD"):
                    return
                run_deferred(flush=True)
                if tiles_info[0]["kind"] == "own":
                    ctx = [None] * T
                    ctx[0] = dsa_idx(0, tiles_info[0]["it"], 0)
                    for ti in range(T):
                        dsa_bis(ctx[ti])
                        if ti + 1 < T:
                            ctx[ti + 1] = dsa_idx(ti + 1, tiles_info[ti + 1]["it"], (ti + 1) % 2)
                            dsa_att(ctx[ti])
                        else:
                            defer(lambda c=ctx[ti]: dsa_att(c), 3)
                else:
                    dsa_sample(0)
            steps.append(([wseg(w_in[:, C_IK:C_IK + 128], 16, 0, 128)], ikw_step))
            if self.stop_after in ("proj", "projA", "projB", "projC", "projD"):
                return steps
            for cb in range(4):
                def ga_step(slot, bslot, cb=cb):
                    def evac(ti, ps, bps):
                        P.act(sg4[:, ti, :], ps[:, 0:512], AF.Sigmoid, [bps], [bsg4[ti]])
                    proj_tiles(slot, bslot, 16, 512, tiles, u_lhs, u_bufs, evac)

                def ro_step(slot, bslot, cb=cb):
                    def evac(ti, ps, bps):
                        P.tt("dve", m1[:, ti, :], ps[:, 0:512], sg4[:, ti, :], ALU.mult, [bps, bsg4[ti]], [bm1[ti]])
                    proj_tiles(slot, bslot, 8, 512, tiles, lambda kc, ti: retyT[:, kc, ti * 128:(ti + 1) * 128],
                               lambda ti: [bretyT[ti]], evac)

                def gb_step(slot, bslot, cb=cb):
                    def evac(ti, ps, bps):
                        P.act(sg4[:, ti, :], ps[:, 0:512], AF.Sigmoid, [bps], [bsg4[ti]])
                    proj_tiles(slot, bslot, 16, 512, tiles, u_lhs, u_bufs, evac)

                def do_step(slot, bslot, cb=cb):
                    def evac(ti, ps, bps):
                        i = P.ev_ctr % 2
                        P.ev_ctr += 1
                        P.tt("dve", tmpA[i][:], ps[:, 0:512], sg4[:, ti, :], ALU.mult, [bps, bsg4[ti]], [btmpA[i]])
                        P.tt("dve", bfA[i][:], tmpA[i][:], m1[:, ti, :], ALU.add, [btmpA[i], bm1[ti]], [bbfA[i]])
                        transposes_to(bfA[i], bbfA[i], 4, lambda j: mrgT[:, cb * 4 + j, ti * 128:(ti + 1) * 128], lambda j: bmrgT[ti])
                    proj_tiles(slot, bslot, 8, 512, tiles, lambda kc, ti: dsaoT[:, kc, ti * 128:(ti + 1) * 128],
                               lambda ti: [bdsaoT[ti]], evac)
                steps.append(([wseg(w_in[:, C_GA + cb * 512:C_GA + cb * 512 + 512], 16, 0, 512)], ga_step))
                steps.append(([wseg(w_ro[:, cb * 512:cb * 512 + 512], 8, 0, 512)], ro_step))
                steps.append(([wseg(w_in[:, C_GB + cb * 512:C_GB + cb * 512 + 512], 16, 0, 512)], gb_step))
                steps.append(([wseg(w_do[:, cb * 512:cb * 512 + 512], 8, 0, 512)], do_step))
            for cb in range(4):
                def wo_step(slot, bslot, cb=cb):
                    if cb == 0:
                        for ti, inf in enumerate(tiles_info):
                            S.dma("sp", H[:, ti, :], inf["x"], writes=[bH[ti]])

                    def evac(ti, ps, bps):
                        hs = H[:, ti, cb * 512:(cb + 1) * 512]
                        P.tt("dve", hs, ps[:, 0:512], hs, ALU.add, [bps, bH[ti]], [bH[ti]])
                    proj_tiles(slot, bslot, 16, 512, tiles, lambda kc, ti: mrgT[:, kc, ti * 128:(ti + 1) * 128],
                               lambda ti: [bmrgT[ti]], evac)
                steps.append(([wseg(w_o[:, cb * 512:cb * 512 + 512], 16, 0, 512)], wo_step))
            smp_group = tiles_info[0]["kind"] == "smp"
            mkT_, mvb_, bmkT_, bmvb_ = (mkTs, mvbs, bmkTs, bmvbs) if smp_group else (mkT, mvb, bmkT, bmvb)

            def mq_step(slot, bslot):
                rmsnorm_T_multi([(H[:, ti, :], bH[ti], ti) for ti in tiles], G_MEM)
                for hd in range(4):
                    b = P.nextps()
                    for kc in range(16):
                        P.mm(PB[b][:, 0:N], slot[:, kc, hd * 128:(hd + 1) * 128], U[:, kc, 0:N], kc == 0, kc == 15,
                             [bU[t] for t in tiles] + bslot, [bPB[b]])
                    P.cp(P.ev_eng(), qmT[:, hd, 0:N], PB[b][:, 0:N], [bPB[b]], [bqmT])
                for hd in range(4):
                    for mc in range(2):
                        b = P.nextps()
                        P.mm(PB[b][:, 0:N], mkT_[:, hd, mc * 128:(mc + 1) * 128], qmT[:, hd, 0:N], True, True, [bmkT_, bqmT], [bPB[b]])
                        P.act(PTm[:, mc, 0:N], PB[b][:, 0:N], AF.Exp, [bPB[b]], [bPTm], scale=128.0 ** -0.5)
                    bo = P.nextps()
                    bd = P.nextps()
                    for mc in range(2):
                        P.mm(PB[bo][:, 0:N], mvb_[:, mc, hd * 128:(hd + 1) * 128], PTm[:, mc, 0:N], mc == 0, mc == 1, [bmvb_, bPTm], [bPB[bo]])
                    for mc in range(2):
                        P.mm(PB[bd][:, 0:N], ones[:], PTm[:, mc, 0:N], mc == 0, mc == 1, [bPTm] + RC, [bPB[bd]])
                    P.S.op("dve", lambda e, bd=bd: e.reciprocal(out=recf2[:, 0:N], in_=PB[bd][:, 0:N]), [bPB[bd]], [brec2])
                    P.tt("dve", omT[:, hd, 0:N], PB[bo][:, 0:N], recf2[:, 0:N], ALU.mult, [bPB[bo], brec2], [bomT])
            steps.append(([wseg(w_mq, 16, 0, 512)], mq_step))
            for cb in range(4):
                def mo_step(slot, bslot, cb=cb):
                    def evac(ti, ps, bps):
                        hs = H[:, ti, cb * 512:(cb + 1) * 512]
                        P.tt("dve", hs, ps[:, 0:512], hs, ALU.add, [bps, bH[ti]], [bH[ti]])
                    proj_tiles(slot, bslot, 4, 512, tiles, lambda kc, ti: omT[:, kc, ti * 128:(ti + 1) * 128],
                               lambda ti: [bomT], evac)
                steps.append(([wseg(w_mo[:, cb * 512:cb * 512 + 512], 4, 0, 512)], mo_step))
            for half in range(2):
                for j in range(8):
                    def up_step(slot, bslot, half=half, j=j):
                        if half == 0 and j == 0:
                            rmsnorm_T_multi([(H[:, ti, :], bH[ti], ti) for ti in tiles], G_MLP)
                        for fb in range(4):
                            b = P.nextps()
                            for kc in range(16):
                                P.mm(PB[b][:, 0:N], slot[:, kc, fb * 128:(fb + 1) * 128], U[:, kc, 0:N], kc == 0, kc == 15,
                                     [bU[t] for t in tiles] + bslot, [bPB[b]])
                            i = P.ev_ctr % 2
                            P.ev_ctr += 1
                            P.act(tmpA[i][:, 0:N], PB[b][:, 0:N], AF.Relu, [bPB[b]], [btmpA[i]])
                            P.tt("dve", aT[:, j * 4 + fb, 0:N], tmpA[i][:, 0:N], tmpA[i][:, 0:N], ALU.mult, [btmpA[i]], [baT])
                    c0 = half * 4096 + j * 512
                    steps.append(([wseg(w_up[:, c0:c0 + 512], 16, 0, 512)], up_step))
                for cb in range(4):
                    for qq in range(2):
                        def dn_step(slot, bslot, half=half, cb=cb, qq=qq):
                            if qq == 0:
                                P._acc = []
                                for ti in tiles:
                                    b = P.nextps()
                                    P.pinned.add(b)
                                    P._acc.append(b)
                            for ti in tiles:
                                b = P._acc[ti]
                                for kc in range(16):
                                    P.mm(PB[b][:], aT[:, qq * 16 + kc, ti * 128:(ti + 1) * 128], slot[:, kc, :],
                                         qq == 0 and kc == 0, qq == 1 and kc == 15, [baT] + bslot, [bPB[b]])
                            if qq == 1:
                                for ti in tiles:
                                    b = P._acc[ti]
                                    hs = H[:, ti, cb * 512:(cb + 1) * 512]
                                    P.tt("dve", hs, PB[b][:], hs, ALU.add, [bPB[b], bH[ti]], [bH[ti]])
                                    P.pinned.discard(b)
                        r0 = half * 4096 + qq * 2048
                        steps.append(([wseg(w_dn[r0:r0 + 2048, cb * 512:cb * 512 + 512], 16, 0, 512)], dn_step))

            def fin_step(slot, bslot):
                S.dma("sp", gfin, gfin_d, writes=[baT])
                for ti, inf in enumerate(tiles_info):
                    stt_, bst = new_stat()
                    P.memset("dve", stt_[:, 0:1], 0.0, [bst])
                    P.act(xn4[ti], H[:, ti, :], AF.Square, [bH[ti], bst], [bxn4[ti], bst], accum=stt_[:, 0:1])
                    P.rsqrt(stt_[:, 1:2], stt_[:, 0:1], 1.0 / D, bst)
                    P.stt("dve", H[:, ti, :], H[:, ti, :], stt_[:, 1:2], gfin, ALU.mult, ALU.mult, [bH[ti], bst, baT], [bH[ti]])
                    S.dma("sp", inf["y_out"], H[:, ti, :], reads=[bH[ti]])
            steps.append(([], fin_step))
            return steps

        steps = []
        steps += phase_mem()
        if self.stop_after == "mem":
            run_steps(steps)
            S.emit_all()
            return nc
        for g in range(self.n_pre_groups):
            steps += prefix_group(g, g == self.n_pre_groups - 1)

        def zero_init():
            for hg in range(2):
                P.memset("dve", S32[:, hg * 4:(hg + 1) * 4, :], 0.0, [bS32[hg]])
                P.memset("dve", Sbf[:, hg * 4:(hg + 1) * 4, :], 0.0, [bSbf[hg]])
            P.memset("dve", KT[:], 0.0, bKT)
            P.memset("dve", Vb[:], 0.0, bVb)
            P.memset("dve", ikT[:], 0.0, bikT)
        for og in range(self.n_own_groups):
            infos = []
            for t in range(4):
                it = og * 4 + t
                r0 = it * 128
                infos.append(dict(kind="own", it=it, x=x_own[r0:r0 + 128, :], tab=tab_own[r0:r0 + 128, :],
                                  k_out=k_own[r0:r0 + 128, :], v_out=v_own[r0:r0 + 128, :], ik_out=ik_own[r0:r0 + 128, :],
                                  y_out=y_own[r0:r0 + 128, :]))
            steps += own_group(infos, pre=zero_init if (og == 0 and self.n_pre_groups == 0) else None)

        def st_out_step(slot, bslot):
            for hg in range(2):
                S.dma("sp", st_own[hg * 4:(hg + 1) * 4].rearrange("h d e -> d h e"), S32[:, hg * 4:(hg + 1) * 4, :], reads=[bS32[hg]])
        if self.n_own_groups > 0:
            steps.append(([], st_out_step))
        if self.do_sample:
            def smp_pre():
                for hg in range(2):
                    S.dma("sp", S32s[:, hg * 4:(hg + 1) * 4, :], st_in[hg * 4:(hg + 1) * 4].rearrange("h d e -> d h e"), writes=[bS32s[hg]])
                    P.cp("dve", Sbfs[:, hg * 4:(hg + 1) * 4, :], S32s[:, hg * 4:(hg + 1) * 4, :], [bS32s[hg]], [bSbfs[hg]])
                load_sample_mem()
                smp_index_prep()
            infos = [dict(kind="smp", it=0, x=x_smp[:, :], tab=tab_smp[:, :], k_out=k_smp[:, :], v_out=v_smp[:, :],
                          ik_out=ik_smp[:, :], y_out=y_smp[:, :])]
            steps += own_group(infos, pre=smp_pre)

            def st_out_s(slot, bslot):
                for hg in range(2):
                    S.dma("sp", st_smp[hg * 4:(hg + 1) * 4].rearrange("h d e -> d h e"), S32s[:, hg * 4:(hg + 1) * 4, :], reads=[bS32s[hg]])
            steps.append(([], st_out_s))
        run_steps(steps)
        S.emit_all()
        return nc


def _rope_tab(pos, n_half, theta):
    inv = (np.float32(theta) ** (-(np.arange(n_half, dtype=np.float32) / np.float32(n_half)))).astype(np.float32)
    ang = pos.astype(np.float32)[:, None] * inv[None, :]
    return np.cos(ang).astype(np.float32), np.sin(ang).astype(np.float32)


def _gammas():
    return np.log1p(-np.exp2(-5.0 - np.arange(8, dtype=np.float64)))


def _tab(pos, kdec=None):
    n = pos.shape[0]
    t = np.zeros((n, 192), np.float32)
    c, s = _rope_tab(pos, 64, 10000.0)
    t[:, 0:64], t[:, 64:128] = c, s
    c, s = _rope_tab(pos, 16, 500000.0)
    t[:, 128:144], t[:, 144:160] = c, s
    c, s = _rope_tab(pos, 8, 500000.0)
    t[:, 160:168], t[:, 168:176] = c, s
    if kdec is not None:
        t[:, 176:184] = kdec
    return t


def _consts(p):
    lg = _gammas()
    c = np.zeros((128, 64), np.float32)
    cb = np.zeros((128, 2304), np.float32)
    cb[:, 0:128] = np.eye(128, dtype=np.float32)
    j = np.arange(128)
    cb[:, 128:256] = np.where(j[None, :] <= j[:, None], 0.0, NEG)
    diff = (j[None, :] - j[:, None]).astype(np.float64)
    for h in range(8):
        cb[:, 256 + h * 128:256 + (h + 1) * 128] = np.where(diff >= 0, np.exp(lg[h] * np.maximum(diff, 0.0)), 0.0)
        qd = np.exp(lg[h] * (j + 1.0))
        cb[:, 1280 + h * 128:1280 + (h + 1) * 128] = np.diag(qd)
        c[:, 0 + h] = qd
        c[:, 8 + h] = np.exp(lg[h] * (127.0 - j))
        c[:, 16 + h] = np.exp(lg[h] * (3.0 - j))
        c[:, 24 + h] = np.exp(lg[h] * 128.0)
        c[:, 32 + h] = np.exp(lg[h] * 4.0)
    for v in range(3):
        c[:, 40 + v] = 0.0 if v < p else NEG
    for t in range(4):
        c[:, 44 + t] = np.where(j <= t, 0.0, NEG)
    return c, cb


_PROG_CACHE = {}


def _get_prog(**kw):
    key = tuple(sorted(kw.items()))
    if key not in _PROG_CACHE:
        _PROG_CACHE[key] = Prog(**kw).build()
    return _PROG_CACHE[key]


def make_in_maps(inp, cores=range(8)):
    f = lambda a: np.ascontiguousarray(np.asarray(a, dtype=np.float32))
    xp = f(inp["x_prompt"]); xs = f(inp["x_sample"]); memp = f(inp["mem_prompt"])
    lg = _gammas()
    gvec = np.zeros((128, 96), np.float32)
    for off, name in ((0, "g_mix"), (16, "g_mem"), (32, "g_memkv"), (48, "g_mlp")):
        gvec[:, off:off + 16] = f(inp[name])[0].reshape(16, 128).T
    gvec[:, 64:72] = f(inp["gn_ret"])[0].reshape(8, 128).T
    gfin = np.ascontiguousarray(np.broadcast_to(f(inp["g_final"])[None, :], (128, D)))
    shared = dict(
        cbf=_consts(0)[1], gvec=gvec, gfin=gfin,
        w_in=f(inp["w_in"])[0], w_ret_out=f(inp["w_ret_out"])[0], w_dsa_out=f(inp["w_dsa_out"])[0], w_o=f(inp["w_o"])[0],
        w_mq=f(inp["w_mq"])[0], w_mk=f(inp["w_mk"])[0], w_mv=f(inp["w_mv"])[0], w_mo=f(inp["w_mo"])[0],
        w_up=f(inp["w_up"])[0], w_down=f(inp["w_down"])[0],
    )
    if USE_POOLS:
        shared.update(pool_k=f(inp["cache_k"])[0].reshape(1280 * 16, 2048), pool_v=f(inp["cache_v"])[0].reshape(1280 * 16, 2048),
                      pool_ik=f(inp["cache_idx_k"])[0].reshape(1280 * 8, 1024))
    pt = np.asarray(inp["page_table"]).astype(np.int32)
    maps = []
    for c in cores:
        b, p = c // 4, c % 4
        m = dict(shared)
        m["x_own"] = np.ascontiguousarray(xp[b, 1024 * p:1024 * (p + 1)])
        xpre = np.zeros((3072, D), np.float32)
        xpre[:1024 * p] = xp[b, :1024 * p]
        m["x_pre"] = xpre
        xsm = np.zeros((128, D), np.float32)
        xsm[:4] = xs[c]
        m["x_smp"] = xsm
        m["mem"] = np.ascontiguousarray(memp[b])
        pos_own = np.arange(1024 * p, 1024 * (p + 1))
        m["tab_own"] = _tab(pos_own)
        pos_pre = np.arange(3072)
        kd = np.zeros((3072, 8), np.float64)
        valid = pos_pre < 1024 * p
        ex = (1024 * p - 1 - pos_pre).astype(np.float64)
        for h in range(8):
            kd[:, h] = np.where(valid, np.exp(lg[h] * np.maximum(ex, 0.0)), 0.0)
        m["tab_pre"] = _tab(pos_pre, kd.astype(np.float32))
        pos_s = PAST + np.arange(128)
        m["tab_smp"] = _tab(pos_s)
        m["cst"] = _consts(p)[0]
        m["state_smp"] = np.ascontiguousarray(f(inp["state_ret"])[0, c])
        m["cmk"] = np.ascontiguousarray(f(inp["cache_mem_k"])[0, c].reshape(256, 512))
        m["cmv"] = np.ascontiguousarray(f(inp["cache_mem_v"])[0, c].reshape(256, 512))
        if USE_POOLS:
            m["ptab"] = np.ascontiguousarray(pt[c].reshape(128, 1))
        maps.append(m)
    return maps


def assemble(res):
    y_p = np.zeros((2, 4096, D), np.float32)
    y_s = np.zeros((8, 4, D), np.float32)
    st_p = np.zeros((1, 2, 8, 128, 128), np.float32)
    k_p = np.zeros((1, 2, 4096, 2, 128), np.float32)
    v_p = np.zeros((1, 2, 4096, 2, 128), np.float32)
    ik_p = np.zeros((1, 2, 4096, 64), np.float32)
    mk_p = np.zeros((1, 2, 256, 4, 128), np.float32)
    mv_p = np.zeros((1, 2, 256, 4, 128), np.float32)
    st_s = np.zeros((1, 8, 8, 128, 128), np.float32)
    k_s = np.zeros((1, 8, 4, 2, 128), np.float32)
    v_s = np.zeros((1, 8, 4, 2, 128), np.float32)
    ik_s = np.zeros((1, 8, 4, 64), np.float32)
    for c, r in enumerate(res):
        b, p = c // 4, c % 4
        sl = slice(1024 * p, 1024 * (p + 1))
        y_p[b, sl] = r["y_own"]
        k_p[0, b, sl] = r["k_own"].reshape(1024, 2, 128)
        v_p[0, b, sl] = r["v_own"].reshape(1024, 2, 128)
        ik_p[0, b, sl] = r["ik_own"]
        if p == 3:
            st_p[0, b] = r["st_own"]
        if p == 0:
            mk_p[0, b] = r["mk_o"].reshape(256, 4, 128)
            mv_p[0, b] = r["mv_o"].reshape(256, 4, 128)
        y_s[c] = r["y_smp"][:4]
        st_s[0, c] = r["st_smp"]
        k_s[0, c] = r["k_smp"][:4].reshape(4, 2, 128)
        v_s[0, c] = r["v_smp"][:4].reshape(4, 2, 128)
        ik_s[0, c] = r["ik_smp"][:4]
    return (y_p, y_s, st_p, k_p, v_p, ik_p, mk_p, mv_p, st_s, k_s, v_s, ik_s)


def kernel(**inputs):
    nc = _get_prog()
    in_maps = make_in_maps(inputs)
    res = run_bass_kernel_spmd(nc, in_maps, core_ids=list(range(8)))
    return assemble(res.results)
```

```python
import numpy as np
from contextlib import ExitStack
import concourse.bass as bass
import concourse.mybir as mybir
from concourse.bass_utils import run_bass_kernel_spmd

F32 = mybir.dt.float32
BF16 = mybir.dt.bfloat16
I32 = mybir.dt.int32
ALU = mybir.AluOpType
AF = mybir.ActivationFunctionType
AX = mybir.AxisListType

D = 2048
NIN = 10832
C_RQ, C_RK, C_RV, C_RG, C_DQ, C_DK, C_DV, C_IQ, C_IK, C_IW, C_GA, C_GB = (
    0, 1024, 2048, 3072, 4096, 5120, 5376, 5632, 6656, 6720, 6736, 8784)
EPS = 1e-6
PAST = 16384
NEG = -30000.0
NIT = 16
TOPK = 256
ENGS = ("pe", "act", "dve", "pool", "sp")
USE_POOLS = True


class Buf:
    __slots__ = ("name", "last_w", "readers", "excl", "last_any")

    def __init__(self, name="", excl=False):
        self.name = name
        self.last_w = None
        self.readers = []
        self.excl = excl
        self.last_any = None


class Op:
    __slots__ = ("eng", "emit", "deps", "is_dma", "sem", "val", "signal")

    def __init__(self, eng, emit, is_dma):
        self.eng = eng
        self.emit = emit
        self.deps = []
        self.is_dma = is_dma
        self.sem = None
        self.val = None
        self.signal = is_dma


class Sched:
    def __init__(self, nc, n_dma_sems=24):
        self.nc = nc
        self.ops = {e: [] for e in ENGS}
        self.n_dma_sems = n_dma_sems

    def op(self, eng, emit, reads=(), writes=(), dma=False):
        o = Op(eng, emit, dma)
        deps = []
        for b in reads:
            if b.last_w is not None:
                deps.append(b.last_w)
        for b in writes:
            if b.last_w is not None:
                deps.append(b.last_w)
            deps.extend(b.readers)
        for b in list(reads) + list(writes):
            if b.excl:
                if b.last_any is not None and b.last_any.eng != eng:
                    deps.append(b.last_any)
                b.last_any = o
        seen = set()
        for d in deps:
            if id(d) in seen or d is o:
                continue
            seen.add(id(d))
            if (not d.is_dma) and (not dma) and d.eng == "pe" and eng == "pe":
                continue
            o.deps.append(d)
            d.signal = True
        for b in reads:
            b.readers.append(o)
        for b in writes:
            b.last_w = o
            b.readers = []
        self.ops[eng].append(o)
        return o

    def dma(self, q, out, in_, reads=(), writes=()):
        return self.op(q, lambda e: e.dma_start(out=out, in_=in_), reads, writes, dma=True)

    def emit_all(self):
        nc = self.nc
        with ExitStack() as st:
            esem = {e: st.enter_context(nc.semaphore(f"s_{e}")) for e in ENGS}
            dsem = {q: [st.enter_context(nc.semaphore(f"d_{q}{i}")) for i in range(self.n_dma_sems)]
                    for q in ("sp", "pool", "act")}
            for e in ENGS:
                cnt = 0
                k = 0
                dcount = [0] * self.n_dma_sems
                for o in self.ops[e]:
                    if o.is_dma:
                        i = k % self.n_dma_sems
                        k += 1
                        dcount[i] += 16
                        o.sem = dsem[e][i]
                        o.val = dcount[i]
                    elif o.signal:
                        cnt += 1
                        o.sem = esem[e]
                        o.val = cnt
            block = st.enter_context(nc.Block())
            ops = self.ops

            def gen(e, engobj):
                clock = {}

                def wait(sem, val):
                    if clock.get(id(sem), 0) >= val:
                        return
                    engobj.wait_ge(sem, val)
                    clock[id(sem)] = val
                for o in ops[e]:
                    need = {}
                    for d in o.deps:
                        k = id(d.sem)
                        if k not in need or need[k][1] < d.val:
                            need[k] = (d.sem, d.val)
                    if o.is_dma and o.val > 16:
                        k = id(o.sem)
                        if k not in need or need[k][1] < o.val - 16:
                            need[k] = (o.sem, o.val - 16)
                    for sem, val in need.values():
                        wait(sem, val)
                    ins = o.emit(engobj)
                    if o.signal:
                        ins.then_inc(o.sem, 16 if o.is_dma else 1)
                if e == "sp":
                    for q in ("sp", "pool", "act"):
                        last = {}
                        for o in ops[q]:
                            if o.is_dma:
                                last[id(o.sem)] = (o.sem, o.val)
                        for sem, val in last.values():
                            wait(sem, val)

            @block.tensor
            def _(eng):
                gen("pe", eng)

            @block.scalar
            def _(eng):
                gen("act", eng)

            @block.vector
            def _(eng):
                gen("dve", eng)

            @block.gpsimd
            def _(eng):
                gen("pool", eng)

            @block.sync
            def _(eng):
                gen("sp", eng)


class Prog:
    def __init__(self, do_sample=True, n_own_groups=2, n_pre_groups=6, stop_after=None):
        self.nc = bass.Bass("TRN2", target_bir_lowering=False)
        self.S = Sched(self.nc)
        self.st = ExitStack()
        self.do_sample = do_sample
        self.n_own_groups = n_own_groups
        self.n_pre_groups = n_pre_groups
        self.stop_after = stop_after
        self.use_pools = False
        self.ps_ctr = 0
        self.pinned = set()
        self.ev_ctr = 0
        self.stat_ctr = 0

    def dram_in(self, name, shape, dt=F32):
        return self.nc.dram_tensor(name, list(shape), dt, kind="ExternalInput").ap()

    def dram_out(self, name, shape, dt=F32):
        return self.nc.dram_tensor(name, list(shape), dt, kind="ExternalOutput").ap()

    def sb(self, name, shape, dt):
        return self.st.enter_context(self.nc.sbuf_tensor(name, list(shape), dt))

    def mm(self, out, lhsT, rhs, st, sp, R, W, skip=False):
        if skip:
            self.S.op("pe", lambda e: e.matmul(out, lhsT=lhsT, rhs=rhs, start=st, stop=sp, skip_group_check=True), R, W)
        else:
            self.S.op("pe", lambda e: e.matmul(out, lhsT=lhsT, rhs=rhs, start=st, stop=sp), R, W)

    def act(self, out, in_, func, R, W, scale=1.0, bias=0.0, accum=None):
        if accum is None:
            self.S.op("act", lambda e: e.activation(out=out, in_=in_, func=func, bias=bias, scale=scale), R, W)
        else:
            self.S.op("act", lambda e: e.activation(out=out, in_=in_, func=func, bias=bias, scale=scale,
                                                    accum_out=accum), R, W)

    def tt(self, eng, out, in0, in1, op, R, W):
        self.S.op(eng, lambda e: e.tensor_tensor(out=out, in0=in0, in1=in1, op=op), R, W)

    def ts(self, eng, out, in0, s1, s2, op0, op1, R, W, accum=None):
        if op1 is None:
            self.S.op(eng, lambda e: e.tensor_scalar(out=out, in0=in0, scalar1=s1, scalar2=None, op0=op0), R, W)
        elif accum is None:
            self.S.op(eng, lambda e: e.tensor_scalar(out=out, in0=in0, scalar1=s1, scalar2=s2, op0=op0, op1=op1), R, W)
        else:
            self.S.op(eng, lambda e: e.tensor_scalar(out=out, in0=in0, scalar1=s1, scalar2=s2, op0=op0, op1=op1,
                                                     accum_out=accum), R, W)

    def rsqrt(self, out, in_, scale, b):
        self.ts("dve", out, in_, scale, EPS, ALU.mult, ALU.add, [b], [b])
        self.S.op("act", lambda e: e.activation(out=out, in_=out, func=AF.Sqrt), [b], [b])
        self.S.op("dve", lambda e: e.reciprocal(out=out, in_=out), [b], [b])

    def stt(self, eng, out, in0, scalar, in1, op0, op1, R, W):
        self.S.op(eng, lambda e: e.scalar_tensor_tensor(out=out, in0=in0, scalar=scalar, in1=in1, op0=op0, op1=op1), R, W)

    def cp(self, eng, out, in_, R, W):
        if eng == "act":
            self.act(out, in_, AF.Copy, R, W)
        else:
            self.S.op(eng, lambda e: e.tensor_copy(out=out, in_=in_), R, W)

    def red(self, eng, out, in_, op, R, W):
        self.S.op(eng, lambda e: e.tensor_reduce(out=out, in_=in_, axis=AX.X, op=op), R, W)

    def memset(self, eng, ap, v, W):
        self.S.op(eng, lambda e: e.memset(ap, v), [], W)

    def ev_eng(self):
        self.ev_ctr += 1
        return "act" if self.ev_ctr % 2 else "dve"

    def nextps(self):
        while True:
            b = self.ps_ctr % 8
            self.ps_ctr += 1
            if b not in self.pinned:
                return b

    def build(self):
        nc, S = self.nc, self.S
        P = self
        x_own = P.dram_in("x_own", [1024, D])
        x_pre = P.dram_in("x_pre", [3072, D])
        x_smp = P.dram_in("x_smp", [128, D])
        mem = P.dram_in("mem", [256, D])
        tab_own = P.dram_in("tab_own", [1024, 192])
        tab_pre = P.dram_in("tab_pre", [3072, 192])
        tab_smp = P.dram_in("tab_smp", [128, 192])
        cst = P.dram_in("cst", [128, 64])
        cbf_d = P.dram_in("cbf", [128, 2304])
        gvec = P.dram_in("gvec", [128, 96])
        gfin_d = P.dram_in("gfin", [128, D])
        w_in = P.dram_in("w_in", [D, NIN])
        w_ro = P.dram_in("w_ret_out", [1024, D])
        w_do = P.dram_in("w_dsa_out", [1024, D])
        w_o = P.dram_in("w_o", [D, D])
        w_mq = P.dram_in("w_mq", [D, 512])
        w_mk = P.dram_in("w_mk", [D, 512])
        w_mv = P.dram_in("w_mv", [D, 512])
        w_mo = P.dram_in("w_mo", [512, D])
        w_up = P.dram_in("w_up", [D, 8192])
        w_dn = P.dram_in("w_down", [8192, D])
        st_in = P.dram_in("state_smp", [8, 128, 128])
        cmk = P.dram_in("cmk", [256, 512])
        cmv = P.dram_in("cmv", [256, 512])
        pool_k2 = P.dram_in("pool_k", [1280 * 16, 2048])
        pool_v2 = P.dram_in("pool_v", [1280 * 16, 2048])
        pool_ik2 = P.dram_in("pool_ik", [1280 * 8, 1024])
        ptab = P.dram_in("ptab", [128, 1], I32)

        y_own = P.dram_out("y_own", [1024, D])
        y_smp = P.dram_out("y_smp", [128, D])
        st_own = P.dram_out("st_own", [8, 128, 128])
        st_smp = P.dram_out("st_smp", [8, 128, 128])
        k_own = P.dram_out("k_own", [1024, 256])
        v_own = P.dram_out("v_own", [1024, 256])
        ik_own = P.dram_out("ik_own", [1024, 64])
        k_smp = P.dram_out("k_smp", [128, 256])
        v_smp = P.dram_out("v_smp", [128, 256])
        ik_smp = P.dram_out("ik_smp", [128, 64])
        mk_o = P.dram_out("mk_o", [256, 512])
        mv_o = P.dram_out("mv_o", [256, 512])

        sb = P.sb
        KT = sb("KT", [128, 2, 4096], BF16); bKT = [Buf() for _ in range(33)]
        Vb = sb("Vb", [128, 32, 256], BF16); bVb = [Buf() for _ in range(33)]
        ikT = sb("ikT", [128, 4096], BF16); bikT = [Buf() for _ in range(33)]
        S32 = sb("S32", [128, 8, 128], F32); bS32 = [Buf(), Buf()]
        Sbf = sb("Sbf", [128, 8, 128], BF16); bSbf = [Buf(), Buf()]
        S32s, Sbfs, bS32s, bSbfs = S32, Sbf, bS32, bSbf
        cstf = sb("cstf_sb", [128, 64], F32); bcst = Buf()
        cbf = sb("cbf_sb", [128, 2304], BF16)
        ident = cbf[:, 0:128]
        tri = cbf[:, 128:256]
        intraT = cbf[:, 256:1280].rearrange("p (h i) -> p h i", h=8)
        diagq = cbf[:, 1280:2304].rearrange("p (h i) -> p h i", h=8)
        I4 = sb("I4", [128, 512], BF16)
        ones = sb("ones", [128, 128], BF16)
        gv = sb("gv", [128, 96], F32); bgv = Buf()
        kdec_own = cstf[:, 8:16]
        kdec_smp = cstf[:, 16:24]
        cdec_own = cstf[:, 24:32]
        cdec_smp = cstf[:, 32:40]
        slotb = cstf[:, 40:43]

        NW = 2
        Wr = [sb(f"Wr{i}", [128, 16, 512], BF16) for i in range(NW)]
        bW = [[Buf() for _ in range(4)] for _ in range(NW)]
        U = sb("U", [128, 16, 512], BF16); bU = [Buf() for _ in range(4)]
        H = sb("H", [128, 4, D], F32); bH = [Buf() for _ in range(4)]
        R = sb("R", [128, 32, 512], BF16)
        retyT = R[:, 0:8, :]; bretyT = [Buf() for _ in range(4)]
        dsaoT = R[:, 8:16, :]; bdsaoT = [Buf() for _ in range(4)]
        mrgT = R[:, 16:32, :]; bmrgT = [Buf() for _ in range(4)]
        aT = R; baT = Buf()
        gfin = R[:, 0:8, :].rearrange("p a b -> p (a b)").bitcast(F32)
        mbs = R[:, 16:24, :].rearrange("p a b -> p (a b)"); bmb = Buf()
        diagw = R[:, 24:28, :].rearrange("p a (b c) -> p (a b) c", c=128); bdiagw = Buf()
        rl = [R[:, 28 + i, :] for i in range(3)]; brl = [Buf() for _ in range(3)]
        xn4 = [R[:, 16 + 4 * i:20 + 4 * i, :].rearrange("p a b -> p (a b)") for i in range(4)]; bxn4 = [Buf() for _ in range(4)]
        stat = sb("stat", [128, 28, 8], F32); bstat = [Buf() for _ in range(28)]
        tab = [sb(f"tab{i}", [128, 192], F32) for i in range(4)]; btab = [Buf() for _ in range(4)]
        tabB = [sb(f"tabB{i}", [128, 192], F32) for i in range(4)]; btabB = [Buf() for _ in range(4)]
        U2 = R[:, 0:16, :]; bU2 = [Buf() for _ in range(4)]
        tmpA = [sb(f"tmpA{i}", [128, 512], F32) for i in range(2)]; btmpA = [Buf() for _ in range(2)]
        rtmp = [sb(f"rtmp{i}", [128, 512], F32) for i in range(2)]; brtmp = Buf()
        bfA = [sb(f"bfA{i}", [128, 512], BF16) for i in range(2)]; bbfA = [Buf() for _ in range(2)]
        PTb = [sb(f"PT{i}", [128, 512], BF16) for i in range(3)]; bPT = [Buf() for _ in range(3)]
        mgbuf = [bfA[0], bfA[1], PTb[1], PTb[2]]; bmgbuf = [bbfA[0], bbfA[1], bPT[1], bPT[2]]
        AR = sb("AR", [128, 9216], BF16)
        q4 = AR[:, 0:2048].rearrange("p (a b) -> p a b", a=4); bq4 = [Buf() for _ in range(4)]
        k4 = AR[:, 2048:4096].rearrange("p (a b) -> p a b", a=4); bk4 = [Buf() for _ in range(4)]
        v4 = AR[:, 4096:6144].rearrange("p (a b) -> p a b", a=4); bv4 = [Buf() for _ in range(4)]
        kd4 = AR[:, 6144:6656]; bkd4 = Buf()
        tr3 = [AR[:, 6656 + i * 512:6656 + (i + 1) * 512] for i in range(3)]; btr3 = [Buf() for _ in range(3)]
        smT = AR[:, 8192:8704]; bsmT = Buf()
        rpre = AR[:, 8704:9216]; brpre = Buf()
        QT = AR[:, 0:4096].rearrange("p (a h t) -> p a h t", a=4, h=8); bQT = [Buf() for _ in range(4)]
        iqT = AR[:, 4096:8192].rearrange("p (a h t) -> p a h t", a=4, h=8); biqT = [Buf() for _ in range(4)]
        kbf = AR[:, 8192:8448]; bkbf = Buf()
        kbf4 = [AR[:, 8192 + 256 * i:8448 + 256 * i] for i in range(4)]; bkbf4 = [Buf() for _ in range(4)]
        ik2 = sb("ik2", [128, 128], BF16); bik2 = Buf()
        qmT = AR[:, 0:2048].rearrange("p (a b) -> p a b", a=4); bqmT = Buf()
        PTm = AR[:, 2048:3072].rearrange("p (a b) -> p a b", a=2); bPTm = Buf()
        omT = AR[:, 3072:5120].rearrange("p (a b) -> p a b", a=4); bomT = Buf()
        recf2 = AR[:, 5120:6144].bitcast(F32); brec2 = Buf()
        sg4 = H[:, 0, :].bitcast(BF16)[:, 0:2048].rearrange("p (a b) -> p a b", a=4); bsg4 = [bH[0]] * 4
        iwf = sb("iwf", [128, 4, 16], F32); biw = [Buf() for _ in range(4)]
        ikf = sb("ikf", [128, 64], F32); bikf = Buf()
        mkT = sb("mkT", [128, 4, 256], BF16); bmkT = Buf()
        mvb = sb("mvb", [128, 2, 512], BF16); bmvb = Buf()
        mkTs, mvbs, bmkTs, bmvbs = mkT, mvb, bmkT, bmvb
        orow = H[:, 2, 0:512]; borow = bH[2]
        kvf = H[:, 2, 512:1024]; bkvf = bH[2]
        recf = H[:, 2, 1024:1536]; brec = bH[2]
        m1 = H[:, 3, :].rearrange("p (a b) -> p a b", a=4); bm1 = [bH[3]] * 4
        Isc = H[:, 0:2, :].rearrange("p a b -> p (a b)")
        bIsc = [bH[0], bH[1]]
        Isc2 = H[:, 2:4, :].rearrange("p a b -> p (a b)")
        bIsc2 = [bH[2], bH[3]]

        MR = R[:, 16:32, :].rearrange("p a b -> p (a b)")
        sIT = MR[:, 0:1032].bitcast(F32); bsIT = Buf()
        sMask = MR[:, 1040:1556]; bsMask = Buf()
        sG = MR[:, 1560:2076]; bsG = Buf()
        sRl = MR[:, 2080:3104].bitcast(F32); bsRl = Buf()
        sTmp = MR[:, 3104:4128].bitcast(F32); bsTmp = Buf()
        sE = MR[:, 4128:4640].bitcast(F32); bsE = Buf()
        sPT = MR[:, 4640:4896]; bsPT = Buf()
        sKTr = [MR[:, 4896:5408], MR[:, 5408:5920]]; bsKTr = [Buf(), Buf()]
        sikTr = [MR[:, 5920:6432], MR[:, 6432:6944]]; bsikTr = [Buf(), Buf()]
        sKTo = MR[:, 6944:7200]; bsKTo = Buf()
        sVo = MR[:, 7200:7456]; bsVo = Buf()
        sikTo = MR[:, 7456:7584]; bsikTo = Buf()
        sWb = MR[:, 7584:7712].bitcast(F32); bsWb = Buf()
        sSt = MR[:, 7712:7840].bitcast(F32); bsSt = Buf()
        sX = MR[:, 7840:7904]; bsX = Buf()
        sPcb = MR[:, 7904:7912]; bsPcb = Buf()
        sIdxF = MR[:, 7912:7976].bitcast(F32)
        sIdxI = MR[:, 7976:8040].bitcast(I32); bsIdx = Buf()
        sPt = MR[:, 8040:8042].bitcast(I32)
        Kst = [KT[:, i, :].bitcast(F32) for i in range(2)]; bKst = [Buf(), Buf()]
        Vst = [Vb[:, 16 * i:16 * (i + 1), :].rearrange("p a b -> p (a b)").bitcast(F32) for i in range(2)]; bVst = [Buf(), Buf()]
        Ist = [ikT[:, 2048 * i:2048 * (i + 1)].bitcast(F32) for i in range(2)]; bIst = [Buf(), Buf()]
        H1bf = H[:, 1, :].bitcast(BF16)
        sKbf = H1bf[:, 0:2048]; sVbf = H1bf[:, 2048:4096]
        sIbf = H[:, 3, 512:1536].bitcast(BF16)
        triT4 = cstf[:, 44:48]

        PB = [self.st.enter_context(nc.psum_tensor(f"pb{i}", [128, 512], F32)) for i in range(8)]
        bPB = [Buf(excl=True) for _ in range(8)]

        S.dma("sp", cstf[:], cst, writes=[bcst])
        S.dma("sp", gv[:], gvec, writes=[bgv])
        bc = Buf()
        S.dma("pool", cbf[:], cbf_d, writes=[bc])
        for j in range(4):
            P.cp("dve", I4[:, j * 128:(j + 1) * 128], ident, [bc], [bc])
        P.memset("dve", ones[:], 1.0, [bc])
        G_MIX, G_MEM, G_MEMKV, G_MLP, G_GN = 0, 16, 32, 48, 64
        RC = [bc, bcst, bgv]

        def new_stat():
            i = P.stat_ctr % 28
            P.stat_ctr += 1
            return stat[:, i, :], bstat[i]

        def rms_a(items):
            sts = []
            for (xap, bx, ti) in items:
                stt_, bst = new_stat()
                sts.append((stt_, bst))
                P.memset("dve", stt_[:, 0:1], 0.0, [bst])
                P.act(xn4[ti], xap, AF.Square, [bx, bst], [bxn4[ti], bst], accum=stt_[:, 0:1])
            for (xap, bx, ti), (stt_, bst) in zip(items, sts):
                P.ts("dve", stt_[:, 1:2], stt_[:, 0:1], 1.0 / D, EPS, ALU.mult, ALU.add, [bst], [bst])
            for (xap, bx, ti), (stt_, bst) in zip(items, sts):
                P.S.op("act", lambda e, o=stt_[:, 1:2]: e.activation(out=o, in_=o, func=AF.Sqrt), [bst], [bst])
            for (xap, bx, ti), (stt_, bst) in zip(items, sts):
                P.S.op("dve", lambda e, o=stt_[:, 1:2]: e.reciprocal(out=o, in_=o), [bst], [bst])
            for (xap, bx, ti), (stt_, bst) in zip(items, sts):
                P.ts("dve", xn4[ti], xap, stt_[:, 1:2], None, ALU.mult, None, [bx, bst], [bxn4[ti]])

        def rms_b(items, goff, Ud=None, bUd=None):
            Ud = U if Ud is None else Ud
            bUd = bU if bUd is None else bUd
            pend = []
            for (xap, bx, ti) in items:
                for q in range(4):
                    bk = P.nextps()
                    for j in range(4):
                        kc = q * 4 + j
                        P.mm(PB[bk][:, j * 128:(j + 1) * 128], xn4[ti][:, kc * 128:(kc + 1) * 128], ident, True, True,
                             [bxn4[ti]] + RC, [bPB[bk]])
                    pend.append((bk, q, ti))
                    if len(pend) >= 4:
                        evac_T(pend.pop(0), goff, Ud, bUd)
            while pend:
                evac_T(pend.pop(0), goff, Ud, bUd)

        def rmsnorm_T_multi(items, goff):
            rms_a(items)
            rms_b(items, goff)

        def evac_T(item, goff, Ud, bUd):
            bk, q, ti = item
            dst = Ud[:, 4 * q:4 * q + 4, ti * 128:(ti + 1) * 128]
            src = PB[bk][:].rearrange("p (j t) -> p j t", j=4)
            gb_ = gv[:, goff + 4 * q:goff + 4 * q + 4].unsqueeze(2).to_broadcast([128, 4, 128])
            P.tt("dve", dst, src, gb_, ALU.mult, [bPB[bk]] + RC, [bUd[ti]])

        def rmsnorm_T(xap, bx, goff, ti, ncols_tile=128):
            rmsnorm_T_multi([(xap, bx, ti)], goff)

        brt4 = [Buf() for _ in range(4)]

        def rope(eng, src3, dst3, cos2, sin2, Hh, half, dh, Rr, Ww, inplace=False):
            cb_ = cos2.unsqueeze(1).to_broadcast([128, Hh, half])
            sb_ = sin2.unsqueeze(1).to_broadcast([128, Hh, half])
            x1 = src3[:, :, 0:half]
            x2 = src3[:, :, half:2 * half]
            n = Hh * half
            tA = rtmp[0][:, 0:n].rearrange("p (h d) -> p h d", h=Hh)
            tB = rtmp[0][:, 256:256 + n].rearrange("p (h d) -> p h d", h=Hh)
            tC = rtmp[1][:, 0:n].rearrange("p (h d) -> p h d", h=Hh)
            tD = rtmp[1][:, 256:256 + n].rearrange("p (h d) -> p h d", h=Hh)
            Rr = list(Rr)
            P.tt(eng, tA, x1, cb_, ALU.mult, Rr, [brt4[0]])
            P.tt(eng, tB, x2, sb_, ALU.mult, Rr, [brt4[1]])
            P.tt(eng, tC, x1, sb_, ALU.mult, Rr, [brt4[2]])
            P.tt(eng, tD, x2, cb_, ALU.mult, Rr, [brt4[3]])
            P.tt(eng, dst3[:, :, 0:half], tA, tB, ALU.subtract, [brt4[0], brt4[1], brt4[2], brt4[3]] + Rr, Ww)
            P.tt(eng, dst3[:, :, half:2 * half], tC, tD, ALU.add, [brt4[2], brt4[3]] + Rr, Ww)
            if 2 * half < dh and not inplace:
                P.cp(eng, dst3[:, :, 2 * half:dh], src3[:, :, 2 * half:dh], Rr, Ww)

        def load_tab(src_rows):
            i = P.stat_ctr % 4
            P.stat_ctr += 1
            S.dma("sp", tab[i][:], src_rows, writes=[btab[i]])
            return tab[i], btab[i]

        def proj_tiles(slot, bslot, nkc, ncol, tiles, lhs_of, lhs_bufs_of, evac):
            banks = []
            for ti in tiles:
                b = P.nextps()
                P.pinned.add(b)
                banks.append(b)
            nq = (nkc + 3) // 4
            for q in range(nq):
                for i, ti in enumerate(tiles):
                    b = banks[i]
                    for kc in range(q * 4, min(nkc, q * 4 + 4)):
                        P.mm(PB[b][:, 0:ncol], lhs_of(kc, ti), slot[:, kc, 0:ncol], kc == 0, kc == nkc - 1,
                             lhs_bufs_of(ti) + [bslot[kc // 4]], [bPB[b]])
            for b in banks:
                P.pinned.discard(b)
            for i, ti in enumerate(tiles):
                evac(ti, PB[banks[i]], bPB[banks[i]])

        def u_lhs(kc, ti):
            return U[:, kc, ti * 128:(ti + 1) * 128]

        def u_bufs(ti):
            return [bU[ti]]

        def transposes_to(src_bf, bsrc, n, dst_of, bdst_of, scale_of=None, part=128):
            b = P.nextps()
            for j in range(n):
                P.mm(PB[b][:, j * 128:(j + 1) * 128], src_bf[:, j * 128:(j + 1) * 128], ident, True, True,
                     [bsrc] + RC, [bPB[b]])
            eng = P.ev_eng()
            for j in range(n):
                src = PB[b][:, j * 128:(j + 1) * 128]
                if scale_of is not None:
                    P.ts("dve", dst_of(j), src, scale_of(j), None, ALU.mult, None, [bPB[b]] + RC, [bdst_of(j)])
                else:
                    P.cp(eng, dst_of(j), src, [bPB[b]], [bdst_of(j)])

        def evac_dkdv(ps, bps, tb, btb, chunk, outk=None, outv=None, to_keys=True):
            P.cp("act", kvf, ps[:, 0:512], [bps], [bkvf])
            rope("dve", kvf[:, 0:256].rearrange("p (h d) -> p h d", h=2), kvf[:, 0:256].rearrange("p (h d) -> p h d", h=2),
                 tb[:, 128:144], tb[:, 144:160], 2, 16, 128, [bkvf, btb], [bkvf], inplace=True)
            if outk is not None:
                S.dma("sp", outk, kvf[:, 0:256], reads=[bkvf])
                S.dma("sp", outv, kvf[:, 256:512], reads=[bkvf])
            if to_keys:
                kb_ = kbf4[chunk % 4]
                bkb2 = bkbf4[chunk % 4]
                P.cp("dve", kb_, kvf[:, 0:256], [bkvf], [bkb2])
                P.cp("dve", Vb[:, chunk, :], kvf[:, 256:512], [bkvf], [bVb[chunk]])
                defer(lambda: transposes_to(kb_, bkb2, 2, lambda j: KT[:, j, chunk * 128:(chunk + 1) * 128], lambda j: bKT[chunk]), 2)

        def evac_ik(ps, bps, tb, btb, chunk, outik=None, to_keys=True):
            P.cp("act", ikf[:], ps[:, 0:64], [bps], [bikf])
            v3 = ikf[:].rearrange("p (h d) -> p h d", h=1)
            rope("dve", v3, v3, tb[:, 160:168], tb[:, 168:176], 1, 8, 64, [bikf, btb], [bikf], inplace=True)
            if self.stop_after == "projB":
                return
            if outik is not None:
                S.dma("sp", outik, ikf[:], reads=[bikf])
            if self.stop_after == "projC":
                return
            if to_keys:
                i2 = PTb[0][:, (chunk % 4) * 128:(chunk % 4 + 1) * 128]
                P.cp("dve", i2[:, 0:64], ikf[:], [bikf], [bPT[0]])
                P.cp("dve", i2[:, 64:128], ikf[:], [bikf], [bPT[0]])
                defer(lambda: transposes_to(i2, bPT[0], 1, lambda j: ikT[:, chunk * 128:(chunk + 1) * 128], lambda j: bikT[chunk]), 2)

        def evac_rk(ps, bps, tb, btb, dst_bf, bdst):
            i = P.ev_ctr % 2
            P.ev_ctr += 1
            P.act(tmpA[i][:], ps[:, 0:512], AF.Copy, [bps], [btmpA[i]], scale=128.0 ** -0.5)
            rope("dve", tmpA[i][:].rearrange("p (h d) -> p h d", h=4), dst_bf.rearrange("p (h d) -> p h d", h=4),
                 tb[:, 0:64], tb[:, 64:128], 4, 64, 128, [btmpA[i], btb], [bdst])

        def wseg(src2d, nkc, coff, ncol):
            return (src2d, nkc, coff, ncol)

        P.deferred = []

        def defer(fn, delay=1):
            P.deferred.append([delay, fn])

        def run_deferred(flush=False):
            keep = []
            todo = []
            for it in P.deferred:
                it[0] -= 1
                if flush or it[0] <= 0:
                    todo.append(it[1])
                else:
                    keep.append(it)
            P.deferred = keep
            for fn in todo:
                fn()

        wcache = {}

        def run_steps(steps):
            def issue(i):
                slot = i % NW
                for (src, nkc, coff, ncol) in steps[i][0]:
                    key = str(src)
                    first = key not in wcache
                    if first:
                        scr = nc.dram_tensor(f"wc{len(wcache)}", [128, nkc, ncol], BF16).ap()
                        wcache[key] = (scr, [Buf() for _ in range(4)])
                    scr, bscr = wcache[key]
                    for k0 in range(0, nkc, 4):
                        k1 = min(nkc, k0 + 4)
                        q = k0 // 4
                        dst = Wr[slot][:, k0:k1, coff:coff + ncol]
                        if first:
                            S.dma("pool", dst, src[k0 * 128:k1 * 128, :].rearrange("(k p) n -> p k n", p=128), writes=[bW[slot][q]])
                            S.dma("sp", scr[:, k0:k1, :], dst, reads=[bW[slot][q]], writes=[bscr[q]])
                        else:
                            S.dma("sp", dst, scr[:, k0:k1, :], reads=[bscr[q]], writes=[bW[slot][q]])
            if not steps:
                return
            issue(0)
            for i in range(len(steps)):
                if i + 1 < len(steps):
                    issue(i + 1)
                steps[i][1](Wr[i % NW], bW[i % NW])
                run_deferred()

        def phase_mem():
            steps = []

            def mk_step(slot, bslot):
                for mt in range(2):
                    S.dma("sp", H[:, mt, :], mem[mt * 128:(mt + 1) * 128, :], writes=[bH[mt]])
                rmsnorm_T_multi([(H[:, mt, :], bH[mt], mt) for mt in range(2)], G_MEMKV)
                def evac(ti, ps, bps):
                    i = ti % 2
                    P.cp("act", tmpA[i][:], ps[:, 0:512], [bps], [btmpA[i]])
                    S.dma("sp", mk_o[ti * 128:(ti + 1) * 128, :], tmpA[i][:], reads=[btmpA[i]])
                    P.cp("dve", bfA[i][:], tmpA[i][:], [btmpA[i]], [bbfA[i]])
                    transposes_to(bfA[i], bbfA[i], 4, lambda j: mkT[:, j, ti * 128:(ti + 1) * 128], lambda j: bmkT)
                proj_tiles(slot, bslot, 16, 512, [0, 1], u_lhs, u_bufs, evac)

            def mv_step(slot, bslot):
                def evac(ti, ps, bps):
                    i = ti % 2
                    P.cp("act", tmpA[i][:], ps[:, 0:512], [bps], [btmpA[i]])
                    S.dma("sp", mv_o[ti * 128:(ti + 1) * 128, :], tmpA[i][:], reads=[btmpA[i]])
                    P.cp("dve", mvb[:, ti, :], tmpA[i][:], [btmpA[i]], [bmvb])
                proj_tiles(slot, bslot, 16, 512, [0, 1], u_lhs, u_bufs, evac)
            steps.append(([wseg(w_mk, 16, 0, 512)], mk_step))
            steps.append(([wseg(w_mv, 16, 0, 512)], mv_step))
            return steps

        def load_sample_mem():
            for mt in range(2):
                i = mt
                S.dma("sp", tmpA[i][:], cmk[mt * 128:(mt + 1) * 128, :], writes=[btmpA[i]])
                P.cp("dve", bfA[i][:], tmpA[i][:], [btmpA[i]], [bbfA[i]])
                transposes_to(bfA[i], bbfA[i], 4, lambda j, mt=mt: mkTs[:, j, mt * 128:(mt + 1) * 128], lambda j: bmkTs)
            S.dma("pool", mvbs[:], cmv.rearrange("(c p) n -> p c n", p=128), writes=[bmvbs])

        pre_state_first = [True, True]

        pre_loads = {}

        def prefix_group(g, last):
            tiles = list(range(4))

            Ug, bUg = (U, bU) if g % 2 == 0 else (U2, bU2)
            tabs, btabs = (tab, btab) if g % 2 == 0 else (tabB, btabB)
            items = [(H[:, ti, :], bH[ti], ti) for ti in tiles]

            def loads_a():
                if g == 0:
                    P.pinned.add(6)
                    P.pinned.add(7)
                    P.memset("dve", PB[6][:], 0.0, [bPB[6]])
                    P.memset("dve", PB[7][:], 0.0, [bPB[7]])
                for ti in tiles:
                    row0 = (g * 4 + ti) * 128
                    S.dma("sp", H[:, ti, :], x_pre[row0:row0 + 128, :], writes=[bH[ti]])
                    S.dma("sp", tabs[ti][:], tab_pre[row0:row0 + 128, :], writes=[btabs[ti]])
                rms_a(items)

            def loads_b():
                rms_b(items, G_MIX, Ug, bUg)
            pre_loads[g] = (loads_a, loads_b)

            def ug_lhs(kc, ti):
                return Ug[:, kc, ti * 128:(ti + 1) * 128]

            def ug_bufs(ti):
                return [bUg[ti]]
            steps = []
            kbuf = [k4, AR[:, 0:2048].rearrange("p (a b) -> p a b", a=4)]
            vbuf = [v4, AR[:, 6144:8192].rearrange("p (a b) -> p a b", a=4)]
            bkb_ = [bk4, bq4]
            bvb_ = [bv4, [bkd4, btr3[0], btr3[1], btr3[2]]]

            def state_mm(hg):
                bank = 6 + hg
                for ti in tiles:
                    for h in range(4):
                        lastf = last and ti == 3
                        P.mm(PB[bank][:, h * 128:(h + 1) * 128], kbuf[hg][:, ti, h * 128:(h + 1) * 128],
                             vbuf[hg][:, ti, h * 128:(h + 1) * 128], False, lastf, [bkb_[hg][ti], bvb_[hg][ti]], [bPB[bank]], skip=True)

            for hg in range(2):
                def rk_step(slot, bslot, hg=hg):
                    if hg == 0 and g == 0:
                        loads_a()
                        loads_b()

                    def evac(ti, ps, bps):
                        evac_rk(ps, bps, tabs[ti], btabs[ti], kbuf[hg][:, ti, :], bkb_[hg][ti])
                        kd3 = kbuf[hg][:, ti, :].rearrange("p (h d) -> p h d", h=4)
                        dec = tabs[ti][:, 176 + hg * 4:176 + hg * 4 + 4].unsqueeze(2).to_broadcast([128, 4, 128])
                        P.tt("dve", kd3, kd3, dec, ALU.mult, [bkb_[hg][ti], btabs[ti]], [bkb_[hg][ti]])
                    proj_tiles(slot, bslot, 16, 512, tiles, ug_lhs, ug_bufs, evac)
                    if hg == 1:
                        state_mm(0)

                def rv_step(slot, bslot, hg=hg):
                    def evac(ti, ps, bps):
                        P.cp(P.ev_eng(), vbuf[hg][:, ti, :], ps[:, 0:512], [bps], [bvb_[hg][ti]])
                    proj_tiles(slot, bslot, 16, 512, tiles, ug_lhs, ug_bufs, evac)
                    if (g + 1) in pre_loads:
                        pre_loads[g + 1][hg]()
                steps.append(([wseg(w_in[:, C_RK + hg * 512:C_RK + hg * 512 + 512], 16, 0, 512)], rk_step))
                steps.append(([wseg(w_in[:, C_RV + hg * 512:C_RV + hg * 512 + 512], 16, 0, 512)], rv_step))

            def kv_step(slot, bslot):
                def evac(ti, ps, bps):
                    evac_dkdv(ps, bps, tabs[ti], btabs[ti], g * 4 + ti)
                proj_tiles(slot, bslot, 16, 512, tiles, ug_lhs, ug_bufs, evac)
                state_mm(1)

            def ik_step(slot, bslot):
                def evac(ti, ps, bps):
                    evac_ik(ps, bps, tabs[ti], btabs[ti], g * 4 + ti)
                proj_tiles(slot, bslot, 16, 64, tiles, ug_lhs, ug_bufs, evac)
                if last:
                    prefix_finish()
                    P.pinned.discard(6)
                    P.pinned.discard(7)
            steps.append(([wseg(w_in[:, C_DK:C_DK + 512], 16, 0, 512)], kv_step))
            steps.append(([wseg(w_in[:, C_IK:C_IK + 64], 16, 0, 64)], ik_step))
            return steps

        def prefix_finish():
            for hg in range(2):
                P.cp("dve", S32[:, hg * 4:(hg + 1) * 4, :].rearrange("p h e -> p (h e)"), PB[6 + hg][:], [bPB[6 + hg]], [bS32[hg]])
                P.cp("act", Sbf[:, hg * 4:(hg + 1) * 4, :].rearrange("p h e -> p (h e)"), PB[6 + hg][:], [bPB[6 + hg]], [bSbf[hg]])

        def retention(ti, hg, smp):
            S32_, Sbf_, bS32_, bSbf_ = (S32s, Sbfs, bS32s, bSbfs) if smp else (S32, Sbf, bS32, bSbf)
            kdec = kdec_smp if smp else kdec_own
            cdec = cdec_smp if smp else cdec_own
            qsrc = q4[:, ti, :]
            ksrc = k4[:, ti, :]
            vsrc = v4[:, ti, :]
            P.tt("dve", kd4.rearrange("p (h d) -> p h d", h=4), ksrc.rearrange("p (h d) -> p h d", h=4),
                 kdec[:, hg * 4:hg * 4 + 4].unsqueeze(2).to_broadcast([128, 4, 128]), ALU.mult, [bk4[ti]] + RC, [bkd4])
            for which, (src, bsrc, rhs_of) in enumerate((
                    (qsrc, bq4[ti], lambda h: ident),
                    (ksrc, bk4[ti], lambda h: ident),
                    (qsrc, bq4[ti], lambda h: diagq[:, hg * 4 + h, :]))):
                b = P.nextps()
                for h in range(4):
                    P.mm(PB[b][:, h * 128:(h + 1) * 128], src[:, h * 128:(h + 1) * 128], rhs_of(h), True, True,
                         [bsrc] + RC, [bPB[b]])
                P.cp(P.ev_eng(), tr3[which], PB[b][:], [bPB[b]], [btr3[which]])
            qT, kT, qdT = tr3
            b = P.nextps()
            for h in range(4):
                P.mm(PB[b][:, h * 128:(h + 1) * 128], kT[:, h * 128:(h + 1) * 128], qT[:, h * 128:(h + 1) * 128], True, True,
                     [btr3[0], btr3[1]], [bPB[b]])
            P.tt("dve", smT, PB[b][:], intraT[:, hg * 4:(hg + 1) * 4, :].rearrange("p h i -> p (h i)"), ALU.mult,
                 [bPB[b]] + RC, [bsmT])
            bo = P.nextps()
            for h in range(4):
                P.mm(PB[bo][:, h * 128:(h + 1) * 128], smT[:, h * 128:(h + 1) * 128], vsrc[:, h * 128:(h + 1) * 128], True, False,
                     [bsmT, bv4[ti]], [bPB[bo]])
                P.mm(PB[bo][:, h * 128:(h + 1) * 128], qdT[:, h * 128:(h + 1) * 128], Sbf_[:, hg * 4 + h, :], False, True,
                     [btr3[2], bSbf_[hg]], [bPB[bo]])
            P.cp("act", orow, PB[bo][:], [bPB[bo]], [borow])
            bs = P.nextps()
            for h in range(4):
                P.mm(PB[bs][:, h * 128:(h + 1) * 128], kd4[:, h * 128:(h + 1) * 128], vsrc[:, h * 128:(h + 1) * 128], True, True,
                     [bkd4, bv4[ti]], [bPB[bs]])
            Sv = S32_[:, hg * 4:(hg + 1) * 4, :]
            P.tt("dve", Sv, Sv, cdec[:, hg * 4:hg * 4 + 4].unsqueeze(2).to_broadcast([128, 4, 128]), ALU.mult, [bS32_[hg]] + RC, [bS32_[hg]])
            P.tt("dve", Sv, Sv, PB[bs][:].rearrange("p (h e) -> p h e", h=4), ALU.add, [bPB[bs], bS32_[hg]], [bS32_[hg]])
            P.cp("act", Sbf_[:, hg * 4:(hg + 1) * 4, :], S32_[:, hg * 4:(hg + 1) * 4, :], [bS32_[hg]], [bSbf_[hg]])
            stt_, bst = new_stat()
            o3 = orow.rearrange("p (h e) -> p h e", h=4)
            P.red("dve", stt_[:, 0:4], o3, ALU.add, [borow], [bst])
            sq = tmpA[1]
            P.tt("dve", sq[:], orow, orow, ALU.mult, [borow], [btmpA[1]])
            P.red("dve", stt_[:, 4:8], sq[:].rearrange("p (h e) -> p h e", h=4), ALU.add, [btmpA[1]], [bst])
            stt2, bst2 = new_stat()
            P.ts("dve", stt2[:, 0:4], stt_[:, 0:4], 1.0 / 128, None, ALU.mult, None, [bst], [bst2])
            P.tt("dve", stt2[:, 4:8], stt2[:, 0:4], stt2[:, 0:4], ALU.mult, [bst2], [bst2])
            P.stt("dve", stt2[:, 4:8], stt_[:, 4:8], 1.0 / 128, stt2[:, 4:8], ALU.mult, ALU.subtract, [bst, bst2], [bst2])
            P.rsqrt(stt2[:, 4:8], stt2[:, 4:8], 1.0, bst2)
            P.tt("dve", o3, o3, stt2[:, 0:4].unsqueeze(2).to_broadcast([128, 4, 128]), ALU.subtract, [borow, bst2], [borow])
            P.tt("dve", o3, o3, stt2[:, 4:8].unsqueeze(2).to_broadcast([128, 4, 128]), ALU.mult, [borow, bst2], [borow])
            rp, brp = bfA[ti % 2], bbfA[ti % 2]
            P.tt("dve", rp[:], orow, sg4[:, ti, :], ALU.mult, [borow, bsg4[ti]], [brp])

            def final():
                transposes_to(rp, brp, 4, lambda j: retyT[:, hg * 4 + j, ti * 128:(ti + 1) * 128], lambda j: bretyT[ti],
                              scale_of=lambda j: gv[:, G_GN + hg * 4 + j:G_GN + hg * 4 + j + 1])
            return final

        def dsa_idx(ti, it, par):
            Isc_, bIsc_ = (Isc, bIsc) if par == 0 else (Isc2, bIsc2)
            nkc = 24 + it + 1
            Sk = nkc * 128
            nblk = (Sk + 511) // 512
            keyb = [bKT[c] for c in range(nkc)]
            for h in range(16):
                P.act(diagw[:, h, :], ident, AF.Copy, [biw[ti]] + RC, [bdiagw], scale=iwf[:, ti, h:h + 1])
            stt_, bst = new_stat()
            stt2, bst2 = new_stat()
            bIs = [P.nextps(), None]
            P.pinned.add(bIs[0])
            bIs[1] = P.nextps()
            P.pinned.add(bIs[1])
            items = [(kb, h) for kb in range(nblk) for h in range(16)]

            def stage_a(kb, h, i):
                c0 = kb * 512
                n = min(512, Sk - c0)
                b = P.nextps()
                hp = (h % 2) * 64
                P.mm(PB[b][:, 0:n], iqT[hp:hp + 64, ti, h // 2, :], ikT[hp:hp + 64, c0:c0 + n], True, True,
                     [biqT[ti]] + [bikT[c] for c in range(c0 // 128, (c0 + n) // 128)], [bPB[b]])
                r = i % 3
                P.act(rl[r][:, 0:n], PB[b][:, 0:n], AF.Relu, [bPB[b]], [brl[r]])

            def stage_b(kb, h, i):
                c0 = kb * 512
                n = min(512, Sk - c0)
                bI = bIs[kb % 2]
                r = i % 3
                P.mm(PB[bI][:, 0:n], diagw[:, h, :], rl[r][:, 0:n], h == 0, h == 15, [bdiagw, brl[r]], [bPB[bI]])
                if h == 15:
                    P.cp("act", Isc_[:, c0:c0 + n], PB[bI][:, 0:n], [bPB[bI]], bIsc_)
                    P.red("dve", stt_[:, kb:kb + 1], Isc_[:, c0:c0 + n], ALU.max, bIsc_, [bst])
                    P.red("dve", stt2[:, kb:kb + 1], Isc_[:, c0:c0 + n], ALU.min, bIsc_, [bst2])
                    if c0 < 3072:
                        P.ts("dve", Isc_[:, c0:c0 + n], Isc_[:, c0:c0 + n], slotb[:, c0 // 1024:c0 // 1024 + 1], None, ALU.add, None, bIsc_ + RC, bIsc_)
                    else:
                        d0 = 3072 + it * 128
                        if d0 >= c0 and d0 < c0 + n:
                            P.tt("dve", Isc_[:, d0:d0 + 128], Isc_[:, d0:d0 + 128], tri, ALU.add, bIsc_ + RC, bIsc_)
            LAG = 2
            for i, (kb, h) in enumerate(items):
                stage_a(kb, h, i)
                if i >= LAG:
                    stage_b(items[i - LAG][0], items[i - LAG][1], i - LAG)
            for i in range(max(0, len(items) - LAG), len(items)):
                stage_b(items[i][0], items[i][1], i)
            P.pinned.discard(bIs[0])
            P.pinned.discard(bIs[1])
            return dict(ti=ti, it=it, nkc=nkc, Sk=Sk, nblk=nblk, Isc=Isc_, bIsc=bIsc_, stt=stt_, bst=bst, stt2=stt2, bst2=bst2)

        def dsa_bis(c):
            ti, it, nkc, Sk, nblk = c["ti"], c["it"], c["nkc"], c["Sk"], c["nblk"]
            Isc_, bIsc_, stt_, bst, stt2, bst2 = c["Isc"], c["bIsc"], c["stt"], c["bst"], c["stt2"], c["bst2"]
            st3, bst3 = new_stat()
            P.red("dve", st3[:, 0:1], stt_[:, 0:nblk], ALU.max, [bst], [bst3])
            P.red("dve", st3[:, 1:2], stt2[:, 0:nblk], ALU.min, [bst2], [bst3])
            P.stt("dve", st3[:, 2:3], st3[:, 0:1], 1.0, st3[:, 1:2], ALU.add, ALU.subtract, [bst3], [bst3])
            P.ts("dve", st3[:, 2:3], st3[:, 2:3], 0.5, None, ALU.mult, None, [bst3], [bst3])
            cnts, bcn = new_stat()
            cnts2, bcn2 = new_stat()
            cnts3, bcn3 = new_stat()
            P.memset("dve", cnts[:], 0.0, [bcn])
            P.memset("dve", cnts2[:], 0.0, [bcn2])
            P.memset("dve", cnts3[:], 0.0, [bcn3])
            for itn in range(NIT):
                cc, bcc = (cnts, bcn) if itn < 8 else ((cnts2, bcn2) if itn < 16 else (cnts3, bcn3))
                col = itn % 8
                P.tt("dve", st3[:, 3:4], st3[:, 1:2], st3[:, 2:3], ALU.add, [bst3], [bst3])
                P.ts("dve", mbs[:, 0:Sk], Isc_[:, 0:Sk], st3[:, 3:4], 0.0, ALU.is_ge, ALU.add, bIsc_ + [bst3, bcc], [bmb, bcc],
                     accum=cc[:, col:col + 1])
                P.ts("dve", st3[:, 4:5], cc[:, col:col + 1], float(TOPK), st3[:, 2:3], ALU.is_ge, ALU.mult, [bcc, bst3], [bst3])
                P.tt("dve", st3[:, 1:2], st3[:, 1:2], st3[:, 4:5], ALU.add, [bst3], [bst3])
                P.ts("dve", st3[:, 2:3], st3[:, 2:3], 0.5, None, ALU.mult, None, [bst3], [bst3])
            P.ts("dve", mbs[:, 0:Sk], Isc_[:, 0:Sk], st3[:, 1:2], NEG, ALU.is_lt, ALU.mult, bIsc_ + [bst3], [bmb])

        def dsa_att(c):
            ti, it, nkc, Sk = c["ti"], c["it"], c["nkc"], c["Sk"]
            for g in range(2):
                bo = P.nextps()
                P.pinned.add(bo)
                bd = P.nextps()
                P.pinned.add(bd)

                def att_a(kc):
                    b = P.nextps()
                    P.mm(PB[b][:], KT[:, g, kc * 128:(kc + 1) * 128], QT[:, ti, g * 4:(g + 1) * 4, :].rearrange("p h t -> p (h t)"),
                         True, False, [bKT[kc], bQT[ti]], [bPB[b]])
                    P.mm(PB[b][:], mbs[:, kc * 128:(kc + 1) * 128], I4[:], False, True, [bmb] + RC, [bPB[b]])
                    r = kc % 3
                    P.act(PTb[r][:], PB[b][:], AF.Exp, [bPB[b]], [bPT[r]], scale=128.0 ** -0.5)

                def att_b(kc):
                    r = kc % 3
                    P.mm(PB[bo][:], Vb[:, kc, g * 128:(g + 1) * 128], PTb[r][:], kc == 0, kc == nkc - 1, [bVb[kc], bPT[r]], [bPB[bo]])
                    P.mm(PB[bd][:], ones[:], PTb[r][:], kc == 0, kc == nkc - 1, [bPT[r]] + RC, [bPB[bd]])
                LAG2 = 2
                for kc in range(nkc):
                    att_a(kc)
                    if kc >= LAG2:
                        att_b(kc - LAG2)
                for kc in range(max(0, nkc - LAG2), nkc):
                    att_b(kc)
                P.S.op("dve", lambda e, bd=bd: e.reciprocal(out=tmpA[0][:], in_=PB[bd][:]), [bPB[bd]], [btmpA[0]])
                for hh in range(4):
                    P.tt("dve", dsaoT[:, g * 4 + hh, ti * 128:(ti + 1) * 128], PB[bo][:, hh * 128:(hh + 1) * 128],
                         tmpA[0][:, hh * 128:(hh + 1) * 128], ALU.mult, [bPB[bo], btmpA[0]], [bdsaoT[ti]])
                P.pinned.discard(bo)
                P.pinned.discard(bd)

        def smp_index_prep():
            S.dma("sp", sPt, ptab, writes=[bsIdx])
            P.cp("dve", sIdxF[:, 31:32], sPt, [bsIdx], [bsIdx])
            for rb in range(16):
                P.ts("dve", sIdxF[:, rb:rb + 1], sIdxF[:, 31:32], 16.0, float(rb), ALU.mult, ALU.add, [bsIdx], [bsIdx])
            for rb in range(8):
                P.ts("dve", sIdxF[:, 16 + rb:17 + rb], sIdxF[:, 31:32], 8.0, float(rb), ALU.mult, ALU.add, [bsIdx], [bsIdx])
            P.cp("dve", sIdxI[:, 0:24], sIdxF[:, 0:24], [bsIdx], [bsIdx])

        def gather(dst, bdst, src2d, col, extra_w=()):
            S.op("pool", lambda e: e.indirect_dma_start(out=dst, out_offset=None, in_=src2d,
                                                        in_offset=bass.IndirectOffsetOnAxis(ap=sIdxI[:, col:col + 1], axis=0)),
                 [bsIdx], [bdst] + list(extra_w), dma=True)

        def dsa_sample(ti):
            SC = 128.0 ** -0.5
            P.memset("dve", dsaoT[:, :, ti * 128:(ti + 1) * 128], 0.0, [bdsaoT[ti]])
            X4 = sX[0:4, 0:64].rearrange("p (q t j) -> p t q j", t=4, q=2)
            iw4 = iwf[0:4, ti, :].rearrange("p (j q) -> p q j", q=2).unsqueeze(1).to_broadcast([4, 4, 2, 8])
            id4 = ident[0:4, 0:4].unsqueeze(2).unsqueeze(3).to_broadcast([4, 4, 2, 8])
            P.tt("dve", X4, iw4, id4, ALU.mult, [biw[ti]] + RC, [bsX])
            b = P.nextps()
            P.mm(PB[b][:, 0:64], ones[0:4, 0:128], sX[0:4, 0:64], True, True, [bsX] + RC, [bPB[b]])
            P.cp("dve", sWb, PB[b][:, 0:64], [bPB[b]], [bsWb])

            def score_batch(bDs, nr, dst_cols):
                n = nr * 32
                for par in range(2):
                    o0 = par * 256
                    P.act(sRl[:, o0:o0 + n], PB[bDs[par]][:, 0:n], AF.Relu, [bPB[bDs[par]]], [bsRl])
                    P.tt("dve", sTmp[:, o0:o0 + n].rearrange("p (r c) -> p r c", c=32), sRl[:, o0:o0 + n].rearrange("p (r c) -> p r c", c=32),
                         sWb[:, par * 32:(par + 1) * 32].unsqueeze(1).to_broadcast([128, nr, 32]), ALU.mult, [bsRl, bsWb], [bsTmp])
                    P.red("dve", sRl[:, o0:o0 + nr * 4], sTmp[:, o0:o0 + n].rearrange("p (a j) -> p a j", j=8), ALU.add, [bsTmp, bsRl], [bsRl])
                P.tt("dve", sIT[:, dst_cols], sRl[:, 0:nr * 4], sRl[:, 256:256 + nr * 4], ALU.add, [bsRl], [bsIT])

            def dots_mm(bDs, col0, ikT_ap, bik):
                for par in range(2):
                    outv = PB[bDs[par]][:, col0:col0 + 32]
                    rhs = iqT[par * 64:(par + 1) * 64, ti, :, 0:4].rearrange("p j t -> p t j")
                    P.mm(outv, ikT_ap[par * 64:(par + 1) * 64, :], rhs, True, True, [biqT[ti], bik], [bPB[bDs[par]]])

            gather(Ist[0], bIst[0], pool_ik2, 16, extra_w=bikT)
            for ib in range(8):
                if ib + 1 < 8:
                    gather(Ist[(ib + 1) % 2], bIst[(ib + 1) % 2], pool_ik2, 16 + ib + 1, extra_w=bikT if ib == 0 else ())
                src3 = Ist[ib % 2].rearrange("p (r d) -> p r d", d=64)
                dst3 = sIbf.rearrange("p (r d) -> p r d", d=128)
                P.cp("dve", dst3[:, :, 0:64], src3, [bIst[ib % 2]], [bH[3]])
                P.cp("act", dst3[:, :, 64:128], src3, [bIst[ib % 2]], [bH[3]])
                for half in range(2):
                    bDs = [P.nextps(), None]
                    P.pinned.add(bDs[0])
                    bDs[1] = P.nextps()
                    P.pinned.add(bDs[1])
                    for q in range(2):
                        qq = half * 2 + q
                        bt = P.nextps()
                        for rr in range(4):
                            r = qq * 4 + rr
                            P.mm(PB[bt][:, rr * 128:(rr + 1) * 128], dst3[:, r, :], ident, True, True, [bH[3]] + RC, [bPB[bt]])
                        P.cp(P.ev_eng(), sikTr[qq % 2], PB[bt][:], [bPB[bt]], [bsikTr[qq % 2]])
                        for rr in range(4):
                            dots_mm(bDs, (q * 4 + rr) * 32, sikTr[qq % 2][:, rr * 128:(rr + 1) * 128], bsikTr[qq % 2])
                    c0 = (ib * 16 + half * 8) * 4
                    score_batch(bDs, 8, slice(c0, c0 + 32))
                    P.pinned.discard(bDs[0])
                    P.pinned.discard(bDs[1])
            bDs = [P.nextps(), None]
            P.pinned.add(bDs[0])
            bDs[1] = P.nextps()
            P.pinned.discard(bDs[0])
            dots_mm(bDs, 0, sikTo, bsikTo)
            score_batch(bDs, 1, slice(512, 516))
            P.tt("dve", sIT[:, 512:516], sIT[:, 512:516], triT4, ALU.add, [bsIT] + RC, [bsIT])
            IT3 = sIT[:, 0:516].rearrange("p (c t) -> p t c", t=4)
            P.red("dve", sSt[:, 0:4], IT3, ALU.max, [bsIT], [bsSt])
            P.red("dve", sSt[:, 4:8], sIT[:, 0:512].rearrange("p (c t) -> p t c", t=4), ALU.min, [bsIT], [bsSt])
            P.cp("dve", sPcb[:, 0:8], sSt[:, 0:8], [bsSt], [bsPcb])
            b = P.nextps()
            P.mm(PB[b][0:4, 0:128], sPcb[:, 0:4], ident, True, True, [bsPcb] + RC, [bPB[b]])
            P.mm(PB[b][0:4, 128:256], sPcb[:, 4:8], ident, True, True, [bsPcb] + RC, [bPB[b]])
            g4 = sSt[0:4, 8:16]
            P.red("dve", g4[:, 0:1], PB[b][0:4, 0:128], ALU.max, [bPB[b]], [bsSt])
            P.red("dve", g4[:, 1:2], PB[b][0:4, 128:256], ALU.min, [bPB[b]], [bsSt])
            P.ts("dve", g4[:, 2:3], g4[:, 0:1], 1.02, None, ALU.mult, None, [bsSt], [bsSt])
            P.ts("dve", g4[:, 4:5], g4[:, 0:1], 0.98, None, ALU.mult, None, [bsSt], [bsSt])
            P.tt("dve", g4[:, 3:4], g4[:, 2:3], g4[:, 4:5], ALU.max, [bsSt], [bsSt])
            P.ts("dve", g4[:, 3:4], g4[:, 3:4], 1.0, None, ALU.add, None, [bsSt], [bsSt])
            P.ts("dve", g4[:, 2:3], g4[:, 1:2], 1.02, None, ALU.mult, None, [bsSt], [bsSt])
            P.ts("dve", g4[:, 4:5], g4[:, 1:2], 0.98, None, ALU.mult, None, [bsSt], [bsSt])
            P.tt("dve", g4[:, 5:6], g4[:, 2:3], g4[:, 4:5], ALU.min, [bsSt], [bsSt])
            P.ts("dve", g4[:, 5:6], g4[:, 5:6], -1.0, None, ALU.add, None, [bsSt], [bsSt])
            D8 = sX[0:4, 0:8]
            P.ts("dve", D8[:, 0:4], ident[0:4, 0:4], g4[:, 3:4], None, ALU.mult, None, [bsSt] + RC, [bsX])
            P.ts("dve", D8[:, 4:8], ident[0:4, 0:4], g4[:, 5:6], None, ALU.mult, None, [bsSt] + RC, [bsX])
            b = P.nextps()
            P.mm(PB[b][:, 0:8], ones[0:4, 0:128], D8, True, True, [bsX] + RC, [bPB[b]])
            hi_b = sSt[:, 16:20]; thr = sSt[:, 20:24]; step = sSt[:, 24:28]; cand = sSt[:, 28:32]; mt = sSt[:, 32:36]
            P.cp("dve", sSt[:, 16:24], PB[b][:, 0:8], [bPB[b]], [bsSt])
            P.tt("dve", step, hi_b, thr, ALU.subtract, [bsSt], [bsSt])
            P.ts("dve", step, step, 0.5, None, ALU.mult, None, [bsSt], [bsSt])
            IT3n = sIT[:, 0:516].rearrange("p (c t) -> p c t", t=4)
            G3n = sG[:, 0:516].rearrange("p (c t) -> p c t", t=4)
            G3 = sG[:, 0:516].rearrange("p (c t) -> p t c", t=4)
            for itn in range(20):
                P.tt("dve", cand, thr, step, ALU.add, [bsSt], [bsSt])
                P.tt("dve", G3n, IT3n, cand.unsqueeze(1).to_broadcast([128, 129, 4]), ALU.is_ge, [bsIT, bsSt], [bsG])
                P.red("dve", sSt[:, 36:40], G3, ALU.add, [bsG], [bsSt])
                P.cp("dve", sPcb[:, 0:4], sSt[:, 36:40], [bsSt], [bsPcb])
                b = P.nextps()
                P.mm(PB[b][:, 0:4], ones[:], sPcb[:, 0:4], True, True, [bsPcb] + RC, [bPB[b]])
                P.ts("dve", mt, PB[b][:, 0:4], float(TOPK), None, ALU.is_ge, None, [bPB[b]], [bsSt])
                P.tt("dve", mt, mt, step, ALU.mult, [bsSt], [bsSt])
                P.tt("dve", thr, thr, mt, ALU.add, [bsSt], [bsSt])
                P.ts("dve", step, step, 0.5, None, ALU.mult, None, [bsSt], [bsSt])
            M3n = sMask[:, 0:516].rearrange("p (c t) -> p c t", t=4)
            P.tt("dve", M3n, IT3n, thr.unsqueeze(1).to_broadcast([128, 129, 4]), ALU.is_ge, [bsIT, bsSt], [bsMask])
            bO = [P.nextps(), None]
            P.pinned.add(bO[0])
            bO[1] = P.nextps()
            P.pinned.add(bO[1])
            bDn = P.nextps()
            P.pinned.add(bDn)
            Kb3 = sKbf.rearrange("p (r c) -> p r c", c=256)
            Vb3 = sVbf.rearrange("p (r c) -> p r c", c=256)
            PT3 = sPT.rearrange("p (r c) -> p r c", c=32)

            def qrhs(g):
                return QT[:, ti, g * 4:(g + 1) * 4, 0:4]

            def softmax_pv(bL, nr, mcol0, vsrc_of, bv, first, last):
                n = nr * 32
                P.act(sE[:, 0:n], PB[bL][:, 0:n], AF.Exp, [bPB[bL]], [bsE], scale=SC)
                P.tt("dve", sPT[:, 0:n].rearrange("p (r a t) -> p r a t", a=8, t=4),
                     sE[:, 0:n].rearrange("p (r a t) -> p r a t", a=8, t=4),
                     sMask[:, mcol0:mcol0 + nr * 4].rearrange("p (r t) -> p r t", t=4).unsqueeze(2).to_broadcast([128, nr, 8, 4]),
                     ALU.mult, [bsE, bsMask], [bsPT])
                for r8 in range(nr):
                    st_ = first and r8 == 0
                    sp_ = last and r8 == nr - 1
                    for g in range(2):
                        P.mm(PB[bO[g]][:, 0:16], vsrc_of(r8, g), PT3[:, r8, g * 16:(g + 1) * 16], st_, sp_, [bv, bsPT], [bPB[bO[g]]])
                    P.mm(PB[bDn][:, 0:32], ones[:], PT3[:, r8, :], st_, sp_, [bsPT] + RC, [bPB[bDn]])

            gather(Kst[0], bKst[0], pool_k2, 0, extra_w=bKT)
            gather(Vst[0], bVst[0], pool_v2, 0, extra_w=bVb)
            H2bf = H[:, 2, :].bitcast(BF16)
            Kbf2 = [sKbf, H2bf[:, 0:2048]]
            Vbf2 = [sVbf, H2bf[:, 2048:4096]]
            bKV2 = [bH[1], bH[2]]
            KTr4 = [MR[:, 4896 + 512 * i:5408 + 512 * i] for i in range(4)]
            bKTr4 = [bsKTr[0], bsKTr[1], bsikTr[0], bsikTr[1]]
            for kb in range(16):
                if kb + 1 < 16:
                    gather(Kst[(kb + 1) % 2], bKst[(kb + 1) % 2], pool_k2, kb + 1, extra_w=bKT if kb == 0 else ())
                    gather(Vst[(kb + 1) % 2], bVst[(kb + 1) % 2], pool_v2, kb + 1, extra_w=bVb if kb == 0 else ())
                pz = kb % 2
                P.cp("dve", Kbf2[pz], Kst[kb % 2], [bKst[kb % 2]], [bKV2[pz]])
                P.cp("act", Vbf2[pz], Vst[kb % 2], [bVst[kb % 2]], [bKV2[pz]])
                Kb3 = Kbf2[pz].rearrange("p (r c) -> p r c", c=256)
                Vb3 = Vbf2[pz].rearrange("p (r c) -> p r c", c=256)
                bL = P.nextps()
                P.pinned.add(bL)
                bts = []
                for q in range(4):
                    bt = P.nextps()
                    P.pinned.add(bt)
                    bts.append(bt)
                    for rr in range(2):
                        for g in range(2):
                            P.mm(PB[bt][:, (rr * 2 + g) * 128:(rr * 2 + g + 1) * 128], Kb3[:, q * 2 + rr, g * 128:(g + 1) * 128], ident,
                                 True, True, [bKV2[pz]] + RC, [bPB[bt]])
                for q in range(4):
                    P.cp("act" if q % 2 == 0 else "dve", KTr4[q], PB[bts[q]][:], [bPB[bts[q]]], [bKTr4[q]])
                    P.pinned.discard(bts[q])
                for q in range(4):
                    for rr in range(2):
                        for g in range(2):
                            r8 = q * 2 + rr
                            P.mm(PB[bL][:, r8 * 32 + g * 16:r8 * 32 + g * 16 + 16], KTr4[q][:, (rr * 2 + g) * 128:(rr * 2 + g + 1) * 128],
                                 qrhs(g), True, True, [bKTr4[q], bQT[ti]], [bPB[bL]])
                softmax_pv(bL, 8, kb * 32, lambda r8, g, Vb3=Vb3: Vb3[:, r8, g * 128:(g + 1) * 128], bKV2[pz], kb == 0, False)
                P.pinned.discard(bL)
            bL = P.nextps()
            for g in range(2):
                P.mm(PB[bL][:, g * 16:(g + 1) * 16], sKTo[:, g * 128:(g + 1) * 128], qrhs(g), True, True, [bsKTo, bQT[ti]], [bPB[bL]])
            softmax_pv(bL, 1, 512, lambda r8, g: sVo[:, g * 128:(g + 1) * 128], bsVo, False, True)
            P.S.op("dve", lambda e: e.reciprocal(out=sSt[:, 0:32], in_=PB[bDn][:, 0:32]), [bPB[bDn]], [bsSt])
            for g in range(2):
                P.tt("dve", dsaoT[:, g * 4:(g + 1) * 4, ti * 128:ti * 128 + 4], PB[bO[g]][:, 0:16].rearrange("p (a t) -> p a t", t=4),
                     sSt[:, g * 16:(g + 1) * 16].rearrange("p (a t) -> p a t", t=4), ALU.mult, [bPB[bO[g]], bsSt], [bdsaoT[ti]])
            for bb in (bO[0], bO[1], bDn):
                P.pinned.discard(bb)

        def own_group(tiles_info, pre=None):
            T = len(tiles_info)
            tiles = list(range(T))
            N = T * 128

            def loads():
                if pre is not None:
                    pre()
                for ti, inf in enumerate(tiles_info):
                    S.dma("sp", H[:, ti, :], inf["x"], writes=[bH[ti]])
                    S.dma("sp", tab[ti][:], inf["tab"], writes=[btab[ti]])
                rmsnorm_T_multi([(H[:, ti, :], bH[ti], ti) for ti in tiles], G_MIX)
            steps = []
            for hg in range(2):
                def rq_step(slot, bslot, hg=hg):
                    if hg == 0:
                        loads()

                    def evac(ti, ps, bps):
                        i = P.ev_ctr % 2
                        P.ev_ctr += 1
                        P.cp("act", tmpA[i][:], ps[:, 0:512], [bps], [btmpA[i]])
                        rope("dve", tmpA[i][:].rearrange("p (h d) -> p h d", h=4), q4[:, ti, :].rearrange("p (h d) -> p h d", h=4),
                             tab[ti][:, 0:64], tab[ti][:, 64:128], 4, 64, 128, [btmpA[i], btab[ti]], [bq4[ti]])
                    proj_tiles(slot, bslot, 16, 512, tiles, u_lhs, u_bufs, evac)

                def rk_step(slot, bslot, hg=hg):
                    def evac(ti, ps, bps):
                        evac_rk(ps, bps, tab[ti], btab[ti], k4[:, ti, :], bk4[ti])
                    proj_tiles(slot, bslot, 16, 512, tiles, u_lhs, u_bufs, evac)

                def rv_step(slot, bslot, hg=hg):
                    def evac(ti, ps, bps):
                        P.cp(P.ev_eng(), v4[:, ti, :], ps[:, 0:512], [bps], [bv4[ti]])
                    proj_tiles(slot, bslot, 16, 512, tiles, u_lhs, u_bufs, evac)

                def rg_step(slot, bslot, hg=hg):
                    def evac(ti, ps, bps):
                        P.act(sg4[:, ti, :], ps[:, 0:512], AF.Silu, [bps], [bsg4[ti]])
                    proj_tiles(slot, bslot, 16, 512, tiles, u_lhs, u_bufs, evac)
                    prev_final = None
                    for ti, inf in enumerate(tiles_info):
                        fin_ = retention(ti, hg, inf["kind"] == "smp")
                        if prev_final is not None:
                            prev_final()
                        prev_final = fin_
                    defer(prev_final, 1)
                for c0, fn in ((C_RQ, rq_step), (C_RK, rk_step), (C_RV, rv_step), (C_RG, rg_step)):
                    steps.append(([wseg(w_in[:, c0 + hg * 512:c0 + hg * 512 + 512], 16, 0, 512)], fn))
            if self.stop_after == "ret":
                return steps
            for blk in range(2):
                def dq_step(slot, bslot, blk=blk):
                    def evac(ti, ps, bps):
                        i = P.ev_ctr % 2
                        P.ev_ctr += 1
                        P.cp("act", tmpA[i][:], ps[:, 0:512], [bps], [btmpA[i]])
                        rope("dve", tmpA[i][:].rearrange("p (h d) -> p h d", h=4), bfA[i][:].rearrange("p (h d) -> p h d", h=4),
                             tab[ti][:, 128:144], tab[ti][:, 144:160], 4, 16, 128, [btmpA[i], btab[ti]], [bbfA[i]])
                        transposes_to(bfA[i], bbfA[i], 4, lambda j: QT[:, ti, blk * 4 + j, :], lambda j: bQT[ti])
                    proj_tiles(slot, bslot, 16, 512, tiles, u_lhs, u_bufs, evac)
                steps.append(([wseg(w_in[:, C_DQ + blk * 512:C_DQ + blk * 512 + 512], 16, 0, 512)], dq_step))

            if self.stop_after == "dq":
                return steps

            def kv_step(slot, bslot):
                def evac(ti, ps, bps):
                    inf = tiles_info[ti]
                    if inf["kind"] == "own":
                        evac_dkdv(ps, bps, tab[ti], btab[ti], 24 + inf["it"], inf["k_out"], inf["v_out"], True)
                    else:
                        evac_dkdv(ps, bps, tab[ti], btab[ti], 32, inf["k_out"], inf["v_out"], False)
                        P.cp("dve", kbf, kvf[:, 0:256], [bkvf], [bkbf])
                        P.cp("dve", sVo, kvf[:, 256:512], [bkvf], [bsVo])
                        transposes_to(kbf, bkbf, 2, lambda j: sKTo[:, j * 128:(j + 1) * 128], lambda j: bsKTo)
                proj_tiles(slot, bslot, 16, 512, tiles, u_lhs, u_bufs, evac)
            steps.append(([wseg(w_in[:, C_DK:C_DK + 512], 16, 0, 512)], kv_step))
            if self.stop_after == "kv":
                return steps
            for blk in range(2):
                def iq_step(slot, bslot, blk=blk):
                    def evac(ti, ps, bps):
                        i = P.ev_ctr % 2
                        P.ev_ctr += 1
                        P.cp("act", tmpA[i][:], ps[:, 0:512], [bps], [btmpA[i]])
                        rope("dve", tmpA[i][:].rearrange("p (h d) -> p h d", h=8), bfA[i][:].rearrange("p (h d) -> p h d", h=8),
                             tab[ti][:, 160:168], tab[ti][:, 168:176], 8, 8, 64, [btmpA[i], btab[ti]], [bbfA[i]])
                        transposes_to(bfA[i], bbfA[i], 4, lambda j: iqT[:, ti, blk * 4 + j, :], lambda j: biqT[ti])
                    proj_tiles(slot, bslot, 16, 512, tiles, u_lhs, u_bufs, evac)
                steps.append(([wseg(w_in[:, C_IQ + blk * 512:C_IQ + blk * 512 + 512], 16, 0, 512)], iq_step))

            if self.stop_after == "iq":
                return steps

            def ikw_step(slot, bslot):
                def evac(ti, ps, bps):
                    inf = tiles_info[ti]
                    P.cp("dve", iwf[:, ti, :], ps[:, 64:80], [bps], [biw[ti]])
                    if self.stop_after == "projA":
                        return
                    if inf["kind"] == "own":
                        evac_ik(ps, bps, tab[ti], btab[ti], 24 + inf["it"], inf["ik_out"], True)
                    else:
                        evac_ik(ps, bps, tab[ti], btab[ti], 32, inf["ik_out"], False)
                        P.cp("dve", ik2[:, 0:64], ikf[:], [bikf], [bik2])
                        P.cp("dve", ik2[:, 64:128], ikf[:], [bikf], [bik2])
                        transposes_to(ik2, bik2, 1, lambda j: sikTo, lambda j: bsikTo)
                proj_tiles(slot, bslot, 16, 80, tiles, u_lhs, u_bufs, evac)
                if self.stop_after in ("proj", "projA", "projB", "projC", "projD"):
                    return
                run_deferred(flush=True)
                if tiles_info[0]["kind"] == "own":
                    ctx = [None] * T
                    ctx[0] = dsa_idx(0, tiles_info[0]["it"], 0)
                    for ti in range(T):
                        dsa_bis(ctx[ti])
                        if ti + 1 < T:
                            ctx[ti + 1] = dsa_idx(ti + 1, tiles_info[ti + 1]["it"], (ti + 1) % 2)
                            dsa_att(ctx[ti])
                        else:
                            defer(lambda c=ctx[ti]: dsa_att(c), 3)
                else:
                    dsa_sample(0)
            steps.append(([wseg(w_in[:, C_IK:C_IK + 128], 16, 0, 128)], ikw_step))
            if self.stop_after in ("proj", "projA", "projB", "projC", "projD"):
                return steps
            for cb in range(4):
                def ga_step(slot, bslot, cb=cb):
                    def evac(ti, ps, bps):
                        P.act(sg4[:, ti, :], ps[:, 0:512], AF.Sigmoid, [bps], [bsg4[ti]])
                    proj_tiles(slot, bslot, 16, 512, tiles, u_lhs, u_bufs, evac)

                def ro_step(slot, bslot, cb=cb):
                    def evac(ti, ps, bps):
                        P.tt("dve", m1[:, ti, :], ps[:, 0:512], sg4[:, ti, :], ALU.mult, [bps, bsg4[ti]], [bm1[ti]])
                    proj_tiles(slot, bslot, 8, 512, tiles, lambda kc, ti: retyT[:, kc, ti * 128:(ti + 1) * 128],
                               lambda ti: [bretyT[ti]], evac)

                def gb_step(slot, bslot, cb=cb):
                    def evac(ti, ps, bps):
                        P.act(sg4[:, ti, :], ps[:, 0:512], AF.Sigmoid, [bps], [bsg4[ti]])
                    proj_tiles(slot, bslot, 16, 512, tiles, u_lhs, u_bufs, evac)

                def do_step(slot, bslot, cb=cb):
                    def evac(ti, ps, bps):
                        i = P.ev_ctr % 2
                        P.ev_ctr += 1
                        mb_, bmb_ = mgbuf[ti], bmgbuf[ti]
                        P.tt("dve", tmpA[i][:], ps[:, 0:512], sg4[:, ti, :], ALU.mult, [bps, bsg4[ti]], [btmpA[i]])
                        P.tt("dve", mb_[:], tmpA[i][:], m1[:, ti, :], ALU.add, [btmpA[i], bm1[ti]], [bmb_])
                        defer(lambda: transposes_to(mb_, bmb_, 4, lambda j: mrgT[:, cb * 4 + j, ti * 128:(ti + 1) * 128],
                                                    lambda j: bmrgT[ti]), 1)
                    proj_tiles(slot, bslot, 8, 512, tiles, lambda kc, ti: dsaoT[:, kc, ti * 128:(ti + 1) * 128],
                               lambda ti: [bdsaoT[ti]], evac)
                steps.append(([wseg(w_in[:, C_GA + cb * 512:C_GA + cb * 512 + 512], 16, 0, 512)], ga_step))
                steps.append(([wseg(w_ro[:, cb * 512:cb * 512 + 512], 8, 0, 512)], ro_step))
                steps.append(([wseg(w_in[:, C_GB + cb * 512:C_GB + cb * 512 + 512], 16, 0, 512)], gb_step))
                steps.append(([wseg(w_do[:, cb * 512:cb * 512 + 512], 8, 0, 512)], do_step))
            for cb in range(4):
                def wo_step(slot, bslot, cb=cb):
                    if cb == 0:
                        run_deferred(flush=True)
                        for ti, inf in enumerate(tiles_info):
                            S.dma("sp", H[:, ti, :], inf["x"], writes=[bH[ti]])

                    def evac(ti, ps, bps):
                        hs = H[:, ti, cb * 512:(cb + 1) * 512]
                        P.tt("dve", hs, ps[:, 0:512], hs, ALU.add, [bps, bH[ti]], [bH[ti]])
                    proj_tiles(slot, bslot, 16, 512, tiles, lambda kc, ti: mrgT[:, kc, ti * 128:(ti + 1) * 128],
                               lambda ti: [bmrgT[ti]], evac)
                steps.append(([wseg(w_o[:, cb * 512:cb * 512 + 512], 16, 0, 512)], wo_step))
            smp_group = tiles_info[0]["kind"] == "smp"
            mkT_, mvb_, bmkT_, bmvb_ = (mkTs, mvbs, bmkTs, bmvbs) if smp_group else (mkT, mvb, bmkT, bmvb)

            def mq_step(slot, bslot):
                rmsnorm_T_multi([(H[:, ti, :], bH[ti], ti) for ti in tiles], G_MEM)
                for hd in range(4):
                    b = P.nextps()
                    for kc in range(16):
                        P.mm(PB[b][:, 0:N], slot[:, kc, hd * 128:(hd + 1) * 128], U[:, kc, 0:N], kc == 0, kc == 15,
                             [bU[t] for t in tiles] + bslot, [bPB[b]])
                    P.cp(P.ev_eng(), qmT[:, hd, 0:N], PB[b][:, 0:N], [bPB[b]], [bqmT])
                for hd in range(4):
                    for mc in range(2):
                        b = P.nextps()
                        P.mm(PB[b][:, 0:N], mkT_[:, hd, mc * 128:(mc + 1) * 128], qmT[:, hd, 0:N], True, True, [bmkT_, bqmT], [bPB[b]])
                        P.act(PTm[:, mc, 0:N], PB[b][:, 0:N], AF.Exp, [bPB[b]], [bPTm], scale=128.0 ** -0.5)
                    bo = P.nextps()
                    bd = P.nextps()
                    for mc in range(2):
                        P.mm(PB[bo][:, 0:N], mvb_[:, mc, hd * 128:(hd + 1) * 128], PTm[:, mc, 0:N], mc == 0, mc == 1, [bmvb_, bPTm], [bPB[bo]])
                    for mc in range(2):
                        P.mm(PB[bd][:, 0:N], ones[:], PTm[:, mc, 0:N], mc == 0, mc == 1, [bPTm] + RC, [bPB[bd]])
                    P.S.op("dve", lambda e, bd=bd: e.reciprocal(out=recf2[:, 0:N], in_=PB[bd][:, 0:N]), [bPB[bd]], [brec2])
                    P.tt("dve", omT[:, hd, 0:N], PB[bo][:, 0:N], recf2[:, 0:N], ALU.mult, [bPB[bo], brec2], [bomT])
            steps.append(([wseg(w_mq, 16, 0, 512)], mq_step))
            for cb in range(4):
                def mo_step(slot, bslot, cb=cb):
                    def evac(ti, ps, bps):
                        hs = H[:, ti, cb * 512:(cb + 1) * 512]
                        P.tt("dve", hs, ps[:, 0:512], hs, ALU.add, [bps, bH[ti]], [bH[ti]])
                    proj_tiles(slot, bslot, 4, 512, tiles, lambda kc, ti: omT[:, kc, ti * 128:(ti + 1) * 128],
                               lambda ti: [bomT], evac)
                steps.append(([wseg(w_mo[:, cb * 512:cb * 512 + 512], 4, 0, 512)], mo_step))
            for half in range(2):
                for j in range(8):
                    def up_step(slot, bslot, half=half, j=j):
                        if half == 0 and j == 0:
                            rmsnorm_T_multi([(H[:, ti, :], bH[ti], ti) for ti in tiles], G_MLP)
                        for fb in range(4):
                            b = P.nextps()
                            for kc in range(16):
                                P.mm(PB[b][:, 0:N], slot[:, kc, fb * 128:(fb + 1) * 128], U[:, kc, 0:N], kc == 0, kc == 15,
                                     [bU[t] for t in tiles] + bslot, [bPB[b]])
                            i = P.ev_ctr % 2
                            P.ev_ctr += 1
                            P.act(tmpA[i][:, 0:N], PB[b][:, 0:N], AF.Relu, [bPB[b]], [btmpA[i]])
                            P.tt("dve", aT[:, j * 4 + fb, 0:N], tmpA[i][:, 0:N], tmpA[i][:, 0:N], ALU.mult, [btmpA[i]], [baT])
                    c0 = half * 4096 + j * 512
                    steps.append(([wseg(w_up[:, c0:c0 + 512], 16, 0, 512)], up_step))
                for cb in range(4):
                    for qq in range(2):
                        def dn_step(slot, bslot, half=half, cb=cb, qq=qq):
                            if qq == 0:
                                P._acc = []
                                for ti in tiles:
                                    b = P.nextps()
                                    P.pinned.add(b)
                                    P._acc.append(b)
                            for ti in tiles:
                                b = P._acc[ti]
                                for kc in range(16):
                                    P.mm(PB[b][:], aT[:, qq * 16 + kc, ti * 128:(ti + 1) * 128], slot[:, kc, :],
                                         qq == 0 and kc == 0, qq == 1 and kc == 15, [baT] + bslot, [bPB[b]])
                            if qq == 1:
                                for ti in tiles:
                                    b = P._acc[ti]
                                    hs = H[:, ti, cb * 512:(cb + 1) * 512]
                                    P.tt("dve", hs, PB[b][:], hs, ALU.add, [bPB[b], bH[ti]], [bH[ti]])
                                    P.pinned.discard(b)
                        r0 = half * 4096 + qq * 2048
                        steps.append(([wseg(w_dn[r0:r0 + 2048, cb * 512:cb * 512 + 512], 16, 0, 512)], dn_step))

            def fin_step(slot, bslot):
                S.dma("sp", gfin, gfin_d, writes=[baT])
                for ti, inf in enumerate(tiles_info):
                    stt_, bst = new_stat()
                    P.memset("dve", stt_[:, 0:1], 0.0, [bst])
                    P.act(xn4[ti], H[:, ti, :], AF.Square, [bH[ti], bst], [bxn4[ti], bst], accum=stt_[:, 0:1])
                    P.rsqrt(stt_[:, 1:2], stt_[:, 0:1], 1.0 / D, bst)
                    P.stt("dve", H[:, ti, :], H[:, ti, :], stt_[:, 1:2], gfin, ALU.mult, ALU.mult, [bH[ti], bst, baT], [bH[ti]])
                    S.dma("sp", inf["y_out"], H[:, ti, :], reads=[bH[ti]])
            steps.append(([], fin_step))
            return steps

        steps = []
        steps += phase_mem()
        if self.stop_after == "mem":
            run_steps(steps)
            S.emit_all()
            return nc
        for g in range(self.n_pre_groups):
            steps += prefix_group(g, g == self.n_pre_groups - 1)

        def zero_init():
            for hg in range(2):
                P.memset("dve", S32[:, hg * 4:(hg + 1) * 4, :], 0.0, [bS32[hg]])
                P.memset("dve", Sbf[:, hg * 4:(hg + 1) * 4, :], 0.0, [bSbf[hg]])
            P.memset("dve", KT[:], 0.0, bKT)
            P.memset("dve", Vb[:], 0.0, bVb)
            P.memset("dve", ikT[:], 0.0, bikT)
        for og in range(self.n_own_groups):
            infos = []
            for t in range(4):
                it = og * 4 + t
                r0 = it * 128
                infos.append(dict(kind="own", it=it, x=x_own[r0:r0 + 128, :], tab=tab_own[r0:r0 + 128, :],
                                  k_out=k_own[r0:r0 + 128, :], v_out=v_own[r0:r0 + 128, :], ik_out=ik_own[r0:r0 + 128, :],
                                  y_out=y_own[r0:r0 + 128, :]))
            steps += own_group(infos, pre=zero_init if (og == 0 and self.n_pre_groups == 0) else None)

        def st_out_step(slot, bslot):
            for hg in range(2):
                S.dma("sp", st_own[hg * 4:(hg + 1) * 4].rearrange("h d e -> d h e"), S32[:, hg * 4:(hg + 1) * 4, :], reads=[bS32[hg]])
        if self.n_own_groups > 0:
            steps.append(([], st_out_step))
        if self.do_sample:
            def smp_pre():
                for hg in range(2):
                    S.dma("sp", S32s[:, hg * 4:(hg + 1) * 4, :], st_in[hg * 4:(hg + 1) * 4].rearrange("h d e -> d h e"), writes=[bS32s[hg]])
                    P.cp("dve", Sbfs[:, hg * 4:(hg + 1) * 4, :], S32s[:, hg * 4:(hg + 1) * 4, :], [bS32s[hg]], [bSbfs[hg]])
                load_sample_mem()
                smp_index_prep()
            infos = [dict(kind="smp", it=0, x=x_smp[:, :], tab=tab_smp[:, :], k_out=k_smp[:, :], v_out=v_smp[:, :],
                          ik_out=ik_smp[:, :], y_out=y_smp[:, :])]
            steps += own_group(infos, pre=smp_pre)

            def st_out_s(slot, bslot):
                for hg in range(2):
                    S.dma("sp", st_smp[hg * 4:(hg + 1) * 4].rearrange("h d e -> d h e"), S32s[:, hg * 4:(hg + 1) * 4, :], reads=[bS32s[hg]])
            steps.append(([], st_out_s))
        run_steps(steps)
        S.emit_all()
        return nc


def _rope_tab(pos, n_half, theta):
    inv = (np.float32(theta) ** (-(np.arange(n_half, dtype=np.float32) / np.float32(n_half)))).astype(np.float32)
    ang = pos.astype(np.float32)[:, None] * inv[None, :]
    return np.cos(ang).astype(np.float32), np.sin(ang).astype(np.float32)


def _gammas():
    return np.log1p(-np.exp2(-5.0 - np.arange(8, dtype=np.float64)))


def _tab(pos, kdec=None):
    n = pos.shape[0]
    t = np.zeros((n, 192), np.float32)
    c, s = _rope_tab(pos, 64, 10000.0)
    t[:, 0:64], t[:, 64:128] = c, s
    c, s = _rope_tab(pos, 16, 500000.0)
    t[:, 128:144], t[:, 144:160] = c, s
    c, s = _rope_tab(pos, 8, 500000.0)
    t[:, 160:168], t[:, 168:176] = c, s
    if kdec is not None:
        t[:, 176:184] = kdec
    return t


def _consts(p):
    lg = _gammas()
    c = np.zeros((128, 64), np.float32)
    cb = np.zeros((128, 2304), np.float32)
    cb[:, 0:128] = np.eye(128, dtype=np.float32)
    j = np.arange(128)
    cb[:, 128:256] = np.where(j[None, :] <= j[:, None], 0.0, NEG)
    diff = (j[None, :] - j[:, None]).astype(np.float64)
    for h in range(8):
        cb[:, 256 + h * 128:256 + (h + 1) * 128] = np.where(diff >= 0, np.exp(lg[h] * np.maximum(diff, 0.0)), 0.0)
        qd = np.exp(lg[h] * (j + 1.0))
        cb[:, 1280 + h * 128:1280 + (h + 1) * 128] = np.diag(qd)
        c[:, 0 + h] = qd
        c[:, 8 + h] = np.exp(lg[h] * (127.0 - j))
        c[:, 16 + h] = np.exp(lg[h] * (3.0 - j))
        c[:, 24 + h] = np.exp(lg[h] * 128.0)
        c[:, 32 + h] = np.exp(lg[h] * 4.0)
    for v in range(3):
        c[:, 40 + v] = 0.0 if v < p else NEG
    for t in range(4):
        c[:, 44 + t] = np.where(j <= t, 0.0, NEG)
    return c, cb


_PROG_CACHE = {}


def _get_prog(**kw):
    key = tuple(sorted(kw.items()))
    if key not in _PROG_CACHE:
        _PROG_CACHE[key] = Prog(**kw).build()
    return _PROG_CACHE[key]


def make_in_maps(inp, cores=range(8)):
    f = lambda a: np.ascontiguousarray(np.asarray(a, dtype=np.float32))
    xp = f(inp["x_prompt"]); xs = f(inp["x_sample"]); memp = f(inp["mem_prompt"])
    lg = _gammas()
    gvec = np.zeros((128, 96), np.float32)
    for off, name in ((0, "g_mix"), (16, "g_mem"), (32, "g_memkv"), (48, "g_mlp")):
        gvec[:, off:off + 16] = f(inp[name])[0].reshape(16, 128).T
    gvec[:, 64:72] = f(inp["gn_ret"])[0].reshape(8, 128).T
    gfin = np.ascontiguousarray(np.broadcast_to(f(inp["g_final"])[None, :], (128, D)))
    shared = dict(
        cbf=_consts(0)[1], gvec=gvec, gfin=gfin,
        w_in=f(inp["w_in"])[0], w_ret_out=f(inp["w_ret_out"])[0], w_dsa_out=f(inp["w_dsa_out"])[0], w_o=f(inp["w_o"])[0],
        w_mq=f(inp["w_mq"])[0], w_mk=f(inp["w_mk"])[0], w_mv=f(inp["w_mv"])[0], w_mo=f(inp["w_mo"])[0],
        w_up=f(inp["w_up"])[0], w_down=f(inp["w_down"])[0],
    )
    if USE_POOLS:
        shared.update(pool_k=f(inp["cache_k"])[0].reshape(1280 * 16, 2048), pool_v=f(inp["cache_v"])[0].reshape(1280 * 16, 2048),
                      pool_ik=f(inp["cache_idx_k"])[0].reshape(1280 * 8, 1024))
    pt = np.asarray(inp["page_table"]).astype(np.int32)
    maps = []
    for c in cores:
        b, p = c // 4, c % 4
        m = dict(shared)
        m["x_own"] = np.ascontiguousarray(xp[b, 1024 * p:1024 * (p + 1)])
        xpre = np.zeros((3072, D), np.float32)
        xpre[:1024 * p] = xp[b, :1024 * p]
        m["x_pre"] = xpre
        xsm = np.zeros((128, D), np.float32)
        xsm[:4] = xs[c]
        m["x_smp"] = xsm
        m["mem"] = np.ascontiguousarray(memp[b])
        pos_own = np.arange(1024 * p, 1024 * (p + 1))
        m["tab_own"] = _tab(pos_own)
        pos_pre = np.arange(3072)
        kd = np.zeros((3072, 8), np.float64)
        valid = pos_pre < 1024 * p
        ex = (1024 * p - 1 - pos_pre).astype(np.float64)
        for h in range(8):
            kd[:, h] = np.where(valid, np.exp(lg[h] * np.maximum(ex, 0.0)), 0.0)
        m["tab_pre"] = _tab(pos_pre, kd.astype(np.float32))
        pos_s = PAST + np.arange(128)
        m["tab_smp"] = _tab(pos_s)
        m["cst"] = _consts(p)[0]
        m["state_smp"] = np.ascontiguousarray(f(inp["state_ret"])[0, c])
        m["cmk"] = np.ascontiguousarray(f(inp["cache_mem_k"])[0, c].reshape(256, 512))
        m["cmv"] = np.ascontiguousarray(f(inp["cache_mem_v"])[0, c].reshape(256, 512))
        if USE_POOLS:
            m["ptab"] = np.ascontiguousarray(pt[c].reshape(128, 1))
        maps.append(m)
    return maps


def assemble(res):
    y_p = np.zeros((2, 4096, D), np.float32)
    y_s = np.zeros((8, 4, D), np.float32)
    st_p = np.zeros((1, 2, 8, 128, 128), np.float32)
    k_p = np.zeros((1, 2, 4096, 2, 128), np.float32)
    v_p = np.zeros((1, 2, 4096, 2, 128), np.float32)
    ik_p = np.zeros((1, 2, 4096, 64), np.float32)
    mk_p = np.zeros((1, 2, 256, 4, 128), np.float32)
    mv_p = np.zeros((1, 2, 256, 4, 128), np.float32)
    st_s = np.zeros((1, 8, 8, 128, 128), np.float32)
    k_s = np.zeros((1, 8, 4, 2, 128), np.float32)
    v_s = np.zeros((1, 8, 4, 2, 128), np.float32)
    ik_s = np.zeros((1, 8, 4, 64), np.float32)
    for c, r in enumerate(res):
        b, p = c // 4, c % 4
        sl = slice(1024 * p, 1024 * (p + 1))
        y_p[b, sl] = r["y_own"]
        k_p[0, b, sl] = r["k_own"].reshape(1024, 2, 128)
        v_p[0, b, sl] = r["v_own"].reshape(1024, 2, 128)
        ik_p[0, b, sl] = r["ik_own"]
        if p == 3:
            st_p[0, b] = r["st_own"]
        if p == 0:
            mk_p[0, b] = r["mk_o"].reshape(256, 4, 128)
            mv_p[0, b] = r["mv_o"].reshape(256, 4, 128)
        y_s[c] = r["y_smp"][:4]
        st_s[0, c] = r["st_smp"]
        k_s[0, c] = r["k_smp"][:4].reshape(4, 2, 128)
        v_s[0, c] = r["v_smp"][:4].reshape(4, 2, 128)
        ik_s[0, c] = r["ik_smp"][:4]
    return (y_p, y_s, st_p, k_p, v_p, ik_p, mk_p, mv_p, st_s, k_s, v_s, ik_s)


def kernel(**inputs):
    nc = _get_prog()
    in_maps = make_in_maps(inputs)
    res = run_bass_kernel_spmd(nc, in_maps, core_ids=list(range(8)))
    return assemble(res.results)
```
